# Optimizing a Trainium2 kernel written in Bass

```python
import math
import jax
import jax.numpy as jnp
from jax import lax
import numpy as np

D_MODEL = 2048
BATCH = 32
SEQ = 256
DEPTH = 4
DEC_BATCH = 8
DEC_SEQ = 4096
PAST_LEN = 512

GRID_W = 64
N_AB = (DEPTH + 1) // 2
N_SSD = DEPTH // 2
NORM_EPS = 1e-6
GN_EPS = 64e-5
NEG = -1e30

RWKV_WIDTH = D_MODEL // 2
RWKV_HEAD = 64
RWKV_HEADS = RWKV_WIDTH // RWKV_HEAD
DECAY_LORA = 64
ICLR_LORA = 64
GATE_LORA = 160
RWKV_COLS = 3 * RWKV_WIDTH + 2 * DECAY_LORA + 2 * ICLR_LORA + GATE_LORA

ATTN_WIDTH = D_MODEL // 2
ATTN_HEAD = 64
ATTN_HEADS = ATTN_WIDTH // ATTN_HEAD
ATTN_KV_HEADS = 4
ATTN_GROUPS = ATTN_HEADS // ATTN_KV_HEADS
WINDOW = 128
WBLK = 128
ROPE_THETA = 10000.0
AB_IN = RWKV_COLS + ATTN_WIDTH + 2 * ATTN_KV_HEADS * ATTN_HEAD

SSD_INNER = 2 * D_MODEL
SSD_HEAD = 64
SSD_HEADS = SSD_INNER // SSD_HEAD
SSD_GROUPS = 8
SSD_HPG = SSD_HEADS // SSD_GROUPS
SSD_STATE = 128
SSD_CONV = 5
SSD_CHUNK = 128
SSD_CONV_DIM = SSD_INNER + 2 * SSD_GROUPS * SSD_STATE
SSD_IN = SSD_INNER + SSD_CONV_DIM + 2 * SSD_HEADS

FFN_HIDDEN = -(-8 * D_MODEL // (3 * 256)) * 256

kernel_name = 'hybrid_rwkv7_swa_ssd_diffusion_step'

F32 = jnp.float32


def rmsnorm(x, g):
    xf = x.astype(F32)
    y = xf * lax.rsqrt(jnp.mean(xf * xf, -1, keepdims=True) + NORM_EPS)
    return (y * g.astype(F32)).astype(x.dtype)


def modulate(cond, w, b):
    m = jax.nn.silu(cond) @ w + b
    return [t[:, None, :] for t in jnp.split(m, 6, axis=-1)]


def swiglu(h, w_in, w_out):
    gt, up = jnp.split(h @ w_in, 2, axis=-1)
    return (jax.nn.silu(gt) * up) @ w_out


def _both(t):
    return jnp.stack([t, jnp.flip(t, 1)])


def _per_dir(t):
    return jnp.stack([t[:, :, 0], jnp.flip(t[:, :, 1], 1)])


def _rwkv_step(S, inp):
    r, w, k, v, kk, b = inp
    sa = jnp.einsum('dbhvk,dbhk->dbhv', S, -kk)
    S = S * w[..., None, :] + sa[..., :, None] * b[..., None, :] + v[..., :, None] * k[..., None, :]
    return S, jnp.einsum('dbhvk,dbhk->dbhv', S, r)


def rwkv_mix(z, p, s0):
    Bsz, L, _ = z.shape
    H, N, W = RWKV_HEADS, RWKV_HEAD, RWKV_WIDTH
    z_prev = jnp.pad(z, ((0, 0), (1, 0), (0, 0)))[:, :L]
    z_next = jnp.pad(z, ((0, 0), (0, 1), (0, 0)))[:, 1:]
    zs = (z + p['mu_prev'] * (z_prev - z) + p['mu_next'] * (z_next - z)).astype(F32)
    r, k, v = zs[..., :W], zs[..., W:2 * W], zs[..., 2 * W:3 * W]
    o = 3 * W
    wd = zs[..., o:o + 2 * DECAY_LORA].reshape(Bsz, L, 2, DECAY_LORA)
    o += 2 * DECAY_LORA
    ad = zs[..., o:o + 2 * ICLR_LORA].reshape(Bsz, L, 2, ICLR_LORA)
    gd = zs[..., o + 2 * ICLR_LORA:]
    w_log = -jax.nn.softplus(-(p['w0'].astype(F32) + jnp.einsum('bldr,drc->bldc', jnp.tanh(wd), p['w2'].astype(F32)))) - 0.5
    decay = jnp.exp(-jnp.exp(w_log))
    a = jax.nn.sigmoid(p['a0'].astype(F32) + jnp.einsum('bldr,drc->bldc', ad, p['a2'].astype(F32)))
    g = jax.nn.sigmoid(gd) @ p['g2'].astype(F32)
    kk = (k * p['k_k'].astype(F32)).reshape(Bsz, L, H, N)
    kk = kk * lax.rsqrt(jnp.sum(kk * kk, -1, keepdims=True) + 1e-12)
    k_dir = k[:, :, None] * (1.0 + (a - 1.0) * p['k_a'].astype(F32))
    heads = lambda t: t.reshape(t.shape[:-1] + (H, N))
    rh, vh = heads(r), heads(v)
    kdh, ah, wh = heads(k_dir), heads(a), heads(decay)
    kk2 = _both(kk)
    xs = (_both(rh), _per_dir(wh), _per_dir(kdh), _both(vh), kk2, kk2 * _per_dir(ah))
    xs = tuple(jnp.moveaxis(t, 2, 0) for t in xs)
    s_fin, ys = lax.scan(_rwkv_step, s0.astype(F32), xs)
    ys = jnp.moveaxis(ys, 0, 2)
    y = ys[0] + jnp.flip(ys[1], 1)
    mu = jnp.mean(y, -1, keepdims=True)
    var = jnp.mean(jnp.square(y - mu), -1, keepdims=True)
    y = ((y - mu) * lax.rsqrt(var + GN_EPS)).reshape(Bsz, L, W) * p['lnx_g'].astype(F32) + p['lnx_b'].astype(F32)
    bonus = jnp.sum(rh[:, :, None] * kdh * p['r_k'].astype(F32), axis=(2, 4))[..., None] * vh
    out = (y + bonus.reshape(Bsz, L, W)) * g
    return out.astype(z.dtype), s_fin


def axial_rope(x):
    L = x.shape[1]
    n_rows = L // GRID_W
    rows = jnp.repeat(jnp.arange(n_rows), GRID_W).astype(F32)
    cols = (jnp.arange(n_rows * GRID_W) % GRID_W).astype(F32)
    half = x.shape[-1] // 2
    inv_freq = ROPE_THETA ** (-jnp.arange(0, half, 2, dtype=F32) / half)

    def rot(xa, pos):
        ang = pos[:, None] * inv_freq[None, :]
        cos, sin = jnp.cos(ang)[None, :, None, :], jnp.sin(ang)[None, :, None, :]
        x1, x2 = xa[..., :half // 2], xa[..., half // 2:]
        return jnp.concatenate([x1 * cos - x2 * sin, x2 * cos + x1 * sin], -1)

    xf = x.astype(F32)
    return jnp.concatenate([rot(xf[..., :half], rows), rot(xf[..., half:], cols)], -1).astype(x.dtype)


def context_attention(q, k, v, sink):
    Bsz, L = q.shape[:2]
    s = jnp.einsum('bqhgd,bkhd->bhgqk', q, k).astype(F32)
    sinkb = jnp.broadcast_to(sink[None, :, :, None, None], s.shape[:-1] + (1,))
    p = jax.nn.softmax(jnp.concatenate([s, sinkb], -1), axis=-1)[..., :-1]
    o = jnp.einsum('bhgqk,bkhd->bqhgd', p.astype(v.dtype), v)
    return o.reshape(Bsz, L, ATTN_WIDTH)


def window_attention(q, k, v, ck, cv, sink):
    Bsz, L = q.shape[:2]
    nb = L // WBLK
    qb = jnp.moveaxis(q.reshape(Bsz, nb, WBLK, ATTN_KV_HEADS, ATTN_GROUPS, ATTN_HEAD), 1, 0)

    def windows(t):
        tp = jnp.pad(t, ((0, 0), (WBLK, WBLK), (0, 0), (0, 0))).reshape(Bsz, nb + 2, WBLK, ATTN_KV_HEADS, ATTN_HEAD)
        w = jnp.concatenate([tp[:, :nb], tp[:, 1:nb + 1], tp[:, 2:nb + 2]], axis=2)
        return jnp.moveaxis(w, 1, 0)

    kw, vw = windows(k), windows(v)
    qi = jnp.arange(WBLK)
    kj = jnp.arange(3 * WBLK)
    rel = (qi[:, None] + WBLK) - kj[None, :]
    n_loc = 3 * WBLK

    def block(args):
        b, qblk, kblk, vblk = args
        kpos = (b - 1) * WBLK + kj
        valid = (jnp.abs(rel) <= WINDOW) & ((kpos >= 0) & (kpos < L))[None, :]
        s_loc = jnp.where(valid, jnp.einsum('bqhgd,bkhd->bhgqk', qblk, kblk).astype(F32), NEG)
        s_ctx = jnp.einsum('bqhgd,bkhd->bhgqk', qblk, ck).astype(F32)
        sinkb = jnp.broadcast_to(sink[None, :, :, None, None], s_ctx.shape[:-1] + (1,))
        p = jax.nn.softmax(jnp.concatenate([s_loc, s_ctx, sinkb], -1), axis=-1)
        o = jnp.einsum('bhgqk,bkhd->bqhgd', p[..., :n_loc].astype(vblk.dtype), vblk)
        return o + jnp.einsum('bhgqk,bkhd->bqhgd', p[..., n_loc:-1].astype(cv.dtype), cv)

    out = lax.map(block, (jnp.arange(nb), qb, kw, vw))
    return jnp.moveaxis(out, 0, 1).reshape(Bsz, L, ATTN_WIDTH)


def ab_mixer(h, p, s0, kv_ctx):
    Bsz, L, _ = h.shape
    z = h @ p['w_in']
    o1 = RWKV_COLS
    o2 = o1 + ATTN_WIDTH
    o3 = o2 + ATTN_KV_HEADS * ATTN_HEAD
    rwkv_out, s_fin = rwkv_mix(z[..., :o1], p, s0)
    q = z[..., o1:o2].reshape(Bsz, L, ATTN_HEADS, ATTN_HEAD)
    k = z[..., o2:o3].reshape(Bsz, L, ATTN_KV_HEADS, ATTN_HEAD)
    v = z[..., o3:].reshape(Bsz, L, ATTN_KV_HEADS, ATTN_HEAD)
    sink = p['sink'].reshape(ATTN_KV_HEADS, ATTN_GROUPS).astype(F32)
    if kv_ctx is None:
        qg = (q * ATTN_HEAD ** -0.5).reshape(Bsz, L, ATTN_KV_HEADS, ATTN_GROUPS, ATTN_HEAD)
        att = context_attention(qg, k, v, sink)
    else:
        q = axial_rope(q)
        k = axial_rope(k)
        qg = (q * ATTN_HEAD ** -0.5).reshape(Bsz, L, ATTN_KV_HEADS, ATTN_GROUPS, ATTN_HEAD)
        att = window_attention(qg, k, v, kv_ctx[0], kv_ctx[1], sink)
    out = jnp.concatenate([rwkv_out, att], -1) @ p['w_out']
    return out, s_fin, k, v


def centred_dwconv(x, w, b):
    pad = (SSD_CONV - 1) // 2
    y = lax.conv_general_dilated(x, w[:, None, :].astype(x.dtype), (1,), [(pad, pad)],
                                 dimension_numbers=('NWC', 'WIO', 'NWC'), feature_group_count=x.shape[-1])
    return y + b


def ssd_chunked(x, dt, A, bm, cm, h0):
    Bsz, L = x.shape[:2]
    nc = L // SSD_CHUNK
    chunks = lambda t: jnp.moveaxis(t.astype(F32).reshape((Bsz, nc, SSD_CHUNK) + t.shape[2:]), 1, 0)
    mask = jnp.tril(jnp.ones((SSD_CHUNK, SSD_CHUNK), bool))[None, :, :, None, None]

    def step(h, inp):
        xc, dtc, bc, cc = inp
        acum = jnp.cumsum(dtc * A, axis=1)
        seg = acum[:, :, None] - acum[:, None, :]
        decay = jnp.exp(jnp.where(mask, seg, -jnp.inf))
        cb = jnp.einsum('bign,bjgn->bijg', cc, bc)
        xdt = xc * dtc[..., None]
        y = jnp.einsum('bijgk,bjgkp->bigkp', cb[..., None] * decay, xdt)
        y = y + jnp.einsum('bign,bgkpn->bigkp', cc, h) * jnp.exp(acum)[..., None]
        a_last = acum[:, -1]
        w_in = jnp.exp(a_last[:, None] - acum)[..., None]
        h = jnp.exp(a_last)[..., None, None] * h + jnp.einsum('bjgn,bjgkp->bgkpn', bc, xdt * w_in)
        return h, y

    h_fin, ys = lax.scan(step, h0.astype(F32), (chunks(x), chunks(dt), chunks(bm), chunks(cm)))
    y = jnp.moveaxis(ys, 0, 1).reshape(x.shape)
    return y, h_fin


def ssd_mixer(h, p, s0):
    Bsz, L, _ = h.shape
    G, K, P, N = SSD_GROUPS, SSD_HPG, SSD_HEAD, SSD_STATE
    zin = h @ p['w_in']
    z = zin[..., :SSD_INNER]
    xbc = jax.nn.silu(centred_dwconv(zin[..., SSD_INNER:SSD_INNER + SSD_CONV_DIM], p['conv_w'], p['conv_b']))
    dt_raw = zin[..., SSD_INNER + SSD_CONV_DIM:]
    x = xbc[..., :SSD_INNER].reshape(Bsz, L, G, K, P)
    bm = xbc[..., SSD_INNER:SSD_INNER + G * N].reshape(Bsz, L, G, N)
    cm = xbc[..., SSD_INNER + G * N:].reshape(Bsz, L, G, N)
    dt = jax.nn.softplus(dt_raw.astype(F32).reshape(Bsz, L, 2, SSD_HEADS) + p['dt_bias'].astype(F32))
    dt = dt.reshape(Bsz, L, 2, G, K)
    a = -jnp.exp(p['a_log'].astype(F32)).reshape(2, G, K)
    y, s_fin = jax.vmap(ssd_chunked)(_both(x), _per_dir(dt), a, _both(bm), _both(cm), s0)
    y = y[0] + jnp.flip(y[1], 1) + p['d'].astype(F32).reshape(G, K, 1) * x.astype(F32)
    y = y.reshape(Bsz, L, SSD_INNER) * jax.nn.silu(z.astype(F32))
    yg = y.reshape(Bsz, L, G, SSD_INNER // G)
    yg = yg * lax.rsqrt(jnp.mean(yg * yg, -1, keepdims=True) + NORM_EPS)
    y = (yg.reshape(Bsz, L, SSD_INNER) * p['norm_g'].astype(F32)).astype(h.dtype)
    return y @ p['w_out'], s_fin


def setup_inputs(seed: int = 0) -> dict:
    key = jax.random.key(seed)
    ks = iter(jax.random.split(key, 48))

    def nrm(shape, scale):
        return jax.random.normal(next(ks), shape, F32) * scale

    def unif(shape, lo, hi):
        return jax.random.uniform(next(ks), shape, F32, lo, hi)

    W = RWKV_WIDTH
    dt0 = jnp.exp(unif((N_SSD, 2, SSD_HEADS), math.log(1e-3), math.log(1e-1)))
    return {
        'x_prompt': nrm((BATCH, SEQ, D_MODEL), 1.0),
        'x_sample': nrm((DEC_BATCH, DEC_SEQ, D_MODEL), 1.0),
        'c': nrm((DEC_BATCH, D_MODEL), 1.0),
        'state_rwkv': nrm((DEC_BATCH, N_AB, 2, RWKV_HEADS, RWKV_HEAD, RWKV_HEAD), 1.0),
        'cache_k': nrm((DEC_BATCH, N_AB, PAST_LEN, ATTN_KV_HEADS, ATTN_HEAD), 1.0),
        'cache_v': nrm((DEC_BATCH, N_AB, PAST_LEN, ATTN_KV_HEADS, ATTN_HEAD), 1.0),
        'state_ssd': nrm((DEC_BATCH, N_SSD, 2, SSD_HEADS, SSD_HEAD, SSD_STATE), 0.1),
        'c_ctx': nrm((D_MODEL,), 1.0),
        'w_mod': nrm((DEPTH, D_MODEL, 6 * D_MODEL), 0.5 * D_MODEL ** -0.5),
        'b_mod': nrm((DEPTH, 6 * D_MODEL), 0.02),
        'norm1_g': 1.0 + nrm((DEPTH, D_MODEL), 0.02),
        'norm2_g': 1.0 + nrm((DEPTH, D_MODEL), 0.02),
        'ffn_w_in': nrm((DEPTH, D_MODEL, 2 * FFN_HIDDEN), D_MODEL ** -0.5),
        'ffn_w_out': nrm((DEPTH, FFN_HIDDEN, D_MODEL), FFN_HIDDEN ** -0.5),
        'final_norm_g': 1.0 + nrm((D_MODEL,), 0.02),
        'ab_w_in': nrm((N_AB, D_MODEL, AB_IN), D_MODEL ** -0.5),
        'ab_w_out': nrm((N_AB, RWKV_WIDTH + ATTN_WIDTH, D_MODEL), (RWKV_WIDTH + ATTN_WIDTH) ** -0.5),
        'rwkv_mu_prev': unif((N_AB, RWKV_COLS), 0.0, 0.5),
        'rwkv_mu_next': unif((N_AB, RWKV_COLS), 0.0, 0.5),
        'rwkv_w0': unif((N_AB, 2, W), -6.0, 1.0),
        'rwkv_w2': nrm((N_AB, 2, DECAY_LORA, W), 0.1),
        'rwkv_a0': nrm((N_AB, 2, W), 0.1),
        'rwkv_a2': nrm((N_AB, 2, ICLR_LORA, W), 0.1),
        'rwkv_g2': nrm((N_AB, GATE_LORA, W), GATE_LORA ** -0.5),
        'rwkv_k_k': 0.85 + nrm((N_AB, W), 0.02),
        'rwkv_k_a': 1.0 + nrm((N_AB, W), 0.02),
        'rwkv_r_k': nrm((N_AB, RWKV_HEADS, RWKV_HEAD), 0.1),
        'rwkv_lnx_g': 1.0 + nrm((N_AB, W), 0.02),
        'rwkv_lnx_b': nrm((N_AB, W), 0.01),
        'attn_sink': nrm((N_AB, ATTN_HEADS), 0.5),
        'ssd_w_in': nrm((N_SSD, D_MODEL, SSD_IN), D_MODEL ** -0.5),
        'ssd_conv_w': nrm((N_SSD, SSD_CONV, SSD_CONV_DIM), SSD_CONV ** -0.5),
        'ssd_conv_b': nrm((N_SSD, SSD_CONV_DIM), 0.02),
        'ssd_dt_bias': dt0 + jnp.log(-jnp.expm1(-dt0)),
        'ssd_a_log': jnp.log(unif((N_SSD, 2, SSD_HEADS), 1.0, 16.0)),
        'ssd_d': 1.0 + nrm((N_SSD, SSD_HEADS), 0.1),
        'ssd_norm_g': 1.0 + nrm((N_SSD, SSD_INNER), 0.02),
        'ssd_w_out': nrm((N_SSD, SSD_INNER, D_MODEL), SSD_INNER ** -0.5),
    }


def reference(x_prompt, x_sample, c, state_rwkv, cache_k, cache_v, state_ssd, c_ctx,
              w_mod, b_mod, norm1_g, norm2_g, ffn_w_in, ffn_w_out, final_norm_g,
              ab_w_in, ab_w_out, rwkv_mu_prev, rwkv_mu_next, rwkv_w0, rwkv_w2, rwkv_a0, rwkv_a2,
              rwkv_g2, rwkv_k_k, rwkv_k_a, rwkv_r_k, rwkv_lnx_g, rwkv_lnx_b, attn_sink,
              ssd_w_in, ssd_conv_w, ssd_conv_b, ssd_dt_bias, ssd_a_log, ssd_d, ssd_norm_g, ssd_w_out):
    def ab_params(i):
        return dict(w_in=ab_w_in[i], w_out=ab_w_out[i], mu_prev=rwkv_mu_prev[i], mu_next=rwkv_mu_next[i],
                    w0=rwkv_w0[i], w2=rwkv_w2[i], a0=rwkv_a0[i], a2=rwkv_a2[i], g2=rwkv_g2[i],
                    k_k=rwkv_k_k[i], k_a=rwkv_k_a[i], r_k=rwkv_r_k[i], lnx_g=rwkv_lnx_g[i],
                    lnx_b=rwkv_lnx_b[i], sink=attn_sink[i])

    def ssd_params(j):
        return dict(w_in=ssd_w_in[j], conv_w=ssd_conv_w[j], conv_b=ssd_conv_b[j], dt_bias=ssd_dt_bias[j],
                    a_log=ssd_a_log[j], d=ssd_d[j], norm_g=ssd_norm_g[j], w_out=ssd_w_out[j])

    def trunk(x, cond, rwkv_init, kv_ctx, ssd_init):
        Bsz = x.shape[0]
        is_ctx = kv_ctx is None
        rw, ks, vs, ss = [], [], [], []
        for l in range(DEPTH):
            sh1, sc1, g1, sh2, sc2, g2 = modulate(cond, w_mod[l], b_mod[l])
            hn = rmsnorm(x, norm1_g[l]) * (1 + sc1) + sh1
            if l % 2 == 0:
                i = l // 2
                s0 = jnp.zeros((2, Bsz, RWKV_HEADS, RWKV_HEAD, RWKV_HEAD), F32) if is_ctx else rwkv_init[i]
                out, s_fin, k, v = ab_mixer(hn, ab_params(i), s0, None if is_ctx else kv_ctx[i])
                if is_ctx:
                    rw.append(s_fin)
                    ks.append(k)
                    vs.append(v)
            else:
                j = l // 2
                s0 = jnp.zeros((2, Bsz, SSD_GROUPS, SSD_HPG, SSD_HEAD, SSD_STATE), F32) if is_ctx else ssd_init[j]
                out, s_fin = ssd_mixer(hn, ssd_params(j), s0)
                if is_ctx:
                    ss.append(s_fin)
            x = x + g1 * out
            hn = rmsnorm(x, norm2_g[l]) * (1 + sc2) + sh2
            x = x + g2 * swiglu(hn, ffn_w_in[l], ffn_w_out[l])
        return rmsnorm(x, final_norm_g), rw, ks, vs, ss

    y_prompt, rw, ks, vs, ss = trunk(x_prompt, c_ctx[None, :], None, None, None)
    sdt = x_prompt.dtype
    new_state_rwkv = jnp.stack([jnp.moveaxis(s, 0, 1) for s in rw], axis=1).astype(sdt)
    new_cache_k = jnp.stack(ks, axis=1)
    new_cache_v = jnp.stack(vs, axis=1)
    new_state_ssd = jnp.stack([jnp.moveaxis(s, 0, 1).reshape(s.shape[1], 2, SSD_HEADS, SSD_HEAD, SSD_STATE)
                               for s in ss], axis=1).astype(sdt)

    db = x_sample.shape[0]
    rwkv_init = [jnp.moveaxis(state_rwkv[:, i], 1, 0).astype(F32) for i in range(N_AB)]
    kv_ctx = [(cache_k[:, i], cache_v[:, i]) for i in range(N_AB)]
    ssd_init = [jnp.moveaxis(state_ssd[:, j], 1, 0).astype(F32).reshape(2, db, SSD_GROUPS, SSD_HPG, SSD_HEAD, SSD_STATE)
                for j in range(N_SSD)]
    y_sample = trunk(x_sample, c, rwkv_init, kv_ctx, ssd_init)[0]
    return (y_prompt, y_sample, new_state_rwkv, new_cache_k, new_cache_v, new_state_ssd)
```

```python
import math
from contextlib import ExitStack
import numpy as np
import concourse.bass as bass
import concourse.mybir as mybir
from concourse.bass_utils import run_bass_kernel_spmd

F32 = mybir.dt.float32
BF16 = mybir.dt.bfloat16
AF = mybir.ActivationFunctionType
ALU = mybir.AluOpType
AX = mybir.AxisListType
P = 128
NCORES = 8
import os
ENABLE_ATTN = os.environ.get('ENABLE_ATTN', '1') == '1'
ENABLE_SSD = os.environ.get('ENABLE_SSD', '1') == '1'
ENABLE_RWKV = os.environ.get('ENABLE_RWKV', '1') == '1'


class T:
    __slots__ = ("t", "w", "r", "excl")

    def __init__(self, t, excl=False):
        self.t = t
        self.w = None
        self.r = []
        self.excl = excl

    def __getitem__(self, idx):
        return self.t[idx]


class Ctx:
    def __init__(self, nc, ndma=10):
        self.nc = nc
        self.eng = {"pe": nc.tensor, "act": nc.scalar, "dve": nc.vector, "pool": nc.gpsimd, "sp": nc.sync}
        self.sem = {k: nc.alloc_semaphore("sem_" + k) for k in self.eng}
        self.cnt = {k: 0 for k in self.eng}
        self.seen = {k: {} for k in self.eng}
        self.dsem = {q: [nc.alloc_semaphore(f"dma_{q}_{i}") for i in range(ndma)] for q in ("sp", "pool")}
        self.dcnt = {q: [0] * ndma for q in self.dsem}
        self.drr = {q: 0 for q in self.dsem}
        self.outstanding = {}

    def _wait(self, e, tok):
        sem, val, owner = tok
        key = id(sem)
        if self.seen[e].get(key, 0) >= val:
            return
        self.seen[e][key] = val
        self.eng[e].wait_ge(sem, val)

    def _deps(self, e, reads, writes):
        for t in reads:
            if t.w is not None and not (e == "pe" and t.w[2] == "pe"):
                self._wait(e, t.w)
        for t in writes:
            if t.w is not None and not (e == "pe" and t.w[2] == "pe"):
                self._wait(e, t.w)
            for tok in t.r:
                if not (e == "pe" and tok[2] == "pe"):
                    self._wait(e, tok)

    def _post(self, tok, reads, writes):
        for t in reads:
            t.r = [x for x in t.r if x[0] is not tok[0]] + [tok]
        for t in writes:
            t.w = tok
            t.r = []

    def op(self, e, fn, reads=(), writes=()):
        ex = [t for t in reads if t.excl]
        if ex and e != "pe":
            reads = [t for t in reads if not t.excl]
            writes = list(writes) + ex
        self._deps(e, reads, writes)
        inst = fn(self.eng[e])
        self.cnt[e] += 1
        inst.then_inc(self.sem[e], 1)
        tok = (self.sem[e], self.cnt[e], e)
        self._post(tok, reads, writes)
        return tok

    def dma(self, q, out, in_, reads=(), writes=(), **kw):
        self._deps(q, reads, writes)
        i = self.drr[q]
        self.drr[q] = (i + 1) % len(self.dsem[q])
        sem = self.dsem[q][i]
        if self.dcnt[q][i] > 0:
            self._wait(q, (sem, self.dcnt[q][i], "dma"))
        inst = self.eng[q].dma_start(out=out, in_=in_, **kw)
        self.dcnt[q][i] += 16
        inst.then_inc(sem, 16)
        tok = (sem, self.dcnt[q][i], "dma")
        self._post(tok, reads, writes)
        self.outstanding[id(sem)] = tok
        return tok

    def barrier(self):
        toks = [(self.sem[k], self.cnt[k], k) for k in self.eng if self.cnt[k] > 0]
        toks += list(self.outstanding.values())
        self.outstanding = {}
        for e in self.eng:
            for tok in toks:
                if tok[2] != e:
                    self._wait(e, tok)


def fm(a, nch=None):
    a = np.asarray(a, np.float32)
    n = a.shape[-1]
    if nch is None:
        nch = -(-n // P)
    pad = nch * P - n
    if pad:
        a = np.concatenate([a, np.zeros(a.shape[:-1] + (pad,), np.float32)], -1)
    a = a.reshape(a.shape[:-1] + (nch, P))
    return np.ascontiguousarray(np.swapaxes(a, -1, -2))


class Builder:
    def __init__(self, Ls, Lp, NPR, depth, past):
        self.Ls, self.Lp, self.NPR, self.depth, self.past = Ls, Lp, NPR, depth, past
        self.D = 2048
        self.TT = 512
        self.Ttot = Ls + NPR * Lp
        assert self.Ttot % self.TT == 0 and Ls % self.TT == 0
        self.nc = bass.Bass("TRN2", target_bir_lowering=False)
        self.c = Ctx(self.nc)
        self.ins = {}
        self.outs = {}
        self.n_ab = (depth + 1) // 2
        self.n_ssd = depth // 2

    def din(self, name, shape, dt=F32):
        ap = self.nc.dram_tensor(name, list(shape), dt, kind="ExternalInput").ap()
        self.ins[name] = ap
        return ap

    def dout(self, name, shape, dt=F32):
        ap = self.nc.dram_tensor(name, list(shape), dt, kind="ExternalOutput").ap()
        self.outs[name] = ap
        return ap

    def dscr(self, name, shape, dt=F32):
        return self.nc.dram_tensor(name, list(shape), dt, kind="Internal").ap()

    def sb(self, es, name, shape, dt=F32):
        self._uid = getattr(self, "_uid", 0) + 1
        return T(es.enter_context(self.nc.sbuf_tensor(f"{name}_{self._uid}", list(shape), dt)))

    def tile_cond(self, t0):
        return 1 if t0 < self.Ls else 0

    def declare(self):
        D, Ttot, depth = self.D, self.Ttot, self.depth
        self.xin = self.din("xin", [Ttot, D])
        self.cond2 = self.din("cond2", [P, 16, 2])
        self.w_mod = self.din("w_mod", [depth, D, 6 * D])
        self.bmod = self.din("bmod", [depth, P, 96])
        self.n1g = self.din("n1g", [depth, P, 16])
        self.n2g = self.din("n2g", [depth, P, 16])
        self.fng = self.din("fng", [P, 16])
        self.ffn_w_in = self.din("ffn_w_in", [depth, D, 11264])
        self.ffn_w_out = self.din("ffn_w_out", [depth, 5632, D])
        self.ident = self.din("ident", [P, P])
        self.ab_w_out = self.din("ab_w_out", [self.n_ab, 2048, D])
        self.ssd_w_out = self.din("ssd_w_out", [max(self.n_ssd, 1), 4096, D])
        self.y_out = self.dout("y_out", [Ttot, D])
        n_ab, NPR, Lp, past = self.n_ab, self.NPR, self.Lp, self.past
        self.ab_w_in = self.din("ab_w_in", [n_ab, D, 54 * P])
        self.ssd_w_in = self.din("ssd_w_in", [max(self.n_ssd, 1), D, 81 * P])
        self.ropec = self.din("ropec", [P, self.Ls])
        self.ropes = self.din("ropes", [P, self.Ls])
        self.maskb = self.din("maskb", [2, P, P])
        self.sinkfm = self.din("sinkfm", [n_ab, P, 8])
        self.cache_k = self.din("cache_k", [n_ab, past, 256])
        self.cache_v = self.din("cache_v", [n_ab, past, 256])
        self.nck = self.dout("nck", [NPR, n_ab, Lp, 256])
        self.ncv = self.dout("ncv", [NPR, n_ab, Lp, 256])
        self.Z = self.dscr("Z", [81 * P, Ttot])
        self.r_mu = self.din("r_mu", [n_ab, P, 2, 28])
        self.r_w0a0 = self.din("r_w0a0", [n_ab, P, 2, 2, 8])
        self.r_w2 = self.din("r_w2", [n_ab, P, 1024])
        self.r_a2 = self.din("r_a2", [n_ab, P, 1024])
        self.r_g2 = self.din("r_g2", [n_ab, P, 2, 1024])
        self.r_vec = self.din("r_vec", [n_ab, P, 5, 8])
        self.r_maskn = self.din("r_maskn", [2, P, 64])
        self.r_masks = self.din("r_masks", [2, P, P])
        self.r_is = self.din("r_is", [P, 64])
        self.r_blk = self.din("r_blk", [P, P])
        self.state_rwkv = self.din("state_rwkv", [n_ab, 2, 16, 64, 64])
        self.nsr = self.dout("nsr", [NPR, n_ab, 2, 16, 64, 64])
        self.RW = [self.dscr(f"RW{i}", [1024, Ttot]) for i in range(12)]
        ns = max(self.n_ssd, 1)
        self.s_cw = self.din("s_cw", [ns, P, 5, 48])
        self.s_cb = self.din("s_cb", [ns, P, 48])
        self.s_dtb = self.din("s_dtb", [ns, P, 1])
        self.s_alog = self.din("s_alog", [ns, P, 1])
        self.s_dfm = self.din("s_dfm", [ns, P, 32])
        self.s_ng = self.din("s_ng", [ns, P, 32])
        self.s_maskl = self.din("s_maskl", [2, P, P])
        self.state_ssd = self.din("state_ssd", [ns, 2, 4096, P])
        self.nss = self.dout("nss", [NPR, ns, 2, 4096, P])
        self.XC = self.dscr("XC", [48 * P, Ttot], BF16)
        self.YF = self.dscr("YF", [Ttot, 4096])
        self.XA = self.dscr("XA", [D, Ttot])
        self.XB = self.dscr("XB", [D, Ttot])
        self.MIX = self.dscr("MIX", [4096, Ttot], BF16)
        c, nc = self.c, self.nc
        self.ps = [T(nc.alloc_psum_tensor(f"ps{i}", [P, 512], F32), excl=True) for i in range(8)]
        self.psi = 0
        self.identF = T(nc.alloc_sbuf_tensor("identF", [P, P], F32))
        self.identB = T(nc.alloc_sbuf_tensor("identB", [P, P], BF16))
        self.onesF = T(nc.alloc_sbuf_tensor("onesF", [P, P], F32))
        self.mod = T(nc.alloc_sbuf_tensor("mod", [P, depth, 96, 2], F32))
        self.gs1 = T(nc.alloc_sbuf_tensor("gs1", [P, depth, 16, 2], F32))
        self.gs2 = T(nc.alloc_sbuf_tensor("gs2", [P, depth, 16, 2], F32))
        self.fngt = T(nc.alloc_sbuf_tensor("fngt", [P, 16], F32))
        self.eps_t = T(nc.alloc_sbuf_tensor("eps_t", [P, 1], F32))
        c.dma("sp", self.identF[:], self.ident, writes=[self.identF])
        c.dma("pool", self.identB[:], self.ident, writes=[self.identB])
        c.dma("sp", self.fngt[:], self.fng, writes=[self.fngt])
        c.op("dve", lambda e: e.memset(self.onesF[:], 1.0), writes=[self.onesF])
        c.op("dve", lambda e: e.memset(self.eps_t[:], 1e-6), writes=[self.eps_t])

    def next_ps(self):
        p = self.ps[self.psi]
        self.psi = (self.psi + 1) % 8
        return p

    def ph_load(self):
        c, nc = self.c, self.nc
        with ExitStack() as es:
            xin_t = [self.sb(es, f"ld_x{i}", [P, 2048]) for i in range(2)]
            xo_t = [self.sb(es, f"ld_o{i}", [P, 16, P]) for i in range(2)]
            XAv = self.XA.rearrange("(fc p) t -> p fc t", p=P)
            for b in range(self.Ttot // P):
                xi, xo = xin_t[b % 2], xo_t[b % 2]
                c.dma("sp", xi[:], self.xin[b * P:(b + 1) * P, :], writes=[xi])
                for g in range(4):
                    ps = self.next_ps()
                    for j in range(4):
                        fc = g * 4 + j
                        c.op("pe", lambda e, fc=fc, j=j, ps=ps, xi=xi: e.transpose(
                            ps[:, j * P:(j + 1) * P], xi[:, fc * P:(fc + 1) * P], self.identF[:]),
                            reads=[xi, self.identF], writes=[ps])
                    eng = "act" if g % 2 else "dve"
                    if eng == "act":
                        c.op("act", lambda e, g=g, ps=ps, xo=xo: e.copy(
                            out=xo[:, g * 4:(g + 1) * 4, :], in_=ps[:].rearrange("p (a b) -> p a b", a=4)),
                            reads=[ps], writes=[xo])
                    else:
                        c.op("dve", lambda e, g=g, ps=ps, xo=xo: e.tensor_copy(
                            out=xo[:, g * 4:(g + 1) * 4, :], in_=ps[:].rearrange("p (a b) -> p a b", a=4)),
                            reads=[ps], writes=[xo])
                c.dma("sp", XAv[:, :, b * P:(b + 1) * P], xo[:], reads=[xo])
            c.barrier()

    def ph_mod(self):
        c, nc = self.c, self.nc
        depth = self.depth
        with ExitStack() as es:
            cd = self.sb(es, "md_c", [P, 16, 2])
            sc = self.sb(es, "md_s", [P, 16, 2])
            bm = self.sb(es, "md_b", [P, depth, 96])
            ng = self.sb(es, "md_g", [P, 2, depth, 16])
            wb = [self.sb(es, f"md_w{i}", [P, 16, 512]) for i in range(2)]
            c.dma("sp", cd[:], self.cond2, writes=[cd])
            c.dma("sp", bm[:], self.bmod.rearrange("l p m -> p l m"), writes=[bm])
            c.dma("sp", ng[:, 0], self.n1g.rearrange("l p m -> p l m"), writes=[ng])
            c.dma("sp", ng[:, 1], self.n2g.rearrange("l p m -> p l m"), writes=[ng])
            c.op("act", lambda e: e.activation(out=sc[:], in_=cd[:], func=AF.Silu), reads=[cd], writes=[sc])
            it = 0
            for l in range(depth):
                Wv = self.w_mod[l].rearrange("(kc p) n -> p kc n", p=P)
                for g in range(24):
                    w = wb[it % 2]
                    it += 1
                    c.dma("sp", w[:], Wv[:, :, g * 512:(g + 1) * 512], writes=[w])
                    ps = self.next_ps()
                    for j in range(4):
                        for kc in range(16):
                            c.op("pe", lambda e, w=w, j=j, kc=kc, ps=ps: e.matmul(
                                ps[:, 2 * j:2 * j + 2], w[:, kc, j * P:(j + 1) * P], sc[:, kc, :],
                                start=(kc == 0), stop=(kc == 15)), reads=[w, sc], writes=[ps])
                    c.op("dve", lambda e, l=l, g=g, ps=ps: e.tensor_tensor(
                        out=self.mod[:, l, g * 4:(g + 1) * 4, :],
                        in0=ps[:, 0:8].rearrange("p (a b) -> p a b", a=4),
                        in1=bm[:, l, g * 4:(g + 1) * 4].unsqueeze(2).broadcast_to([P, 4, 2]), op=ALU.add),
                        reads=[ps, bm], writes=[self.mod])
                for which, gs, base in ((0, self.gs1, 16), (1, self.gs2, 64)):
                    c.op("dve", lambda e, l=l, gs=gs, base=base: e.tensor_scalar(
                        out=gs[:, l], in0=self.mod[:, l, base:base + 16, :], scalar1=1.0, scalar2=None,
                        op0=ALU.add), reads=[self.mod], writes=[gs])
                    c.op("dve", lambda e, l=l, gs=gs, which=which: e.tensor_tensor(
                        out=gs[:, l], in0=gs[:, l],
                        in1=ng[:, which, l].unsqueeze(2).broadcast_to([P, 16, 2]), op=ALU.mult),
                        reads=[gs, ng], writes=[gs])
            c.barrier()

    def norm_tile(self, es_tiles, X, t0, hn, scale_ap_fn, bias_ap_fn):
        c = self.c
        xs_l, sq_l, rs_l = es_tiles
        Xv = X.rearrange("(fc p) t -> p fc t", p=P)
        SUB = 256
        for s in range(self.TT // SUB):
            xs = xs_l[s % len(xs_l)]
            sq = sq_l[s % len(sq_l)]
            rs = rs_l[s % len(rs_l)]
            c0 = t0 + s * SUB
            c.dma("sp", xs[:], Xv[:, :, c0:c0 + SUB], writes=[xs])
            c.op("act", lambda e, xs=xs, sq=sq: e.activation(out=sq[:], in_=xs[:], func=AF.Square),
                 reads=[xs], writes=[sq])
            ps = self.next_ps()
            for kc in range(16):
                c.op("pe", lambda e, kc=kc, ps=ps, sq=sq: e.matmul(
                    ps[:, :SUB], self.onesF[:], sq[:, kc, :], start=(kc == 0), stop=(kc == 15)),
                    reads=[self.onesF, sq], writes=[ps])
            c.op("act", lambda e, ps=ps, rs=rs: e.activation(
                out=rs[:], in_=ps[:, :SUB], func=AF.Sqrt, scale=1.0 / self.D, bias=self.eps_t[:]),
                reads=[ps, self.eps_t], writes=[rs])
            c.op("dve", lambda e, rs=rs: e.reciprocal(out=rs[:], in_=rs[:]), reads=[rs], writes=[rs])
            c.op("dve", lambda e, xs=xs, rs=rs: e.tensor_tensor(
                out=xs[:], in0=xs[:], in1=rs[:].unsqueeze(1).broadcast_to([P, 16, SUB]), op=ALU.mult),
                reads=[xs, rs], writes=[xs])
            for fc in range(16):
                c.op("act", lambda e, fc=fc, xs=xs, s=s: e.activation(
                    out=hn[:, fc, s * SUB:(s + 1) * SUB], in_=xs[:, fc, :], func=AF.Identity,
                    scale=scale_ap_fn(fc), bias=bias_ap_fn(fc)), reads=[xs, self.mod, self.gs1, self.gs2, self.fngt],
                    writes=[hn])

    def gemm(self, W, KC, nch, rhs, wbufs, epilogue, gw=512, chunk_filter=None):
        c = self.c
        Wv = W.rearrange("(kc p) n -> p kc n", p=P)
        cpg = gw // P
        it = 0
        for g in range(-(-nch // cpg)):
            chunks = [mc for mc in range(g * cpg, min(nch, (g + 1) * cpg)) if chunk_filter is None or chunk_filter(mc)]
            if not chunks:
                continue
            lo, hi = chunks[0], chunks[-1] + 1
            w = wbufs[it % len(wbufs)]
            it += 1
            c.dma("pool", w[:, :, 0:(hi - lo) * P], Wv[:, :, lo * P:hi * P], writes=[w])
            for mc in chunks:
                ps = self.next_ps()
                j = mc - lo
                for kc in range(KC):
                    c.op("pe", lambda e, w=w, j=j, kc=kc, ps=ps: e.matmul(
                        ps[:, :self.TT], w[:, kc, j * P:(j + 1) * P], rhs[:, kc, :],
                        start=(kc == 0), stop=(kc == KC - 1)), reads=[w, rhs], writes=[ps])
                epilogue(mc, ps)

    def ph_in(self, l, X, W, nch, Z, chunk_filter_fn=None):
        c = self.c
        with ExitStack() as es:
            xs_l = [self.sb(es, f"in_xs{i}", [P, 16, 256]) for i in range(2)]
            sq_l = [self.sb(es, f"in_sq{i}", [P, 16, 256]) for i in range(1)]
            rs_l = [self.sb(es, f"in_rs{i}", [P, 256]) for i in range(2)]
            hn_l = [self.sb(es, f"in_hn{i}", [P, 16, self.TT], BF16) for i in range(2)]
            wb = [self.sb(es, f"in_w{i}", [P, 16, 512], BF16) for i in range(3)]
            st = [self.sb(es, f"in_st{i}", [P, self.TT]) for i in range(4)]
            Zv = Z.rearrange("(mc p) t -> p mc t", p=P)
            sti = [0]
            for ti in range(self.Ttot // self.TT):
                t0 = ti * self.TT
                cond = self.tile_cond(t0)
                hn = hn_l[ti % 2]
                self.norm_tile((xs_l, sq_l, rs_l), X, t0, hn,
                               lambda fc: self.gs1[:, l, fc, cond:cond + 1],
                               lambda fc: self.mod[:, l, 0 + fc, cond:cond + 1])

                def epi(mc, ps, t0=t0):
                    s = st[sti[0] % 4]
                    sti[0] += 1
                    if sti[0] % 2:
                        c.op("act", lambda e: e.copy(out=s[:], in_=ps[:, :self.TT]), reads=[ps], writes=[s])
                    else:
                        c.op("dve", lambda e: e.tensor_copy(out=s[:], in_=ps[:, :self.TT]), reads=[ps], writes=[s])
                    c.dma("sp", Zv[:, mc, t0:t0 + self.TT], s[:], reads=[s])
                flt = None if chunk_filter_fn is None else (lambda mc, cond=cond: chunk_filter_fn(mc, cond))
                self.gemm(W, 16, nch, hn, wb, epi, chunk_filter=flt)
            c.barrier()

    def ph_out(self, l, Xin, Xout, W, KC):
        c = self.c
        with ExitStack() as es:
            mx_l = [self.sb(es, f"ou_mx{i}", [P, KC, self.TT], BF16) for i in range(2)]
            gw = 512 if KC <= 16 else 256
            wb = [self.sb(es, f"ou_w{i}", [P, KC, gw], BF16) for i in range(3)]
            xo = [self.sb(es, f"ou_x{i}", [P, self.TT]) for i in range(4)]
            Xi = Xin.rearrange("(fc p) t -> p fc t", p=P)
            Xo = Xout.rearrange("(fc p) t -> p fc t", p=P)
            Mv = self.MIX.rearrange("(kc p) t -> p kc t", p=P)
            k = [0]
            for ti in range(self.Ttot // self.TT):
                t0 = ti * self.TT
                cond = self.tile_cond(t0)
                mx = mx_l[ti % 2]
                c.dma("sp", mx[:], Mv[:, 0:KC, t0:t0 + self.TT], writes=[mx])

                def epi(mc, ps, t0=t0, cond=cond):
                    x = xo[k[0] % 4]
                    k[0] += 1
                    c.dma("sp", x[:], Xi[:, mc, t0:t0 + self.TT], writes=[x])
                    c.op("dve", lambda e: e.scalar_tensor_tensor(
                        out=x[:], in0=ps[:, :self.TT], scalar=self.mod[:, l, 32 + mc, cond:cond + 1], in1=x[:],
                        op0=ALU.mult, op1=ALU.add), reads=[ps, x, self.mod], writes=[x])
                    c.dma("sp", Xo[:, mc, t0:t0 + self.TT], x[:], reads=[x])
                self.gemm(W, KC, 16, mx, wb, epi, gw=gw)
            c.barrier()

    def ph_ffn(self, l, Xin, Xout):
        c = self.c
        TT = self.TT
        with ExitStack() as es:
            xs_l = [self.sb(es, f"ff_xs{i}", [P, 16, 256]) for i in range(1)]
            sq_l = [self.sb(es, f"ff_sq{i}", [P, 16, 256]) for i in range(1)]
            rs_l = [self.sb(es, f"ff_rs{i}", [P, 256]) for i in range(2)]
            hn_l = [self.sb(es, f"ff_hn{i}", [P, 16, TT], BF16) for i in range(1)]
            h_l = [self.sb(es, f"ff_h{i}", [P, 44, TT], BF16) for i in range(1)]
            wb = [self.sb(es, f"ff_w{i}", [P, 16, 512], BF16) for i in range(2)]
            wb2 = [self.sb(es, f"ff_v{i}", [P, 44, 128], BF16) for i in range(2)]
            sg = [self.sb(es, f"ff_sg{i}", [P, TT]) for i in range(5)]
            xo = [self.sb(es, f"ff_x{i}", [P, TT]) for i in range(3)]
            Xi = Xin.rearrange("(fc p) t -> p fc t", p=P)
            Xo = Xout.rearrange("(fc p) t -> p fc t", p=P)
            k = [0]
            for ti in range(self.Ttot // TT):
                t0 = ti * TT
                cond = self.tile_cond(t0)
                hn = hn_l[0]
                h = h_l[0]
                self.norm_tile((xs_l, sq_l, rs_l), Xin, t0, hn,
                               lambda fc: self.gs2[:, l, fc, cond:cond + 1],
                               lambda fc: self.mod[:, l, 48 + fc, cond:cond + 1])
                sgm = {}

                def epi1(mc, ps):
                    s = sg[mc % 5] if mc < 44 else None
                    if mc < 44:
                        c.op("act", lambda e: e.activation(out=s[:], in_=ps[:, :TT], func=AF.Silu),
                             reads=[ps], writes=[s])
                        sgm[mc] = s
                    else:
                        hc = mc - 44
                        s2 = sgm.pop(hc)
                        c.op("dve", lambda e: e.tensor_tensor(out=h[:, hc, :], in0=ps[:, :TT], in1=s2[:],
                                                              op=ALU.mult), reads=[ps, s2], writes=[h])
                Wv = self.ffn_w_in[l]
                for g in range(11):
                    self.gemm(Wv, 16, 88, hn, wb, epi1, chunk_filter=lambda mc, g=g: g * 4 <= mc < g * 4 + 4)
                    self.gemm(Wv, 16, 88, hn, wb, epi1, chunk_filter=lambda mc, g=g: 44 + g * 4 <= mc < 48 + g * 4)

                def epi2(mc, ps, t0=t0, cond=cond):
                    x = xo[k[0] % 3]
                    k[0] += 1
                    c.dma("sp", x[:], Xi[:, mc, t0:t0 + TT], writes=[x])
                    c.op("dve", lambda e: e.scalar_tensor_tensor(
                        out=x[:], in0=ps[:, :TT], scalar=self.mod[:, l, 80 + mc, cond:cond + 1], in1=x[:],
                        op0=ALU.mult, op1=ALU.add), reads=[ps, x, self.mod], writes=[x])
                    c.dma("sp", Xo[:, mc, t0:t0 + TT], x[:], reads=[x])
                self.gemm(self.ffn_w_out[l], 44, 16, h, wb2, epi2, gw=128)
            c.barrier()

    def ph_final(self, X):
        c = self.c
        TT = self.TT
        with ExitStack() as es:
            xs_l = [self.sb(es, f"fi_xs{i}", [P, 16, 256]) for i in range(2)]
            sq_l = [self.sb(es, f"fi_sq{i}", [P, 16, 256]) for i in range(1)]
            rs_l = [self.sb(es, f"fi_rs{i}", [P, 256]) for i in range(2)]
            yo = [self.sb(es, f"fi_y{i}", [P, 2048]) for i in range(2)]
            Xv = X.rearrange("(fc p) t -> p fc t", p=P)
            SUB = 256
            for s in range(self.Ttot // SUB):
                xs, sq, rs = xs_l[s % 2], sq_l[0], rs_l[s % 2]
                c0 = s * SUB
                c.dma("sp", xs[:], Xv[:, :, c0:c0 + SUB], writes=[xs])
                c.op("act", lambda e, xs=xs, sq=sq: e.activation(out=sq[:], in_=xs[:], func=AF.Square),
                     reads=[xs], writes=[sq])
                ps = self.next_ps()
                for kc in range(16):
                    c.op("pe", lambda e, kc=kc, ps=ps, sq=sq: e.matmul(
                        ps[:, :SUB], self.onesF[:], sq[:, kc, :], start=(kc == 0), stop=(kc == 15)),
                        reads=[self.onesF, sq], writes=[ps])
                c.op("act", lambda e, ps=ps, rs=rs: e.activation(
                    out=rs[:], in_=ps[:, :SUB], func=AF.Sqrt, scale=1.0 / self.D, bias=self.eps_t[:]),
                    reads=[ps, self.eps_t], writes=[rs])
                c.op("dve", lambda e, rs=rs: e.reciprocal(out=rs[:], in_=rs[:]), reads=[rs], writes=[rs])
                c.op("dve", lambda e, xs=xs, rs=rs: e.tensor_tensor(
                    out=xs[:], in0=xs[:], in1=rs[:].unsqueeze(1).broadcast_to([P, 16, SUB]), op=ALU.mult),
                    reads=[xs, rs], writes=[xs])
                c.op("dve", lambda e, xs=xs: e.tensor_tensor(
                    out=xs[:], in0=xs[:], in1=self.fngt[:].unsqueeze(2).broadcast_to([P, 16, SUB]), op=ALU.mult),
                    reads=[xs, self.fngt], writes=[xs])
                for tb in range(SUB // P):
                    y = yo[(s * 2 + tb) % 2]
                    for g in range(4):
                        ps = self.next_ps()
                        for j in range(4):
                            fc = g * 4 + j
                            c.op("pe", lambda e, fc=fc, j=j, ps=ps, xs=xs, tb=tb: e.transpose(
                                ps[:, j * P:(j + 1) * P], xs[:, fc, tb * P:(tb + 1) * P], self.identF[:]),
                                reads=[xs, self.identF], writes=[ps])
                        if g % 2:
                            c.op("act", lambda e, g=g, ps=ps, y=y: e.copy(out=y[:, g * 512:(g + 1) * 512], in_=ps[:]),
                                 reads=[ps], writes=[y])
                        else:
                            c.op("dve", lambda e, g=g, ps=ps, y=y: e.tensor_copy(out=y[:, g * 512:(g + 1) * 512],
                                                                                in_=ps[:]), reads=[ps], writes=[y])
                    r0 = c0 + tb * P
                    c.dma("sp", self.y_out[r0:r0 + P, :], y[:], reads=[y])
            c.barrier()


    def ph_attn(self, l2):
        c, nc = self.c, self.nc
        Ls, Lp, NPR, past = self.Ls, self.Lp, self.NPR, self.past
        Zv = self.Z.rearrange("(mc p) t -> p mc t", p=P)
        Mv = self.MIX.rearrange("(kc p) t -> p kc t", p=P)
        npb = past // P
        with ExitStack() as es:
            Lmax = max(Ls, Lp)
            QT = self.sb(es, "at_qt", [P, 8, Lmax], BF16)
            KD = self.sb(es, "at_kd", [P, 4, Lmax], BF16)
            VT = self.sb(es, "at_vt", [P, Lmax // P, 256], BF16)
            CK = self.sb(es, "at_ck", [P, 4, past], BF16)
            CV = self.sb(es, "at_cv", [P, npb, 256], BF16)
            MB = self.sb(es, "at_mb", [P, 2, P], BF16)
            onesB = self.sb(es, "at_ones", [P, 64], BF16)
            sk = self.sb(es, "at_sk", [P, 8])
            qa = [self.sb(es, f"at_qa{i}", [P, 12, P]) for i in range(2)]
            qb = [self.sb(es, f"at_qb{i}", [P, 12, P]) for i in range(2)]
            cs = [self.sb(es, f"at_cs{i}", [P, 2, P]) for i in range(2)]
            vin = [self.sb(es, f"at_vin{i}", [P, 2, P]) for i in range(2)]
            kvo = [self.sb(es, f"at_kvo{i}", [P, 2, 256]) for i in range(2)]
            ckin = self.sb(es, "at_ckin", [P, npb, 4, 2, 64])
            PT = [self.sb(es, f"at_pt{i}", [P, 8, 512], BF16) for i in range(2)]
            den = [self.sb(es, f"at_den{i}", [P, 256]) for i in range(2)]
            ao = [self.sb(es, f"at_ao{i}", [P, 2, P], BF16) for i in range(2)]
            c.op("dve", lambda e: e.memset(onesB[:], 1.0), writes=[onesB])
            c.dma("pool", MB[:], self.maskb.rearrange("a p q -> p a q"), writes=[MB])
            c.dma("sp", sk[:], self.sinkfm[l2], writes=[sk])
            c.op("act", lambda e: e.activation(out=sk[:], in_=sk[:], func=AF.Exp), reads=[sk], writes=[sk])
            c.dma("pool", CV[:], self.cache_v[l2].rearrange("(kb p) f -> p kb f", p=P), writes=[CV])
            ckv = self.cache_k[l2].rearrange("(kb p) (g d) -> p kb g d", p=P, d=64)
            for kb in range(npb):
                c.dma("sp", ckin[:, kb, :, 0, :], ckv[:, kb], writes=[ckin])
                c.dma("sp", ckin[:, kb, :, 1, :], ckv[:, kb], writes=[ckin])
            for kb in range(npb):
                ps = self.next_ps()
                for g in range(4):
                    c.op("pe", lambda e, ps=ps, g=g, kb=kb: e.transpose(
                        ps[:, g * P:(g + 1) * P], ckin[:, kb, g].rearrange("p a d -> p (a d)"), self.identF[:]),
                        reads=[ckin, self.identF], writes=[ps])
                c.op("act", lambda e, ps=ps, kb=kb: e.copy(
                    out=CK[:, :, kb * P:(kb + 1) * P], in_=ps[:].rearrange("p (g k) -> p g k", g=4)),
                    reads=[ps], writes=[CK])
            it = 0
            import os
            STG = int(os.environ.get("ATT_STAGE", "9"))
            for si in range((1 + NPR) if STG >= 1 else 0):
                ctx = si > 0
                L = Lp if ctx else Ls
                s0 = 0 if not ctx else Ls + (si - 1) * Lp
                nb = L // P
                for b in range(nb):
                    t0 = s0 + b * P
                    a, bb = qa[b % 2], qb[b % 2]
                    c.dma("sp", a[:, 0:8], Zv[:, 28:36, t0:t0 + P], writes=[a])
                    c.dma("sp", a[:, 8:12], Zv[:, 44:48, t0:t0 + P], writes=[a])
                    SK = os.environ.get("SK", "")
                    if not ctx and "r" not in SK:
                        c.dma("sp", bb[:, 0:8], Zv[:, 36:44, t0:t0 + P], writes=[bb])
                        c.dma("sp", bb[:, 8:12], Zv[:, 48:52, t0:t0 + P], writes=[bb])
                        cst = cs[b % 2]
                        c.dma("sp", cst[:, 0], self.ropec[:, b * P:(b + 1) * P], writes=[cst])
                        c.dma("sp", cst[:, 1], self.ropes[:, b * P:(b + 1) * P], writes=[cst])
                        c.op("dve", lambda e, a=a, cst=cst: e.tensor_tensor(
                            out=a[:], in0=a[:], in1=cst[:, 0:1, :].broadcast_to([P, 12, P]), op=ALU.mult),
                            reads=[a, cst], writes=[a])
                        c.op("dve", lambda e, bb=bb, cst=cst: e.tensor_tensor(
                            out=bb[:], in0=bb[:], in1=cst[:, 1:2, :].broadcast_to([P, 12, P]), op=ALU.mult),
                            reads=[bb, cst], writes=[bb])
                        c.op("dve", lambda e, a=a, bb=bb: e.tensor_tensor(out=a[:], in0=a[:], in1=bb[:], op=ALU.add),
                             reads=[a, bb], writes=[a])
                    if "q" not in SK:
                        c.op("act", lambda e, a=a, b=b: e.copy(out=QT[:, :, b * P:(b + 1) * P], in_=a[:, 0:8]),
                             reads=[a], writes=[QT])
                        c.op("act", lambda e, a=a, b=b: e.copy(out=KD[:, :, b * P:(b + 1) * P], in_=a[:, 8:12]),
                             reads=[a], writes=[KD])
                    if "v" in SK:
                        continue
                    v = vin[b % 2]
                    c.dma("sp", v[:], Zv[:, 52:54, t0:t0 + P], writes=[v])
                    ps = self.next_ps()
                    for j in range(2):
                        c.op("pe", lambda e, ps=ps, j=j, v=v: e.transpose(
                            ps[:, j * P:(j + 1) * P], v[:, j, :], self.identF[:]), reads=[v, self.identF], writes=[ps])
                    if ctx and "k" not in SK:
                        for g in range(4):
                            c.op("pe", lambda e, ps=ps, g=g, a=a: e.transpose(
                                ps[:, 256 + g * 64:256 + (g + 1) * 64], a[0:64, 8 + g, :], self.identF[0:64, 0:64]),
                                reads=[a, self.identF], writes=[ps])
                    c.op("dve", lambda e, ps=ps, b=b: e.tensor_copy(out=VT[:, b, :], in_=ps[:, 0:256]),
                         reads=[ps], writes=[VT])
                    if ctx and "o" not in SK:
                        ko = kvo[b % 2]
                        if "c" not in SK:
                            c.op("act", lambda e, ps=ps, ko=ko: e.copy(
                                out=ko[:], in_=ps[:].rearrange("p (a f) -> p a f", a=2)), reads=[ps], writes=[ko])
                        if "d" not in SK:
                            c.dma("sp", self.ncv[si - 1, l2, b * P:(b + 1) * P, :], ko[:, 0, :], reads=[ko])
                            c.dma("sp", self.nck[si - 1, l2, b * P:(b + 1) * P, :], ko[:, 1, :], reads=[ko])
                for b in range(nb if STG >= 2 else 0):
                    if ctx:
                        kbs = [("w", j, None) for j in range(nb)]
                    else:
                        kbs = [("w", j, j - b) for j in (b - 1, b, b + 1) if 0 <= j < nb] + [("c", j, None) for j in range(npb)]
                    for g in range(4):
                        pt = PT[it % 2]
                        it += 1
                        for ki, (kind, j, rel) in enumerate(kbs):
                            masked = rel is not None and rel != 0
                            for hp in range(2):
                                ps = self.next_ps()
                                if masked:
                                    mi = 0 if rel < 0 else 1
                                    c.op("pe", lambda e, ps=ps, mi=mi: e.matmul(
                                        ps[:, 0:256].rearrange("p (h q) -> p h q", h=2), self.identB[:],
                                        MB[:, mi:mi + 1, :].broadcast_to([P, 2, P]), start=True, stop=False),
                                        reads=[self.identB, MB], writes=[ps])
                                for ii in range(2):
                                    h = 4 * g + 2 * ii + hp
                                    ksrc = KD[hp * 64:(hp + 1) * 64, g, j * P:(j + 1) * P] if kind == "w" else \
                                        CK[hp * 64:(hp + 1) * 64, g, j * P:(j + 1) * P]
                                    c.op("pe", lambda e, ps=ps, ii=ii, ksrc=ksrc, hp=hp, h=h, masked=masked: e.matmul(
                                        ps[:, ii * P:(ii + 1) * P], ksrc,
                                        QT[hp * 64:(hp + 1) * 64, h // 2, b * P:(b + 1) * P],
                                        start=(ii == 0 and not masked), stop=(ii == 1)),
                                        reads=[KD, CK, QT], writes=[ps])
                                c.op("act", lambda e, ps=ps, ki=ki, pt=pt, hp=hp: e.activation(
                                    out=pt[:, ki, :].rearrange("p (i q) -> p i q", i=4)[:, hp::2, :],
                                    in_=ps[:, 0:256].rearrange("p (i q) -> p i q", i=2), func=AF.Exp, scale=0.125),
                                    reads=[ps], writes=[pt])
                        po = self.next_ps()
                        pd = self.next_ps()
                        nk = len(kbs)
                        for ki, (kind, j, rel) in enumerate(kbs if STG >= 3 else []):
                            vsrc = VT[:, j, g * 64:(g + 1) * 64] if kind == "w" else CV[:, j, g * 64:(g + 1) * 64]
                            for hp in range(2):
                                rhs = pt[:, ki, :].rearrange("p (i q) -> p i q", i=4)[:, hp::2, :]
                                c.op("pe", lambda e, po=po, hp=hp, vsrc=vsrc, rhs=rhs, ki=ki: e.matmul(
                                    po[hp * 64:(hp + 1) * 64, 0:256].rearrange("p (i q) -> p i q", i=2), vsrc, rhs,
                                    start=(ki == 0), stop=(ki == nk - 1), tile_position=(0, hp * 64)),
                                    reads=[VT, CV, pt], writes=[po])
                                c.op("pe", lambda e, pd=pd, hp=hp, rhs=rhs, ki=ki: e.matmul(
                                    pd[hp * 64:(hp + 1) * 64, 0:256].rearrange("p (i q) -> p i q", i=2), onesB[:], rhs,
                                    start=(ki == 0), stop=(ki == nk - 1), tile_position=(0, hp * 64)),
                                    reads=[onesB, pt], writes=[pd])
                        if STG < 4:
                            continue
                        dn = den[it % 2]
                        o = ao[it % 2]
                        c.op("dve", lambda e, pd=pd, dn=dn, g=g: e.tensor_tensor(
                            out=dn[:].rearrange("p (i q) -> p i q", i=2),
                            in0=pd[:, 0:256].rearrange("p (i q) -> p i q", i=2),
                            in1=sk[:, 2 * g:2 * g + 2].unsqueeze(2).broadcast_to([P, 2, P]), op=ALU.add),
                            reads=[pd, sk], writes=[dn])
                        c.op("dve", lambda e, dn=dn: e.reciprocal(out=dn[:], in_=dn[:]), reads=[dn], writes=[dn])
                        c.op("dve", lambda e, po=po, dn=dn, o=o: e.tensor_tensor(
                            out=o[:], in0=po[:, 0:256].rearrange("p (i q) -> p i q", i=2),
                            in1=dn[:].rearrange("p (i q) -> p i q", i=2), op=ALU.mult), reads=[po, dn], writes=[o])
                        t0 = s0 + b * P
                        c.dma("sp", Mv[:, 8 + 2 * g:8 + 2 * g + 2, t0:t0 + P], o[:], reads=[o])
            c.barrier()


    def ph_rw0(self, l2):
        c = self.c
        Ls, Lp, NPR = self.Ls, self.Lp, self.NPR
        Zv = self.Z.rearrange("(mc p) t -> p mc t", p=P)
        RWv = [a.rearrange("(fc p) t -> p fc t", p=P) for a in self.RW]
        with ExitStack() as es:
            sb = lambda n, sh, dt=F32: self.sb(es, "r0_" + n, sh, dt)
            mu, w0a0, vec = sb("mu", [P, 2, 28]), sb("w0a0", [P, 2, 2, 8]), sb("vec", [P, 5, 8])
            w2s, a2s, g2s = sb("w2", [P, 1024], BF16), sb("a2", [P, 1024], BF16), sb("g2", [P, 2, 1024], BF16)
            blk, omk, eps12 = sb("blk", [P, P]), sb("omk", [P, 8]), sb("eps12", [P, 1])
            zt, t1, t2 = sb("zt", [P, 28, P + 2]), sb("t1", [P, 28, P]), sb("t2", [P, 28, P])
            twd, adb, sgd = sb("twd", [P, P], BF16), sb("adb", [P, P], BF16), sb("sgd", [P, 2, P], BF16)
            LW = [sb(f"lw{i}", [P, 8, P]) for i in range(2)]
            AD = [sb(f"ad{i}", [P, 8, P]) for i in range(2)]
            KD = [sb(f"kd{i}", [P, 8, P]) for i in range(2)]
            BD = [sb(f"bd{i}", [P, 8, P]) for i in range(2)]
            G, KK, SQ, BV, U = sb("g", [P, 8, P]), sb("kk", [P, 8, P]), sb("sq", [P, 8, P]), sb("bv", [P, 8, P]), sb("u", [P, 8, P])
            c.dma("sp", mu[:], self.r_mu[l2], writes=[mu])
            c.dma("sp", w0a0[:], self.r_w0a0[l2], writes=[w0a0])
            c.dma("sp", vec[:], self.r_vec[l2], writes=[vec])
            c.dma("pool", w2s[:], self.r_w2[l2], writes=[w2s])
            c.dma("pool", a2s[:], self.r_a2[l2], writes=[a2s])
            c.dma("pool", g2s[:], self.r_g2[l2], writes=[g2s])
            c.dma("sp", blk[:], self.r_blk, writes=[blk])
            c.op("dve", lambda e: e.memset(eps12[:], 1e-12), writes=[eps12])
            c.op("dve", lambda e: e.tensor_scalar(out=omk[:], in0=vec[:, 1, :], scalar1=-1.0, scalar2=1.0,
                                                  op0=ALU.mult, op1=ALU.add), reads=[vec], writes=[omk])
            bc = lambda ap: ap.unsqueeze(2).broadcast_to([P, ap.shape[1], P])
            for si in range(1 + NPR):
                ctx = si > 0
                L = Lp if ctx else Ls
                s0 = 0 if not ctx else Ls + (si - 1) * Lp
                for b in range(L // P):
                    t0 = s0 + b * P
                    lo = 1 if b == 0 else 0
                    hi = P + 1 if b == L // P - 1 else P + 2
                    if lo or hi < P + 2:
                        c.op("dve", lambda e: e.memset(zt[:], 0.0), writes=[zt])
                    c.dma("sp", zt[:, :, lo:hi], Zv[:, 0:28, t0 - 1 + lo:t0 - 1 + hi], writes=[zt])
                    z = zt[:, :, 1:P + 1]
                    c.op("dve", lambda e: e.tensor_tensor(out=t1[:], in0=zt[:, :, 0:P], in1=z, op=ALU.subtract), reads=[zt], writes=[t1])
                    c.op("dve", lambda e: e.tensor_tensor(out=t1[:], in0=t1[:], in1=bc(mu[:, 0, :]), op=ALU.mult), reads=[t1, mu], writes=[t1])
                    c.op("dve", lambda e: e.tensor_tensor(out=t2[:], in0=zt[:, :, 2:P + 2], in1=z, op=ALU.subtract), reads=[zt], writes=[t2])
                    c.op("dve", lambda e: e.tensor_tensor(out=t2[:], in0=t2[:], in1=bc(mu[:, 1, :]), op=ALU.mult), reads=[t2, mu], writes=[t2])
                    c.op("dve", lambda e: e.tensor_tensor(out=t1[:], in0=t1[:], in1=z, op=ALU.add), reads=[t1, zt], writes=[t1])
                    c.op("dve", lambda e: e.tensor_tensor(out=t1[:], in0=t1[:], in1=t2[:], op=ALU.add), reads=[t1, t2], writes=[t1])
                    zs = t1
                    c.op("act", lambda e: e.activation(out=twd[:], in_=zs[:, 24, :], func=AF.Tanh), reads=[zs], writes=[twd])
                    c.op("act", lambda e: e.copy(out=adb[:], in_=zs[:, 25, :]), reads=[zs], writes=[adb])
                    c.op("act", lambda e: e.activation(out=sgd[:], in_=zs[:, 26:28, :], func=AF.Sigmoid), reads=[zs], writes=[sgd])
                    for which, (wsb, src, dst) in enumerate(((w2s, twd, LW), (a2s, adb, AD))):
                        for d in range(2):
                            for half in range(2):
                                ps = self.next_ps()
                                for q in range(4):
                                    oc = half * 4 + q
                                    c.op("pe", lambda e, ps=ps, q=q, oc=oc, d=d, wsb=wsb, src=src: e.matmul(
                                        ps[:, q * P:(q + 1) * P], wsb[d * 64:(d + 1) * 64, oc * P:(oc + 1) * P],
                                        src[d * 64:(d + 1) * 64, :], start=True, stop=True), reads=[wsb, src], writes=[ps])
                                for q in range(4):
                                    oc = half * 4 + q
                                    c.op("act", lambda e, ps=ps, q=q, oc=oc, d=d, which=which, dst=dst: e.activation(
                                        out=dst[d][:, oc, :], in_=ps[:, q * P:(q + 1) * P], func=AF.Sigmoid,
                                        bias=w0a0[:, which, d, oc:oc + 1]), reads=[ps, w0a0], writes=[dst[d]])
                    for d in range(2):
                        c.op("dve", lambda e, d=d: e.tensor_scalar(out=LW[d][:], in0=LW[d][:], scalar1=-math.exp(-0.5),
                                                                  scalar2=None, op0=ALU.mult), reads=[LW[d]], writes=[LW[d]])
                    for half in range(2):
                        ps = self.next_ps()
                        for q in range(4):
                            oc = half * 4 + q
                            for sl in range(2):
                                c.op("pe", lambda e, ps=ps, q=q, oc=oc, sl=sl: e.matmul(
                                    ps[:, q * P:(q + 1) * P], g2s[:, sl, oc * P:(oc + 1) * P], sgd[:, sl, :],
                                    start=(sl == 0), stop=(sl == 1)), reads=[g2s, sgd], writes=[ps])
                        c.op("act", lambda e, ps=ps, half=half: e.copy(
                            out=G[:, half * 4:(half + 1) * 4, :].rearrange("p a b -> p (a b)"), in_=ps[:]), reads=[ps], writes=[G])
                    c.op("dve", lambda e: e.tensor_tensor(out=KK[:], in0=zs[:, 8:16, :], in1=bc(vec[:, 0, :]), op=ALU.mult),
                         reads=[zs, vec], writes=[KK])
                    c.op("act", lambda e: e.activation(out=SQ[:], in_=KK[:], func=AF.Square), reads=[KK], writes=[SQ])
                    for half in range(2):
                        ps = self.next_ps()
                        for q in range(4):
                            fc = half * 4 + q
                            c.op("pe", lambda e, ps=ps, q=q, fc=fc: e.matmul(ps[:, q * P:(q + 1) * P], blk[:], SQ[:, fc, :],
                                                                             start=True, stop=True), reads=[blk, SQ], writes=[ps])
                        c.op("act", lambda e, ps=ps, half=half: e.activation(
                            out=U[:, half * 4:(half + 1) * 4, :].rearrange("p a b -> p (a b)"), in_=ps[:], func=AF.Sqrt,
                            bias=eps12[:, 0:1]), reads=[ps, eps12], writes=[U])
                    c.op("dve", lambda e: e.reciprocal(out=U[:], in_=U[:]), reads=[U], writes=[U])
                    c.op("dve", lambda e: e.tensor_tensor(out=KK[:], in0=KK[:], in1=U[:], op=ALU.mult), reads=[KK, U], writes=[KK])
                    for d in range(2):
                        c.op("dve", lambda e, d=d: e.tensor_tensor(out=U[:], in0=AD[d][:], in1=bc(vec[:, 1, :]), op=ALU.mult),
                             reads=[AD[d], vec], writes=[U])
                        c.op("dve", lambda e: e.tensor_tensor(out=U[:], in0=U[:], in1=bc(omk[:]), op=ALU.add), reads=[U, omk], writes=[U])
                        c.op("dve", lambda e, d=d: e.tensor_tensor(out=KD[d][:], in0=U[:], in1=zs[:, 8:16, :], op=ALU.mult),
                             reads=[U, zs], writes=[KD[d]])
                        c.op("dve", lambda e, d=d: e.tensor_tensor(out=BD[d][:], in0=KK[:], in1=AD[d][:], op=ALU.mult),
                             reads=[KK, AD[d]], writes=[BD[d]])
                        c.op("dve", lambda e, d=d: e.tensor_tensor(out=AD[d][:], in0=KD[d][:], in1=zs[:, 0:8, :], op=ALU.mult),
                             reads=[KD[d], zs], writes=[AD[d]])
                        c.op("dve", lambda e, d=d: e.tensor_tensor(out=AD[d][:], in0=AD[d][:], in1=bc(vec[:, 2, :]), op=ALU.mult),
                             reads=[AD[d], vec], writes=[AD[d]])
                    for half in range(2):
                        ps = self.next_ps()
                        for q in range(4):
                            fc = half * 4 + q
                            for d in range(2):
                                c.op("pe", lambda e, ps=ps, q=q, fc=fc, d=d: e.matmul(
                                    ps[:, q * P:(q + 1) * P], blk[:], AD[d][:, fc, :], start=(d == 0), stop=(d == 1)),
                                    reads=[blk, AD[d]], writes=[ps])
                        c.op("dve", lambda e, ps=ps, half=half: e.tensor_tensor(
                            out=BV[:, half * 4:(half + 1) * 4, :], in0=ps[:].rearrange("p (a b) -> p a b", a=4),
                            in1=zs[:, 16 + half * 4:20 + half * 4, :], op=ALU.mult), reads=[ps, zs], writes=[BV])
                    outs = [(0, zs[:, 0:8, :], zs), (1, zs[:, 16:24, :], zs), (2, KK[:], KK), (3, G[:], G), (4, BV[:], BV),
                            (5, LW[0][:], LW[0]), (6, LW[1][:], LW[1]), (7, KD[0][:], KD[0]), (8, KD[1][:], KD[1]),
                            (9, BD[0][:], BD[0]), (10, BD[1][:], BD[1])]
                    for idx, ap, tl in outs:
                        c.dma("sp", RWv[idx][:, :, t0:t0 + P], ap, reads=[tl])
            c.barrier()

    def ph_rw1(self, l2, d):
        c = self.c
        Ls, Lp, NPR = self.Ls, self.Lp, self.NPR
        RWv = [a.rearrange("(fc p) t -> p fc t", p=P) for a in self.RW]
        Mv = self.MIX.rearrange("(kc p) t -> p kc t", p=P)
        C = 64
        with ExitStack() as es:
            sb = lambda n, sh, dt=F32: self.sb(es, "r1_" + n, sh, dt)
            f4 = lambda n: sb(n, [P, 8, P])
            R, V, KKN, LWt, KDt, Bt = f4("R"), f4("V"), f4("KKN"), f4("LW"), f4("KD"), f4("B")
            cum, Wt, Wi, Wp, Wh, tmp = f4("cum"), f4("Wt"), f4("Wi"), f4("Wp"), f4("Wh"), f4("tmp")
            Yt, YFt = f4("Yt"), f4("YFt")
            b4 = lambda n: sb(n, [P, 8, 2, P], BF16)
            AR, AtD, KtD, KhD, VD = b4("AR"), b4("AtD"), b4("KtD"), b4("KhD"), b4("VD")
            ARf, BtDf, BsF = sb("ARf", [P, 8, 2, P]), sb("BtDf", [P, 8, 2, P]), sb("BsF", [P, 8, 2, C])
            AtDf, BhDf = sb("AtDf", [P, 8, 2, P]), sb("BhDf", [P, 8, 2, P])
            TtDf, btDf = sb("TtDf", [P, 8, P]), sb("btDf", [P, 8, P])
            RHSf, USf = sb("RHSf", [P, 8, C]), sb("USf", [P, 8, C])
            sS = lambda n: sb(n, [P, 8, C], BF16)
            sD = lambda n: sb(n, [P, 8, P], BF16)
            fS = lambda n: sb(n, [P, 8, C])
            fD = lambda n: sb(n, [P, 8, P])
            XS, XD, YS, YD = [fS("XS0"), fS("XS1")], [fD("XD0"), fD("XD1")], [fS("YS0"), fS("YS1")], [fD("YD0"), fD("YD1")]
            AakD, ArkS, ArbS = sD("AakD"), sS("ArkS"), sS("ArbS")
            TtF = sb("TtF", [P, 8, C])
            VtD, VtS, ktD = sD("VtD"), sS("VtS"), sD("ktD")
            US, UD = sS("US"), sD("UD")
            SF, SBf, STD, stmp = sb("SF", [P, 8, C]), sS("SB"), sD("STD"), sb("stmp", [P, 8, C])
            cumC, Wc = sb("cumC", [P, 8, 2]), sb("Wc", [P, 8, 2])
            mN, mS = sb("mN", [P, C]), sb("mS", [P, P])
            IS, blk64, ones64 = sb("IS", [P, C]), sb("blk64", [P, P]), sb("ones64", [P, C])
            vec, epsg = sb("vec", [P, 5, 8]), sb("epsg", [P, 1])
            sio = sb("sio", [C, 8, P])
            Gt, BVt, yo = f4("Gt"), f4("BVt"), sb("yo", [P, 8, P], BF16)
            c.dma("sp", mN[:], self.r_maskn[d], writes=[mN])
            c.dma("sp", mS[:], self.r_masks[d], writes=[mS])
            c.dma("sp", IS[:], self.r_is, writes=[IS])
            c.dma("sp", blk64[:], self.r_blk, writes=[blk64])
            c.dma("sp", vec[:], self.r_vec[l2], writes=[vec])
            c.op("dve", lambda e: e.tensor_scalar(out=blk64[:], in0=blk64[:], scalar1=1.0 / 64, scalar2=None, op0=ALU.mult),
                 reads=[blk64], writes=[blk64])
            c.op("dve", lambda e: e.memset(ones64[:], 1.0), writes=[ones64])
            c.op("dve", lambda e: e.memset(epsg[:], 64e-5), writes=[epsg])
            for t in (AtD, KtD, BtDf, KhD, VD, XD[0], XD[1], YD[0], YD[1], AakD, UD, STD, AtDf, BhDf, TtDf):
                c.op("dve", lambda e, t=t: e.memset(t[:], 0.0), writes=[t])
            H = ((0, 64), (64, 128))
            bcv = lambda ap, n: ap.unsqueeze(1).broadcast_to([ap.shape[0], n, ap.shape[1]])

            def to_diag(dst, src, rd):
                for i, (a, b) in enumerate(H):
                    eng = "act" if i else "dve"
                    if eng == "act":
                        c.op("act", lambda e, a=a, b=b: e.copy(out=dst[a:b, :, a:b], in_=src[a:b, :, :]), reads=rd, writes=[dst])
                    else:
                        c.op("dve", lambda e, a=a, b=b: e.tensor_copy(out=dst[a:b, :, a:b], in_=src[a:b, :, :]), reads=rd, writes=[dst])

            def mm8(out_ps, cols, lhs_fn, rhs_fn, reads, groups=1, accum=None):
                for fc in range(8):
                    terms = lhs_fn(fc) if isinstance(lhs_fn(fc), list) else [lhs_fn(fc)]
                    rterms = rhs_fn(fc) if isinstance(rhs_fn(fc), list) else [rhs_fn(fc)]
                    n = len(terms)
                    for i in range(n):
                        c.op("pe", lambda e, fc=fc, i=i, terms=terms, rterms=rterms, n=n: e.matmul(
                            out_ps[:, fc * cols:(fc + 1) * cols], terms[i], rterms[i], start=(i == 0), stop=(i == n - 1)),
                            reads=reads, writes=[out_ps])

            for si in range(1 + NPR):
                ctx = si > 0
                L = Lp if ctx else Ls
                s0 = 0 if not ctx else Ls + (si - 1) * Lp
                if ctx:
                    c.op("dve", lambda e: e.memset(SF[:], 0.0), writes=[SF])
                else:
                    for hh in range(2):
                        c.dma("sp", sio[:, :, hh * C:(hh + 1) * C],
                              self.state_rwkv[l2, d].rearrange("(fc hh) v k -> hh v fc k", hh=2)[hh], writes=[sio])
                    ps = self.next_ps()
                    for fc in range(8):
                        c.op("pe", lambda e, ps=ps, fc=fc: e.transpose(ps[:, fc * C:(fc + 1) * C], sio[:, fc, :],
                                                                       self.identF[0:C, 0:C]), reads=[sio, self.identF], writes=[ps])
                    c.op("dve", lambda e, ps=ps: e.tensor_copy(out=SF[:].rearrange("p a b -> p (a b)"), in_=ps[:]), reads=[ps], writes=[SF])
                c.op("act", lambda e: e.copy(out=SBf[:], in_=SF[:]), reads=[SF], writes=[SBf])
                to_diag(STD, SBf, [SBf])
                nt = L // P
                for b in (range(nt) if d == 0 else range(nt - 1, -1, -1)):
                    t0 = s0 + b * P
                    for tl, idx in ((R, 0), (V, 1), (KKN, 2), (LWt, 5 + d), (KDt, 7 + d), (Bt, 9 + d)):
                        c.dma("sp", tl[:], RWv[idx][:, :, t0:t0 + P], writes=[tl])
                    for fc in range(8):
                        for cc in range(2):
                            sl = slice(cc * C, (cc + 1) * C)
                            if d == 0:
                                c.op("dve", lambda e, fc=fc, sl=sl: e.tensor_tensor_scan(
                                    out=cum[:, fc, sl], data0=ones64[:], data1=LWt[:, fc, sl], initial=0.0,
                                    op0=ALU.mult, op1=ALU.add), reads=[ones64, LWt], writes=[cum])
                            else:
                                c.op("dve", lambda e, fc=fc, sl=sl: e.tensor_tensor_scan(
                                    out=cum[:, fc, sl][:, ::-1], data0=ones64[:], data1=LWt[:, fc, sl][:, ::-1], initial=0.0,
                                    op0=ALU.mult, op1=ALU.add), reads=[ones64, LWt], writes=[cum])
                    cend = (C - 1) if d == 0 else 0
                    c.op("dve", lambda e: e.tensor_copy(out=cumC[:], in_=cum[:].rearrange("p f (c t) -> p f c t", c=2)[:, :, :, cend]),
                         reads=[cum], writes=[cumC])
                    c.op("act", lambda e: e.activation(out=Wc[:], in_=cumC[:], func=AF.Exp), reads=[cumC], writes=[Wc])
                    c.op("act", lambda e: e.activation(out=Wt[:], in_=cum[:], func=AF.Exp), reads=[cum], writes=[Wt])
                    c.op("act", lambda e: e.activation(out=Wi[:], in_=cum[:], func=AF.Exp, scale=-1.0), reads=[cum], writes=[Wi])
                    c.op("dve", lambda e: e.tensor_tensor(out=tmp[:], in0=cum[:], in1=LWt[:], op=ALU.subtract), reads=[cum, LWt], writes=[tmp])
                    c.op("act", lambda e: e.activation(out=Wp[:], in_=tmp[:], func=AF.Exp), reads=[tmp], writes=[Wp])
                    for fc in range(8):
                        for cc in range(2):
                            sl = slice(cc * C, (cc + 1) * C)
                            c.op("act", lambda e, fc=fc, cc=cc, sl=sl: e.activation(
                                out=Wh[:, fc, sl], in_=cum[:, fc, sl], func=AF.Exp, scale=-1.0, bias=cumC[:, fc, cc:cc + 1]),
                                reads=[cum, cumC], writes=[Wh])
                    v4 = lambda t: t[:].rearrange("p f (c t) -> p f c t", c=2)
                    c.op("dve", lambda e: e.scalar_tensor_tensor(out=ARf[:, :, :, 0:C], in0=v4(KKN), scalar=-1.0, in1=v4(Wp),
                                                                 op0=ALU.mult, op1=ALU.mult), reads=[KKN, Wp], writes=[ARf])
                    c.op("dve", lambda e: e.tensor_tensor(out=ARf[:, :, :, C:P], in0=v4(R), in1=v4(Wt), op=ALU.mult), reads=[R, Wt], writes=[ARf])
                    c.op("act", lambda e: e.copy(out=AR[:], in_=ARf[:]), reads=[ARf], writes=[AR])
                    c.op("dve", lambda e: e.tensor_tensor(out=BsF[:], in0=v4(Bt), in1=v4(Wi), op=ALU.mult), reads=[Bt, Wi], writes=[BsF])
                    for a, bb in H:
                        hv = lambda t, a=a, bb=bb: t[a:bb].rearrange("p f (c t) -> p f c t", c=2)
                        c.op("dve", lambda e, a=a, bb=bb: e.tensor_copy(out=AtD[a:bb, :, :, a:bb], in_=AR[a:bb, :, :, 0:C]), reads=[AR], writes=[AtD])
                        c.op("dve", lambda e, a=a, bb=bb, hv=hv: e.scalar_tensor_tensor(
                            out=AtDf[a:bb, :, :, a:bb], in0=hv(KKN), scalar=-1.0, in1=hv(Wp), op0=ALU.mult, op1=ALU.mult),
                            reads=[KKN, Wp], writes=[AtDf])
                        c.op("dve", lambda e, a=a, bb=bb, hv=hv: e.tensor_tensor(out=BhDf[a:bb, :, :, a:bb], in0=hv(Bt), in1=hv(Wh), op=ALU.mult),
                             reads=[Bt, Wh], writes=[BhDf])
                        c.op("dve", lambda e, a=a, bb=bb, hv=hv: e.tensor_tensor(out=KtD[a:bb, :, :, a:bb], in0=hv(KDt), in1=hv(Wi), op=ALU.mult),
                             reads=[KDt, Wi], writes=[KtD])
                        c.op("dve", lambda e, a=a, bb=bb: e.tensor_copy(out=BtDf[a:bb, :, :, a:bb], in_=BsF[a:bb]), reads=[BsF], writes=[BtDf])
                        c.op("dve", lambda e, a=a, bb=bb, hv=hv: e.tensor_tensor(out=KhD[a:bb, :, :, a:bb], in0=hv(KDt), in1=hv(Wh), op=ALU.mult),
                             reads=[KDt, Wh], writes=[KhD])
                        c.op("act", lambda e, a=a, bb=bb, hv=hv: e.copy(out=VD[a:bb, :, :, a:bb], in_=hv(V)), reads=[V], writes=[VD])
                    for cc in ((0, 1) if d == 0 else (1, 0)):
                        pN = self.next_ps()
                        mm8(pN, C, lambda fc: AtDf[:, fc, cc, :], lambda fc: BsF[:, fc, cc, :], [AtDf, BsF])
                        c.op("dve", lambda e, pN=pN: e.tensor_tensor(out=XS[0][:], in0=pN[:].rearrange("p (f s) -> p f s", f=8),
                                                                     in1=bcv(mN[:], 8), op=ALU.mult), reads=[pN, mN], writes=[XS[0]])
                        to_diag(XD[0], XS[0], [XS[0]])
                        for which, lD in ((0, KtD), (1, BtDf)):
                            for half in range(2):
                                ps = self.next_ps()
                                for q in range(4):
                                    fc = half * 4 + q
                                    rA = AR if which == 0 else ARf
                                    c.op("pe", lambda e, ps=ps, q=q, fc=fc, lD=lD, rA=rA: e.matmul(
                                        ps[:, q * P:(q + 1) * P], lD[:, fc, cc, :], rA[:, fc, cc, :], start=True, stop=True),
                                        reads=[lD, rA], writes=[ps])
                                pv = ps[:].rearrange("p (f s) -> p f s", f=4)
                                fs = slice(half * 4, half * 4 + 4)
                                dstA = (AakD if which == 0 else YS[0])
                                dstR = (ArkS if which == 0 else ArbS)
                                if which == 0:
                                    for a, bb in H:
                                        c.op("dve", lambda e, a=a, bb=bb, pv=pv, fs=fs: e.tensor_tensor(
                                            out=AakD[a:bb, fs, a:bb], in0=pv[a:bb, :, 0:C], in1=bcv(mS[a:bb, 0:C], 4), op=ALU.mult),
                                            reads=[ps, mS], writes=[AakD])
                                else:
                                    c.op("dve", lambda e, pv=pv, fs=fs: e.tensor_tensor(
                                        out=YS[0][:, fs, :], in0=pv[:, :, 0:C], in1=bcv(mS[:, 0:C], 4), op=ALU.mult),
                                        reads=[ps, mS], writes=[YS[0]])
                                c.op("dve", lambda e, pv=pv, fs=fs, dstR=dstR: e.tensor_tensor(
                                    out=dstR[:, fs, :], in0=pv[:, :, C:P], in1=bcv(mS[:, C:P], 4), op=ALU.mult),
                                    reads=[ps, mS], writes=[dstR])
                        to_diag(YD[0], YS[0], [YS[0]])
                        c.op("dve", lambda e: e.tensor_tensor(out=TtF[:], in0=YS[0][:], in1=bcv(IS[:], 8), op=ALU.add),
                             reads=[YS[0], IS], writes=[TtF])
                        cur = 0
                        for lvl in range(1, 6):
                            nx = 1 - cur
                            pA = self.next_ps()
                            mm8(pA, C, lambda fc: YD[cur][:, fc, :], lambda fc: XS[cur][:, fc, :], [YD[cur], XS[cur]])
                            if lvl < 5:
                                pB = self.next_ps()
                                mm8(pB, C, lambda fc: XD[cur][:, fc, :], lambda fc: YS[cur][:, fc, :], [XD[cur], YS[cur]])
                            c.op("dve", lambda e, pA=pA, nx=nx: e.tensor_copy(out=XS[nx][:].rearrange("p a b -> p (a b)"), in_=pA[:]),
                                 reads=[pA], writes=[XS[nx]])
                            to_diag(XD[nx], XS[nx], [XS[nx]])
                            if lvl < 5:
                                c.op("act", lambda e, pB=pB, nx=nx: e.copy(out=YS[nx][:].rearrange("p a b -> p (a b)"), in_=pB[:]),
                                     reads=[pB], writes=[YS[nx]])
                                to_diag(YD[nx], YS[nx], [YS[nx]])
                            pT = self.next_ps()
                            mm8(pT, C, lambda fc: XD[nx][:, fc, :], lambda fc: TtF[:, fc, :], [XD[nx], TtF])
                            c.op("dve", lambda e, pT=pT: e.tensor_tensor(out=TtF[:].rearrange("p a b -> p (a b)"),
                                                                         in0=TtF[:].rearrange("p a b -> p (a b)"), in1=pT[:], op=ALU.add),
                                 reads=[TtF, pT], writes=[TtF])
                            cur = nx
                        to_diag(TtDf, TtF, [TtF])
                        for half in range(2):
                            ps = self.next_ps()
                            for q in range(4):
                                fc = half * 4 + q
                                c.op("pe", lambda e, ps=ps, q=q, fc=fc: e.transpose(ps[:, q * P:(q + 1) * P], BhDf[:, fc, cc, :], self.identF[:]),
                                     reads=[BhDf, self.identF], writes=[ps])
                            c.op("dve", lambda e, ps=ps, half=half: e.tensor_copy(
                                out=btDf[:, half * 4:(half + 1) * 4, :].rearrange("p a b -> p (a b)"), in_=ps[:]), reads=[ps], writes=[btDf])
                        for src, dstD in ((VD, VtD), (KhD, ktD)):
                            ps = self.next_ps()
                            pb = ps.t[:].bitcast(BF16)
                            for fc in range(8):
                                c.op("pe", lambda e, pb=pb, fc=fc, src=src: e.transpose(pb[:, fc * P:(fc + 1) * P], src[:, fc, cc, :],
                                                                                       self.identB[:]), reads=[src, self.identB], writes=[ps])
                            c.op("act", lambda e, pb=pb, dstD=dstD: e.copy(out=dstD[:].rearrange("p a b -> p (a b)"), in_=pb),
                                 reads=[ps], writes=[dstD])
                        for a, bb in H:
                            c.op("dve", lambda e, a=a, bb=bb: e.tensor_copy(out=VtS[a:bb], in_=VtD[a:bb, :, a:bb]), reads=[VtD], writes=[VtS])
                        pR = self.next_ps()
                        mm8(pR, C, lambda fc: [AtDf[:, fc, cc, :], AakD[:, fc, :]], lambda fc: [SF[:, fc, :], VtS[:, fc, :]],
                            [AtDf, AakD, SF, VtS])
                        c.op("act", lambda e, pR=pR: e.copy(out=RHSf[:].rearrange("p a b -> p (a b)"), in_=pR[:]), reads=[pR], writes=[RHSf])
                        pU = self.next_ps()
                        mm8(pU, C, lambda fc: TtDf[:, fc, :], lambda fc: RHSf[:, fc, :], [TtDf, RHSf])
                        c.op("act", lambda e, pU=pU: e.copy(out=USf[:].rearrange("p a b -> p (a b)"), in_=pU[:]), reads=[pU], writes=[USf])
                        c.op("dve", lambda e: e.tensor_copy(out=US[:], in_=USf[:]), reads=[USf], writes=[US])
                        to_diag(UD, US, [US])
                        pO = self.next_ps()
                        mm8(pO, C, lambda fc: [STD[:, fc, :], UD[:, fc, :], VtD[:, fc, :]],
                            lambda fc: [AR[:, fc, cc, C:P], ArbS[:, fc, :], ArkS[:, fc, :]], [STD, UD, VtD, AR, ArbS, ArkS])
                        c.op("dve", lambda e, pO=pO: e.tensor_copy(out=Yt[:, :, cc * C:(cc + 1) * C], in_=pO[:].rearrange("p (f t) -> p f t", f=8)),
                             reads=[pO], writes=[Yt])
                        pS = self.next_ps()
                        mm8(pS, C, lambda fc: [btDf[:, fc, :], ktD[:, fc, :]], lambda fc: [USf[:, fc, :], VtS[:, fc, :]], [btDf, ktD, USf, VtS])
                        c.op("dve", lambda e: e.tensor_tensor(out=stmp[:], in0=SF[:], in1=Wc[:, :, cc:cc + 1].broadcast_to([P, 8, C]), op=ALU.mult),
                             reads=[SF, Wc], writes=[stmp])
                        c.op("dve", lambda e, pS=pS: e.tensor_tensor(out=SF[:].rearrange("p a b -> p (a b)"),
                                                                     in0=stmp[:].rearrange("p a b -> p (a b)"), in1=pS[:], op=ALU.add),
                             reads=[stmp, pS], writes=[SF])
                        c.op("act", lambda e: e.copy(out=SBf[:], in_=SF[:]), reads=[SF], writes=[SBf])
                        to_diag(STD, SBf, [SBf])
                    if d == 0:
                        c.dma("sp", RWv[11][:, :, t0:t0 + P], Yt[:], reads=[Yt])
                    else:
                        c.dma("sp", YFt[:], RWv[11][:, :, t0:t0 + P], writes=[YFt])
                        c.dma("sp", Gt[:], RWv[3][:, :, t0:t0 + P], writes=[Gt])
                        c.dma("sp", BVt[:], RWv[4][:, :, t0:t0 + P], writes=[BVt])
                        c.op("dve", lambda e: e.tensor_tensor(out=Yt[:], in0=Yt[:], in1=YFt[:], op=ALU.add), reads=[Yt, YFt], writes=[Yt])
                        for stage in range(2):
                            src = Yt if stage == 0 else tmp
                            for half in range(2):
                                ps = self.next_ps()
                                for q in range(4):
                                    fc = half * 4 + q
                                    c.op("pe", lambda e, ps=ps, q=q, fc=fc, src=src: e.matmul(
                                        ps[:, q * P:(q + 1) * P], blk64[:], src[:, fc, :], start=True, stop=True), reads=[blk64, src], writes=[ps])
                                fs = slice(half * 4, half * 4 + 4)
                                pv = ps[:].rearrange("p (f t) -> p f t", f=4)
                                if stage == 0:
                                    c.op("dve", lambda e, pv=pv, fs=fs: e.tensor_tensor(out=Yt[:, fs, :], in0=Yt[:, fs, :], in1=pv, op=ALU.subtract),
                                         reads=[Yt, ps], writes=[Yt])
                                else:
                                    c.op("act", lambda e, pv=pv, fs=fs: e.activation(out=cum[:, fs, :], in_=pv, func=AF.Sqrt, bias=epsg[:, 0:1]),
                                         reads=[ps, epsg], writes=[cum])
                            if stage == 0:
                                c.op("act", lambda e: e.activation(out=tmp[:], in_=Yt[:], func=AF.Square), reads=[Yt], writes=[tmp])
                        c.op("dve", lambda e: e.reciprocal(out=cum[:], in_=cum[:]), reads=[cum], writes=[cum])
                        c.op("dve", lambda e: e.tensor_tensor(out=Yt[:], in0=Yt[:], in1=cum[:], op=ALU.mult), reads=[Yt, cum], writes=[Yt])
                        bc = lambda ap: ap.unsqueeze(2).broadcast_to([P, 8, P])
                        c.op("dve", lambda e: e.tensor_tensor(out=Yt[:], in0=Yt[:], in1=bc(vec[:, 3, :]), op=ALU.mult), reads=[Yt, vec], writes=[Yt])
                        c.op("dve", lambda e: e.tensor_tensor(out=Yt[:], in0=Yt[:], in1=bc(vec[:, 4, :]), op=ALU.add), reads=[Yt, vec], writes=[Yt])
                        c.op("dve", lambda e: e.tensor_tensor(out=Yt[:], in0=Yt[:], in1=BVt[:], op=ALU.add), reads=[Yt, BVt], writes=[Yt])
                        c.op("dve", lambda e: e.tensor_tensor(out=yo[:], in0=Yt[:], in1=Gt[:], op=ALU.mult), reads=[Yt, Gt], writes=[yo])
                        c.dma("sp", Mv[:, 0:8, t0:t0 + P], yo[:], reads=[yo])
                if ctx:
                    for half in range(2):
                        ps = self.next_ps()
                        for q in range(4):
                            fc = half * 4 + q
                            c.op("pe", lambda e, ps=ps, q=q, fc=fc: e.transpose(ps[0:C, q * P:(q + 1) * P], SF[:, fc, :], self.identF[:]),
                                 reads=[SF, self.identF], writes=[ps])
                        c.op("dve", lambda e, ps=ps, half=half: e.tensor_copy(
                            out=sio[:, half * 4:(half + 1) * 4, :].rearrange("p a b -> p (a b)"), in_=ps[0:C, :]), reads=[ps], writes=[sio])
                    for hh in range(2):
                        c.dma("sp", self.nsr[si - 1, l2, d].rearrange("(fc hh) v k -> hh v fc k", hh=2)[hh],
                              sio[:, :, hh * C:(hh + 1) * C], reads=[sio])
            c.barrier()

    def ph_ssd(self, j2, d):
        c, nc = self.c, self.nc
        Ls, Lp, NPR = self.Ls, self.Lp, self.NPR
        Zv = self.Z.rearrange("(mc p) t -> p mc t", p=P)
        XCv = self.XC.rearrange("(mc p) t -> p mc t", p=P)
        Mv = self.MIX.rearrange("(kc p) t -> p kc t", p=P)
        with ExitStack() as es:
            sb = lambda n, sh, dt=F32: self.sb(es, "sd_" + n, sh, dt)
            cw, cbias = sb("cw", [P, 5, 48]), sb("cb", [P, 48])
            dtb, Aneg = sb("dtb", [P, 1]), sb("A", [P, 1])
            dfm, ngf = sb("dfm", [P, 32]), sb("ng", [P, 32])
            mL = sb("mL", [P, 2, P], BF16)
            ones = sb("ones", [P, P])
            eps512 = sb("eps", [P, 1])
            xc = sb("xc", [P, 48, P], BF16)
            hF, hB = sb("hF", [P, 4096]), sb("hB", [P, 4096], BF16)
            Y = sb("Y", [P, 4096])
            LT = sb("LT", [P, 64, P], BF16)
            xdt, xdtw = sb("xdt", [P, 4096], BF16), sb("xdtw", [P, 4096], BF16)
            cbT, Btok = sb("cbT", [P, 8, P], BF16), sb("Btok", [P, 8, P], BF16)
            dtr, e1, dtt, dtA = sb("dtr", [P, P]), sb("e1", [P, P]), sb("dt", [P, P]), sb("dtA", [P, P])
            pre, suf, acum = sb("pre", [P, P]), sb("suf", [P, P]), sb("acum", [P, P])
            st4 = sb("st4", [P, 4, P])
            tokm = sb("tokm", [P, 4, P])
            alast, ealast = sb("alast", [P, 1]), sb("ealast", [P, 1])
            dgE, eaB = sb("dgE", [P, P]), sb("eaB", [P, P])
            tmp5 = sb("tmp5", [P, 512])
            big = sb("big", [P, 32, P])
            yg = sb("yg", [P, 32, P])
            if d == 0:
                xb3 = sb("xb3", [P, 16, P + 4])
                tm3 = sb("tm3", [P, 16, P])
            else:
                yo = sb("yo", [P, 32, P], BF16)
                rs = sb("rs", [P, 8, P])
            c.dma("sp", cw[:], self.s_cw[j2], writes=[cw])
            c.dma("sp", cbias[:], self.s_cb[j2], writes=[cbias])
            c.dma("sp", dtb[:], self.s_dtb[j2], writes=[dtb])
            c.dma("sp", Aneg[:], self.s_alog[j2], writes=[Aneg])
            c.dma("sp", dfm[:], self.s_dfm[j2], writes=[dfm])
            c.dma("sp", ngf[:], self.s_ng[j2], writes=[ngf])
            c.dma("pool", mL[:], self.s_maskl.rearrange("a p q -> p a q"), writes=[mL])
            c.op("act", lambda e: e.activation(out=Aneg[:], in_=Aneg[:], func=AF.Exp), reads=[Aneg], writes=[Aneg])
            c.op("dve", lambda e: e.tensor_scalar(out=Aneg[:], in0=Aneg[:], scalar1=-1.0, scalar2=None, op0=ALU.mult),
                 reads=[Aneg], writes=[Aneg])
            c.op("dve", lambda e: e.memset(ones[:], 1.0), writes=[ones])
            c.op("dve", lambda e: e.memset(eps512[:], 1e-6), writes=[eps512])

            def evac(i, fn_act, fn_dve, reads, writes):
                if i % 2:
                    c.op("act", fn_act, reads=reads, writes=writes)
                else:
                    c.op("dve", fn_dve, reads=reads, writes=writes)

            for si in range(1 + NPR):
                ctx = si > 0
                L = Lp if ctx else Ls
                s0 = 0 if not ctx else Ls + (si - 1) * Lp
                nck = L // P
                if ctx:
                    c.op("dve", lambda e: e.memset(hF[:], 0.0), writes=[hF])
                    c.op("dve", lambda e: e.memset(hB[:], 0.0), writes=[hB])
                else:
                    c.dma("sp", big[:], self.state_ssd[j2, d].rearrange("(bk p) n -> p bk n", p=P), writes=[big])
                    for bk8 in range(8):
                        ps = self.next_ps()
                        for q in range(4):
                            bk = bk8 * 4 + q
                            c.op("pe", lambda e, ps=ps, q=q, bk=bk: e.transpose(
                                ps[:, q * P:(q + 1) * P], big[:, bk, :], self.identF[:]),
                                reads=[big, self.identF], writes=[ps])
                        c.op("dve", lambda e, ps=ps, bk8=bk8: e.tensor_copy(
                            out=hF[:, bk8 * 512:(bk8 + 1) * 512], in_=ps[:]), reads=[ps], writes=[hF])
                    c.op("act", lambda e: e.copy(out=hB[:], in_=hF[:]), reads=[hF], writes=[hB])
                order = range(nck) if d == 0 else range(nck - 1, -1, -1)
                for ck in order:
                    t0 = s0 + ck * P
                    if d == 0:
                        lo = 2 if ck == 0 else 0
                        hi = P + 2 if ck == nck - 1 else P + 4
                        for third in range(3):
                            f0 = third * 16
                            if lo or hi < P + 4:
                                c.op("dve", lambda e: e.memset(xb3[:], 0.0), writes=[xb3])
                            c.dma("sp", xb3[:, :, lo:hi], Zv[:, 32 + f0:48 + f0, t0 - 2 + lo:t0 - 2 + hi], writes=[xb3])
                            acc = big[:, f0 % 32:f0 % 32 + 16, :] if False else None
                            c.op("dve", lambda e, f0=f0: e.tensor_tensor(
                                out=yg[:, 0:16, :], in0=xb3[:, :, 0:P],
                                in1=cw[:, 0, f0:f0 + 16].unsqueeze(2).broadcast_to([P, 16, P]), op=ALU.mult),
                                reads=[xb3, cw], writes=[yg])
                            for k in range(1, 5):
                                c.op("dve", lambda e, f0=f0, k=k: e.tensor_tensor(
                                    out=tm3[:], in0=xb3[:, :, k:k + P],
                                    in1=cw[:, k, f0:f0 + 16].unsqueeze(2).broadcast_to([P, 16, P]), op=ALU.mult),
                                    reads=[xb3, cw], writes=[tm3])
                                c.op("dve", lambda e: e.tensor_tensor(out=yg[:, 0:16, :], in0=yg[:, 0:16, :], in1=tm3[:],
                                                                      op=ALU.add), reads=[yg, tm3], writes=[yg])
                            for f in range(16):
                                c.op("act", lambda e, f=f, f0=f0: e.activation(
                                    out=xc[:, f0 + f, :], in_=yg[:, f, :], func=AF.Silu, bias=cbias[:, f0 + f:f0 + f + 1]),
                                    reads=[yg, cbias], writes=[xc])
                        c.dma("sp", XCv[:, :, t0:t0 + P], xc[:], reads=[xc])
                    else:
                        c.dma("sp", xc[:], XCv[:, :, t0:t0 + P], writes=[xc])
                    c.dma("sp", dtr[:], Zv[:, 80, t0:t0 + P], writes=[dtr])
                    c.op("act", lambda e: e.activation(out=e1[:], in_=dtr[:], func=AF.Exp, bias=dtb[:, 0:1]),
                         reads=[dtr, dtb], writes=[e1])
                    c.op("act", lambda e: e.activation(out=st4[:, 0, :], in_=e1[:], func=AF.Ln, bias=ones[:, 0:1]),
                         reads=[e1, ones], writes=[st4])
                    c.op("dve", lambda e: e.tensor_scalar(out=dtA[:], in0=st4[:, 0, :], scalar1=Aneg[:, 0:1], scalar2=None,
                                                          op0=ALU.mult), reads=[st4, Aneg], writes=[dtA])
                    c.op("dve", lambda e: e.tensor_tensor_scan(out=pre[:], data0=ones[:], data1=dtA[:], initial=0.0,
                                                               op0=ALU.mult, op1=ALU.add), reads=[ones, dtA], writes=[pre])
                    c.op("dve", lambda e: e.tensor_tensor_scan(out=suf[:, ::-1], data0=ones[:], data1=dtA[:, ::-1],
                                                               initial=0.0, op0=ALU.mult, op1=ALU.add),
                         reads=[ones, dtA], writes=[suf])
                    c.op("dve", lambda e: e.tensor_copy(out=acum[0:64, :], in_=pre[0:64, :]), reads=[pre], writes=[acum])
                    c.op("dve", lambda e: e.tensor_copy(out=acum[64:128, :], in_=suf[64:128, :]), reads=[suf], writes=[acum])
                    c.op("dve", lambda e: e.tensor_copy(out=alast[0:64, :], in_=pre[0:64, P - 1:P]), reads=[pre], writes=[alast])
                    c.op("dve", lambda e: e.tensor_copy(out=alast[64:128, :], in_=suf[64:128, 0:1]), reads=[suf], writes=[alast])
                    c.op("dve", lambda e: e.tensor_scalar(out=st4[:, 2, :], in0=acum[:], scalar1=-1.0, scalar2=None,
                                                          op0=ALU.mult), reads=[acum], writes=[st4])
                    c.op("act", lambda e: e.activation(out=st4[:, 3, :], in_=acum[:], func=AF.Exp), reads=[acum], writes=[st4])
                    c.op("act", lambda e: e.activation(out=e1[:], in_=acum[:], func=AF.Exp, scale=-1.0, bias=alast[:, 0:1]),
                         reads=[acum, alast], writes=[e1])
                    c.op("dve", lambda e: e.tensor_tensor(out=st4[:, 1, :], in0=st4[:, 0, :], in1=e1[:], op=ALU.mult),
                         reads=[st4, e1], writes=[st4])
                    c.op("act", lambda e: e.activation(out=ealast[:], in_=alast[:], func=AF.Exp), reads=[alast], writes=[ealast])
                    ps = self.next_ps()
                    for q in range(4):
                        c.op("pe", lambda e, ps=ps, q=q: e.transpose(ps[:, q * P:(q + 1) * P], st4[:, q, :], self.identF[:]),
                             reads=[st4, self.identF], writes=[ps])
                    c.op("dve", lambda e, ps=ps: e.tensor_copy(out=tokm[:].rearrange("p a b -> p (a b)"), in_=ps[:]),
                         reads=[ps], writes=[tokm])
                    c.op("dve", lambda e: e.tensor_scalar(out=dgE[:], in0=self.identF[:], scalar1=ealast[:, 0:1], scalar2=None,
                                                          op0=ALU.mult), reads=[self.identF, ealast], writes=[dgE])
                    ps = self.next_ps()
                    c.op("pe", lambda e, ps=ps: e.matmul(ps[:, 0:P], ones[:], dgE[:], start=True, stop=True),
                         reads=[ones, dgE], writes=[ps])
                    c.op("act", lambda e, ps=ps: e.copy(out=eaB[:], in_=ps[:, 0:P]), reads=[ps], writes=[eaB])
                    for half in range(2):
                        ps = self.next_ps()
                        for q in range(4):
                            g = half * 4 + q
                            c.op("pe", lambda e, ps=ps, q=q, g=g: e.matmul(
                                ps[:, q * P:(q + 1) * P], xc[:, 32 + g, :], xc[:, 40 + g, :], start=True, stop=True),
                                reads=[xc], writes=[ps])
                        evac(half, lambda e, ps=ps, half=half: e.copy(
                            out=cbT[:, half * 4:(half + 1) * 4, :].rearrange("p a b -> p (a b)"), in_=ps[:]),
                            lambda e, ps=ps, half=half: e.tensor_copy(
                            out=cbT[:, half * 4:(half + 1) * 4, :].rearrange("p a b -> p (a b)"), in_=ps[:]),
                            [ps], [cbT])
                    ps = self.next_ps()
                    pb = ps.t[:].bitcast(BF16)
                    for g in range(8):
                        c.op("pe", lambda e, pb=pb, g=g: e.transpose(pb[:, g * P:(g + 1) * P], xc[:, 32 + g, :], self.identB[:]),
                             reads=[xc, self.identB], writes=[ps])
                    c.op("act", lambda e, pb=pb: e.copy(out=Btok[:].rearrange("p a b -> p (a b)"), in_=pb),
                         reads=[ps], writes=[Btok])
                    for hb in range(16):
                        ps = self.next_ps()
                        for q in range(4):
                            h = hb * 4 + q
                            col = d * 64 + h
                            c.op("pe", lambda e, ps=ps, q=q, col=col: e.matmul(
                                ps[:, q * P:(q + 1) * P], self.identF[:, col:col + 1].broadcast_to([P, P]), acum[:],
                                start=True, stop=False), reads=[self.identF, acum], writes=[ps])
                            c.op("pe", lambda e, ps=ps, q=q: e.matmul(
                                ps[:, q * P:(q + 1) * P], self.identB[:], mL[:, d, :], start=False, stop=True),
                                reads=[self.identB, mL], writes=[ps])
                        for q in range(4):
                            h = hb * 4 + q
                            col = d * 64 + h
                            c.op("act", lambda e, ps=ps, q=q, h=h, col=col: e.activation(
                                out=LT[:, h, :], in_=ps[:, q * P:(q + 1) * P], func=AF.Exp, bias=tokm[:, 2, col:col + 1]),
                                reads=[ps, tokm], writes=[LT])
                    for g in range(8):
                        c.op("dve", lambda e, g=g: e.tensor_tensor(
                            out=LT[:, g * 8:(g + 1) * 8, :], in0=LT[:, g * 8:(g + 1) * 8, :],
                            in1=cbT[:, g:g + 1, :].broadcast_to([P, 8, P]), op=ALU.mult), reads=[LT, cbT], writes=[LT])
                    for bk in range(4):
                        ps = self.next_ps()
                        pb = ps.t[:].bitcast(BF16)
                        for q in range(8):
                            fc = bk * 8 + q
                            c.op("pe", lambda e, pb=pb, q=q, fc=fc: e.transpose(
                                pb[:, q * P:(q + 1) * P], xc[:, fc, :], self.identB[:]), reads=[xc, self.identB], writes=[ps])
                        hs = d * 64 + bk * 16
                        c.op("dve", lambda e, pb=pb, bk=bk, hs=hs: e.tensor_tensor(
                            out=xdt[:, bk * 1024:(bk + 1) * 1024].rearrange("p (h q) -> p h q", h=16),
                            in0=pb.rearrange("p (h q) -> p h q", h=16),
                            in1=tokm[:, 0, hs:hs + 16].unsqueeze(2).broadcast_to([P, 16, 64]), op=ALU.mult),
                            reads=[ps, tokm], writes=[xdt])
                        c.op("dve", lambda e, pb=pb, bk=bk, hs=hs: e.tensor_tensor(
                            out=xdtw[:, bk * 1024:(bk + 1) * 1024].rearrange("p (h q) -> p h q", h=16),
                            in0=pb.rearrange("p (h q) -> p h q", h=16),
                            in1=tokm[:, 1, hs:hs + 16].unsqueeze(2).broadcast_to([P, 16, 64]), op=ALU.mult),
                            reads=[ps, tokm], writes=[xdtw])
                    for g in range(8):
                        py, pst, ph = self.next_ps(), self.next_ps(), self.next_ps()
                        for q in range(8):
                            h = g * 8 + q
                            c.op("pe", lambda e, py=py, q=q, h=h: e.matmul(
                                py[:, q * 64:(q + 1) * 64], LT[:, h, :], xdt[:, h * 64:(h + 1) * 64], start=True, stop=True),
                                reads=[LT, xdt], writes=[py])
                        c.op("pe", lambda e, pst=pst, g=g: e.matmul(
                            pst[:], xc[:, 40 + g, :], hB[:, g * 512:(g + 1) * 512], start=True, stop=True),
                            reads=[xc, hB], writes=[pst])
                        c.op("pe", lambda e, ph=ph, g=g: e.matmul(
                            ph[:], Btok[:, g, :], xdtw[:, g * 512:(g + 1) * 512], start=True, stop=True),
                            reads=[Btok, xdtw], writes=[ph])
                        hs = d * 64 + g * 8
                        c.op("dve", lambda e, pst=pst, hs=hs: e.tensor_tensor(
                            out=tmp5[:].rearrange("p (h q) -> p h q", h=8), in0=pst[:].rearrange("p (h q) -> p h q", h=8),
                            in1=tokm[:, 3, hs:hs + 8].unsqueeze(2).broadcast_to([P, 8, 64]), op=ALU.mult),
                            reads=[pst, tokm], writes=[tmp5])
                        c.op("dve", lambda e, py=py, g=g: e.tensor_tensor(
                            out=Y[:, g * 512:(g + 1) * 512], in0=py[:], in1=tmp5[:], op=ALU.add),
                            reads=[py, tmp5], writes=[Y])
                        c.op("dve", lambda e, g=g, hs=hs: e.tensor_tensor(
                            out=tmp5[:].rearrange("p (h q) -> p h q", h=8),
                            in0=hF[:, g * 512:(g + 1) * 512].rearrange("p (h q) -> p h q", h=8),
                            in1=eaB[:, hs:hs + 8].unsqueeze(2).broadcast_to([P, 8, 64]), op=ALU.mult),
                            reads=[hF, eaB], writes=[tmp5])
                        c.op("dve", lambda e, ph=ph, g=g: e.tensor_tensor(
                            out=hF[:, g * 512:(g + 1) * 512], in0=ph[:], in1=tmp5[:], op=ALU.add),
                            reads=[ph, tmp5], writes=[hF])
                        c.op("act", lambda e, g=g: e.copy(out=hB[:, g * 512:(g + 1) * 512], in_=hF[:, g * 512:(g + 1) * 512]),
                             reads=[hF], writes=[hB])
                    if d == 0:
                        c.dma("sp", self.YF[t0:t0 + P, :], Y[:], reads=[Y])
                    else:
                        ygf = yg[:].rearrange("p a b -> p (a b)")
                        c.dma("sp", ygf, self.YF[t0:t0 + P, :], writes=[yg])
                        c.op("dve", lambda e: e.tensor_tensor(out=Y[:], in0=Y[:], in1=ygf, op=ALU.add), reads=[Y, yg], writes=[Y])
                        c.dma("sp", big[:], Zv[:, 0:32, t0:t0 + P], writes=[big])
                        c.op("act", lambda e: e.activation(out=big[:], in_=big[:], func=AF.Silu), reads=[big], writes=[big])
                        for bk in range(8):
                            ps = self.next_ps()
                            for q in range(4):
                                fc = bk * 4 + q
                                c.op("pe", lambda e, ps=ps, q=q, fc=fc: e.transpose(
                                    ps[:, q * P:(q + 1) * P], Y[:, fc * P:(fc + 1) * P], self.identF[:]),
                                    reads=[Y, self.identF], writes=[ps])
                            for q in range(4):
                                fc = bk * 4 + q
                                c.op("dve", lambda e, ps=ps, q=q, fc=fc: e.scalar_tensor_tensor(
                                    out=yg[:, fc, :], in0=xc[:, fc, :], scalar=dfm[:, fc:fc + 1], in1=ps[:, q * P:(q + 1) * P],
                                    op0=ALU.mult, op1=ALU.add), reads=[xc, dfm, ps], writes=[yg])
                        c.op("dve", lambda e: e.tensor_tensor(out=yg[:], in0=yg[:], in1=big[:], op=ALU.mult),
                             reads=[yg, big], writes=[yg])
                        c.op("act", lambda e: e.activation(out=big[:], in_=yg[:], func=AF.Square), reads=[yg], writes=[big])
                        for half in range(2):
                            ps = self.next_ps()
                            for gq in range(4):
                                g = half * 4 + gq
                                for q in range(4):
                                    c.op("pe", lambda e, ps=ps, gq=gq, g=g, q=q: e.matmul(
                                        ps[:, gq * P:(gq + 1) * P], ones[:], big[:, g * 4 + q, :], start=(q == 0), stop=(q == 3)),
                                        reads=[ones, big], writes=[ps])
                            c.op("act", lambda e, ps=ps, half=half: e.activation(
                                out=rs[:, half * 4:(half + 1) * 4, :].rearrange("p a b -> p (a b)"), in_=ps[:], func=AF.Sqrt,
                                scale=1.0 / 512, bias=eps512[:, 0:1]), reads=[ps, eps512], writes=[rs])
                        c.op("dve", lambda e: e.reciprocal(out=rs[:], in_=rs[:]), reads=[rs], writes=[rs])
                        c.op("dve", lambda e: e.tensor_tensor(
                            out=yg[:].rearrange("p (g q) t -> p g q t", g=8), in0=yg[:].rearrange("p (g q) t -> p g q t", g=8),
                            in1=rs[:].unsqueeze(2).broadcast_to([P, 8, 4, P]), op=ALU.mult), reads=[yg, rs], writes=[yg])
                        c.op("dve", lambda e: e.tensor_tensor(
                            out=yo[:], in0=yg[:], in1=ngf[:].unsqueeze(2).broadcast_to([P, 32, P]), op=ALU.mult),
                            reads=[yg, ngf], writes=[yo])
                        c.dma("sp", Mv[:, 0:32, t0:t0 + P], yo[:], reads=[yo])
                if ctx:
                    for bk8 in range(8):
                        ps = self.next_ps()
                        for q in range(4):
                            bk = bk8 * 4 + q
                            c.op("pe", lambda e, ps=ps, q=q, bk=bk: e.transpose(
                                ps[:, q * P:(q + 1) * P], hF[:, bk * P:(bk + 1) * P], self.identF[:]),
                                reads=[hF, self.identF], writes=[ps])
                        c.op("dve", lambda e, ps=ps, bk8=bk8: e.tensor_copy(
                            out=big[:, bk8 * 4:(bk8 + 1) * 4, :].rearrange("p a b -> p (a b)"), in_=ps[:]),
                            reads=[ps], writes=[big])
                    c.dma("sp", self.nss[si - 1, j2, d].rearrange("(bk p) n -> p bk n", p=P), big[:], reads=[big])
            c.barrier()

    def ph_mix_stub(self, KC):
        c = self.c
        with ExitStack() as es:
            z = self.sb(es, "stub_z", [P, self.TT], BF16)
            c.op("dve", lambda e: e.memset(z[:], 0.0), writes=[z])
            Mv = self.MIX.rearrange("(kc p) t -> p kc t", p=P)
            for kc in range(KC):
                for ti in range(self.Ttot // self.TT):
                    c.dma("sp", Mv[:, kc, ti * self.TT:(ti + 1) * self.TT], z[:], reads=[z])
            c.barrier()

    def build(self, stub=True):
        self.declare()
        self.ph_load()
        self.ph_mod()
        X, Xn = self.XA, self.XB
        for l in range(self.depth):
            if l % 2 == 0:
                KC, Wout = 16, self.ab_w_out[l // 2]
                self.ph_mix_stub(KC)
                if ENABLE_ATTN:
                    self.ph_in(l, X, self.ab_w_in[l // 2], 54, self.Z,
                               lambda mc, cond: cond == 1 or not (36 <= mc < 44 or 48 <= mc < 52))
                    self.ph_attn(l // 2)
                    if ENABLE_RWKV:
                        self.ph_rw0(l // 2)
                        self.ph_rw1(l // 2, 0)
                        self.ph_rw1(l // 2, 1)
            else:
                KC, Wout = 32, self.ssd_w_out[l // 2]
                if ENABLE_SSD:
                    self.ph_in(l, X, self.ssd_w_in[l // 2], 81, self.Z)
                    self.ph_ssd(l // 2, 0)
                    self.ph_ssd(l // 2, 1)
                else:
                    self.ph_mix_stub(KC)
            self.ph_out(l, X, Xn, Wout, KC)
            X, Xn = Xn, X
            self.ph_ffn(l, X, Xn)
            X, Xn = Xn, X
        self.ph_final(X)
        return self.nc


_CACHE = {}


def _perm64():
    d = np.arange(64)
    return np.where((d % 32) < 16, d + 16, d - 16)


def ab_colidx():
    Z = 5024
    idx = list(range(0, 3488)) + [Z] * 96
    q0, k0, v0 = 3488, 4512, 4768
    pm = _perm64()
    idx += list(range(q0, q0 + 1024))
    idx += [q0 + h * 64 + int(pm[d]) for h in range(16) for d in range(64)]
    for g in range(4):
        idx += [k0 + g * 64 + d for d in range(64)] * 2
    for g in range(4):
        idx += [k0 + g * 64 + int(pm[d]) for d in range(64)] * 2
    idx += list(range(v0, v0 + 256))
    assert len(idx) == 54 * P
    return np.asarray(idx)


def rope_tables(L, grid_w=64, theta=10000.0):
    rows = (np.arange(L) // grid_w).astype(np.float32)
    cols = (np.arange(L) % grid_w).astype(np.float32)
    inv = (theta ** (-np.arange(0, 32, 2, dtype=np.float32) / 32)).astype(np.float32)
    cs = np.zeros((64, L), np.float32)
    sn = np.zeros((64, L), np.float32)
    for d in range(64):
        pos = rows if d < 32 else cols
        ang = pos * inv[d % 16]
        cs[d] = np.cos(ang)
        sn[d] = np.sin(ang) * (-1.0 if (d % 32) < 16 else 1.0)
    return np.ascontiguousarray(np.concatenate([cs, cs], 0)), np.ascontiguousarray(np.concatenate([sn, sn], 0))


def kernel(**inp):
    x_prompt = np.asarray(inp["x_prompt"], np.float32)
    x_sample = np.asarray(inp["x_sample"], np.float32)
    B, Lp, D = x_prompt.shape
    DB, Ls, _ = x_sample.shape
    depth = inp["w_mod"].shape[0]
    NPR = B // NCORES
    past = inp["cache_k"].shape[2]
    assert DB == NCORES
    key = (Ls, Lp, NPR, depth, past)
    bld = Builder(*key)
    nc = bld.build()
    Ttot = bld.Ttot
    shared = {
        "w_mod": np.asarray(inp["w_mod"], np.float32),
        "bmod": fm(inp["b_mod"], 96),
        "n1g": fm(inp["norm1_g"], 16),
        "n2g": fm(inp["norm2_g"], 16),
        "fng": fm(inp["final_norm_g"], 16),
        "ffn_w_in": np.asarray(inp["ffn_w_in"], np.float32),
        "ffn_w_out": np.asarray(inp["ffn_w_out"], np.float32),
        "ident": np.eye(P, dtype=np.float32),
        "ab_w_out": np.asarray(inp["ab_w_out"], np.float32),
        "ssd_w_out": np.asarray(inp["ssd_w_out"], np.float32),
        "ssd_w_in": np.asarray(inp["ssd_w_in"], np.float32),
    }
    ci = ab_colidx()
    abw = np.asarray(inp["ab_w_in"], np.float32)
    abw = np.concatenate([abw, np.zeros(abw.shape[:2] + (1,), np.float32)], -1)
    shared["ab_w_in"] = np.ascontiguousarray(abw[:, :, ci])
    rc, rs = rope_tables(Ls)
    shared["ropec"], shared["ropes"] = rc, rs
    kj = np.arange(P)[:, None]
    qi = np.arange(P)[None, :]
    shared["maskb"] = np.stack([np.where(kj >= qi, 0.0, -30000.0), np.where(kj <= qi, 0.0, -30000.0)]).astype(np.float32)
    sink = np.asarray(inp["attn_sink"], np.float32)
    hh = 2 * np.arange(8)[None, :] + (np.arange(P) // 64)[:, None]
    shared["sinkfm"] = np.ascontiguousarray(sink[:, hh])
    f32 = lambda k: np.asarray(inp[k], np.float32)
    na = bld.n_ab
    shared["r_mu"] = np.ascontiguousarray(np.stack([fm(f32("rwkv_mu_prev"), 28), fm(f32("rwkv_mu_next"), 28)], 2))
    w0 = fm(f32("rwkv_w0"), 8)
    a0 = fm(f32("rwkv_a0"), 8)
    shared["r_w0a0"] = np.ascontiguousarray(np.stack([w0, a0], 1).transpose(0, 3, 1, 2, 4))
    shared["r_w2"] = np.ascontiguousarray(f32("rwkv_w2").reshape(na, P, 1024))
    shared["r_a2"] = np.ascontiguousarray(f32("rwkv_a2").reshape(na, P, 1024))
    g2 = np.concatenate([f32("rwkv_g2"), np.zeros((na, 96, 1024), np.float32)], 1)
    shared["r_g2"] = np.ascontiguousarray(g2.reshape(na, 2, P, 1024).transpose(0, 2, 1, 3))
    shared["r_vec"] = np.ascontiguousarray(np.stack([fm(f32("rwkv_k_k"), 8), fm(f32("rwkv_k_a"), 8),
                                                     fm(f32("rwkv_r_k").reshape(na, 1024), 8), fm(f32("rwkv_lnx_g"), 8),
                                                     fm(f32("rwkv_lnx_b"), 8)], 2))
    tt = (np.arange(P) % 64)[:, None]
    ss = np.arange(64)[None, :]
    shared["r_maskn"] = np.stack([(ss < tt), (ss > tt)]).astype(np.float32)
    sm0 = np.concatenate([(tt < ss), (tt <= ss)], 1)
    sm1 = np.concatenate([(tt > ss), (tt >= ss)], 1)
    shared["r_masks"] = np.stack([sm0, sm1]).astype(np.float32)
    shared["r_is"] = (tt == ss).astype(np.float32)
    shared["r_blk"] = np.kron(np.eye(2, dtype=np.float32), np.ones((64, 64), np.float32))
    ns = max(bld.n_ssd, 1)
    cwf = fm(inp["ssd_conv_w"], 48)
    shared["s_cw"] = np.ascontiguousarray(cwf.transpose(0, 2, 1, 3))
    shared["s_cb"] = fm(inp["ssd_conv_b"], 48)
    shared["s_dtb"] = np.ascontiguousarray(np.asarray(inp["ssd_dt_bias"], np.float32).reshape(ns, P, 1))
    shared["s_alog"] = np.ascontiguousarray(np.asarray(inp["ssd_a_log"], np.float32).reshape(ns, P, 1))
    shared["s_dfm"] = fm(np.repeat(np.asarray(inp["ssd_d"], np.float32), 64, axis=-1), 32)
    shared["s_ng"] = fm(inp["ssd_norm_g"], 32)
    jj = np.arange(P)[:, None]
    ii = np.arange(P)[None, :]
    shared["s_maskl"] = np.stack([np.where(jj <= ii, 0.0, -30000.0), np.where(jj >= ii, 0.0, -30000.0)]).astype(np.float32)
    in_maps = []
    for i in range(NCORES):
        m = dict(shared)
        m["xin"] = np.ascontiguousarray(np.concatenate(
            [x_sample[i]] + [x_prompt[NPR * i + j] for j in range(NPR)], 0))
        cc = np.stack([np.asarray(inp["c_ctx"], np.float32), np.asarray(inp["c"], np.float32)[i]], -1)
        m["cond2"] = np.ascontiguousarray(cc.reshape(16, P, 2).transpose(1, 0, 2))
        m["state_rwkv"] = np.ascontiguousarray(f32("state_rwkv")[i])
        m["state_ssd"] = np.ascontiguousarray(np.asarray(inp["state_ssd"], np.float32)[i].reshape(ns, 2, 4096, P))
        m["cache_k"] = np.ascontiguousarray(np.asarray(inp["cache_k"], np.float32)[i].reshape(-1, past, 256))
        m["cache_v"] = np.ascontiguousarray(np.asarray(inp["cache_v"], np.float32)[i].reshape(-1, past, 256))
        in_maps.append({k: m[k] for k in bld.ins})
    res = run_bass_kernel_spmd(nc, in_maps, core_ids=list(range(NCORES)))
    y_sample = np.stack([res.results[i]["y_out"][:Ls] for i in range(NCORES)], 0)
    y_prompt = np.stack([res.results[i]["y_out"][Ls + j * Lp: Ls + (j + 1) * Lp]
                         for i in range(NCORES) for j in range(NPR)], 0)
    n_ab = bld.n_ab
    nck = np.concatenate([res.results[i]["nck"] for i in range(NCORES)], 0).reshape(B, n_ab, Lp, 4, 64)
    ncv = np.concatenate([res.results[i]["ncv"] for i in range(NCORES)], 0).reshape(B, n_ab, Lp, 4, 64)
    nsr = np.concatenate([res.results[i]["nsr"] for i in range(NCORES)], 0)
    nss = np.concatenate([res.results[i]["nss"] for i in range(NCORES)], 0).reshape(B, bld.n_ssd, 2, 64, 64, 128)
    return y_prompt, y_sample, nsr, nck, ncv, nss
```

```python
import math
from contextlib import ExitStack
import numpy as np
import concourse.bass as bass
import concourse.mybir as mybir
from concourse.bass_utils import run_bass_kernel_spmd

F32 = mybir.dt.float32
BF16 = mybir.dt.bfloat16
AF = mybir.ActivationFunctionType
ALU = mybir.AluOpType
AX = mybir.AxisListType
P = 128
NCORES = 8
import os
ENABLE_ATTN = os.environ.get('ENABLE_ATTN', '1') == '1'
ENABLE_SSD = os.environ.get('ENABLE_SSD', '1') == '1'
ENABLE_RWKV = os.environ.get('ENABLE_RWKV', '1') == '1'


class T:
    __slots__ = ("t", "w", "r", "excl")

    def __init__(self, t, excl=False):
        self.t = t
        self.w = None
        self.r = []
        self.excl = excl

    def __getitem__(self, idx):
        return self.t[idx]


class Ctx:
    def __init__(self, nc, ndma=10):
        self.nc = nc
        self.eng = {"pe": nc.tensor, "act": nc.scalar, "dve": nc.vector, "pool": nc.gpsimd, "sp": nc.sync}
        self.sem = {k: nc.alloc_semaphore("sem_" + k) for k in self.eng}
        self.cnt = {k: 0 for k in self.eng}
        self.seen = {k: {} for k in self.eng}
        self.dsem = {q: [nc.alloc_semaphore(f"dma_{q}_{i}") for i in range(ndma)] for q in ("sp", "pool")}
        self.dcnt = {q: [0] * ndma for q in self.dsem}
        self.drr = {q: 0 for q in self.dsem}
        self.outstanding = {}

    def _wait(self, e, tok):
        sem, val, owner = tok
        key = id(sem)
        if self.seen[e].get(key, 0) >= val:
            return
        self.seen[e][key] = val
        self.eng[e].wait_ge(sem, val)

    def _deps(self, e, reads, writes):
        for t in reads:
            if t.w is not None and not (e == "pe" and t.w[2] == "pe"):
                self._wait(e, t.w)
        for t in writes:
            if t.w is not None and not (e == "pe" and t.w[2] == "pe"):
                self._wait(e, t.w)
            for tok in t.r:
                if not (e == "pe" and tok[2] == "pe"):
                    self._wait(e, tok)

    def _post(self, tok, reads, writes):
        for t in reads:
            t.r = [x for x in t.r if x[0] is not tok[0]] + [tok]
        for t in writes:
            t.w = tok
            t.r = []

    def op(self, e, fn, reads=(), writes=()):
        ex = [t for t in reads if t.excl]
        if ex and e != "pe":
            reads = [t for t in reads if not t.excl]
            writes = list(writes) + ex
        self._deps(e, reads, writes)
        inst = fn(self.eng[e])
        self.cnt[e] += 1
        inst.then_inc(self.sem[e], 1)
        tok = (self.sem[e], self.cnt[e], e)
        self._post(tok, reads, writes)
        return tok

    def dma(self, q, out, in_, reads=(), writes=(), **kw):
        if q == "sp" and not writes and getattr(self, "st_pool", False):
            q = "pool"
        self._deps(q, reads, writes)
        i = self.drr[q]
        self.drr[q] = (i + 1) % len(self.dsem[q])
        sem = self.dsem[q][i]
        if self.dcnt[q][i] > 0:
            self._wait(q, (sem, self.dcnt[q][i], "dma"))
        inst = self.eng[q].dma_start(out=out, in_=in_, **kw)
        self.dcnt[q][i] += 16
        inst.then_inc(sem, 16)
        tok = (sem, self.dcnt[q][i], "dma")
        self._post(tok, reads, writes)
        self.outstanding[id(sem)] = tok
        return tok

    def barrier(self):
        toks = [(self.sem[k], self.cnt[k], k) for k in self.eng if self.cnt[k] > 0]
        toks += list(self.outstanding.values())
        self.outstanding = {}
        for e in self.eng:
            for tok in toks:
                if tok[2] != e:
                    self._wait(e, tok)


def fm(a, nch=None):
    a = np.asarray(a, np.float32)
    n = a.shape[-1]
    if nch is None:
        nch = -(-n // P)
    pad = nch * P - n
    if pad:
        a = np.concatenate([a, np.zeros(a.shape[:-1] + (pad,), np.float32)], -1)
    a = a.reshape(a.shape[:-1] + (nch, P))
    return np.ascontiguousarray(np.swapaxes(a, -1, -2))


class Builder:
    def __init__(self, Ls, Lp, NPR, depth, past):
        self.Ls, self.Lp, self.NPR, self.depth, self.past = Ls, Lp, NPR, depth, past
        self.D = 2048
        self.TT = 512
        self.Ttot = Ls + NPR * Lp
        assert self.Ttot % self.TT == 0 and Ls % self.TT == 0
        self.nc = bass.Bass("TRN2", target_bir_lowering=False)
        self.c = Ctx(self.nc)
        self.ins = {}
        self.outs = {}
        self.n_ab = (depth + 1) // 2
        self.n_ssd = depth // 2

    def din(self, name, shape, dt=F32):
        ap = self.nc.dram_tensor(name, list(shape), dt, kind="ExternalInput").ap()
        self.ins[name] = ap
        return ap

    def dout(self, name, shape, dt=F32):
        ap = self.nc.dram_tensor(name, list(shape), dt, kind="ExternalOutput").ap()
        self.outs[name] = ap
        return ap

    def dscr(self, name, shape, dt=F32):
        return self.nc.dram_tensor(name, list(shape), dt, kind="Internal").ap()

    def sb(self, es, name, shape, dt=F32):
        self._uid = getattr(self, "_uid", 0) + 1
        return T(es.enter_context(self.nc.sbuf_tensor(f"{name}_{self._uid}", list(shape), dt)))

    def tile_cond(self, t0):
        return 1 if t0 < self.Ls else 0

    def declare(self):
        D, Ttot, depth = self.D, self.Ttot, self.depth
        self.xin = self.din("xin", [Ttot, D])
        self.cond2 = self.din("cond2", [P, 16, 2])
        self.w_mod = self.din("w_mod", [depth, D, 6 * D])
        self.bmod = self.din("bmod", [depth, P, 96])
        self.n1g = self.din("n1g", [depth, P, 16])
        self.n2g = self.din("n2g", [depth, P, 16])
        self.fng = self.din("fng", [P, 16])
        self.ffn_w_in = self.din("ffn_w_in", [depth, D, 11264])
        self.ffn_w_out = self.din("ffn_w_out", [depth, 5632, D])
        self.ident = self.din("ident", [P, P])
        self.ab_w_out = self.din("ab_w_out", [self.n_ab, 2048, D])
        self.ssd_w_out = self.din("ssd_w_out", [max(self.n_ssd, 1), 4096, D])
        self.y_out = self.dout("y_out", [Ttot, D])
        n_ab, NPR, Lp, past = self.n_ab, self.NPR, self.Lp, self.past
        self.ab_w_in = self.din("ab_w_in", [n_ab, D, 54 * P])
        self.ssd_w_in = self.din("ssd_w_in", [max(self.n_ssd, 1), D, 81 * P])
        self.ropec = self.din("ropec", [P, self.Ls])
        self.ropes = self.din("ropes", [P, self.Ls])
        self.maskb = self.din("maskb", [2, P, P])
        self.sinkfm = self.din("sinkfm", [n_ab, P, 8])
        self.cache_k = self.din("cache_k", [n_ab, past, 256])
        self.cache_v = self.din("cache_v", [n_ab, past, 256])
        self.nck = self.dout("nck", [NPR, n_ab, Lp, 256])
        self.ncv = self.dout("ncv", [NPR, n_ab, Lp, 256])
        self.Z = self.dscr("Z", [81 * P, Ttot])
        self.r_mu = self.din("r_mu", [n_ab, P, 2, 28])
        self.r_w0a0 = self.din("r_w0a0", [n_ab, P, 2, 2, 8])
        self.r_w2 = self.din("r_w2", [n_ab, P, 1024])
        self.r_a2 = self.din("r_a2", [n_ab, P, 1024])
        self.r_g2 = self.din("r_g2", [n_ab, P, 2, 1024])
        self.r_vec = self.din("r_vec", [n_ab, P, 5, 8])
        self.r_maskn = self.din("r_maskn", [2, P, 64])
        self.r_masks = self.din("r_masks", [2, P, P])
        self.r_is = self.din("r_is", [P, 64])
        self.r_blk = self.din("r_blk", [P, P])
        self.state_rwkv = self.din("state_rwkv", [n_ab, 2, 16, 64, 64])
        self.nsr = self.dout("nsr", [NPR, n_ab, 2, 16, 64, 64])
        self.RW = [self.dscr(f"RW{i}", [1024, Ttot]) for i in range(12)]
        ns = max(self.n_ssd, 1)
        self.s_cw = self.din("s_cw", [ns, P, 5, 48])
        self.s_cb = self.din("s_cb", [ns, P, 48])
        self.s_dtb = self.din("s_dtb", [ns, P, 1])
        self.s_alog = self.din("s_alog", [ns, P, 1])
        self.s_dfm = self.din("s_dfm", [ns, P, 32])
        self.s_ng = self.din("s_ng", [ns, P, 32])
        self.s_maskl = self.din("s_maskl", [2, P, P])
        self.state_ssd = self.din("state_ssd", [ns, 2, 4096, P])
        self.nss = self.dout("nss", [NPR, ns, 2, 4096, P])
        self.XC = self.dscr("XC", [48 * P, Ttot], BF16)
        self.YF = self.dscr("YF", [Ttot, 4096])
        self.XA = self.dscr("XA", [D, Ttot])
        self.XB = self.dscr("XB", [D, Ttot])
        self.MIX = self.dscr("MIX", [4096, Ttot], BF16)
        c, nc = self.c, self.nc
        self.ps = [T(nc.alloc_psum_tensor(f"ps{i}", [P, 512], F32), excl=True) for i in range(8)]
        self.psi = 0
        self.identF = T(nc.alloc_sbuf_tensor("identF", [P, P], F32))
        self.identB = T(nc.alloc_sbuf_tensor("identB", [P, P], BF16))
        self.onesF = T(nc.alloc_sbuf_tensor("onesF", [P, P], F32))
        self.mod = T(nc.alloc_sbuf_tensor("mod", [P, depth, 96, 2], F32))
        self.gs1 = T(nc.alloc_sbuf_tensor("gs1", [P, depth, 16, 2], F32))
        self.gs2 = T(nc.alloc_sbuf_tensor("gs2", [P, depth, 16, 2], F32))
        self.fngt = T(nc.alloc_sbuf_tensor("fngt", [P, 16], F32))
        self.eps_t = T(nc.alloc_sbuf_tensor("eps_t", [P, 1], F32))
        c.dma("sp", self.identF[:], self.ident, writes=[self.identF])
        c.dma("pool", self.identB[:], self.ident, writes=[self.identB])
        c.dma("sp", self.fngt[:], self.fng, writes=[self.fngt])
        c.op("dve", lambda e: e.memset(self.onesF[:], 1.0), writes=[self.onesF])
        c.op("dve", lambda e: e.memset(self.eps_t[:], 1e-6), writes=[self.eps_t])

    def next_ps(self):
        p = self.ps[self.psi]
        self.psi = (self.psi + 1) % 8
        return p

    def ph_load(self):
        c, nc = self.c, self.nc
        with ExitStack() as es:
            xin_t = [self.sb(es, f"ld_x{i}", [P, 2048]) for i in range(2)]
            xo_t = [self.sb(es, f"ld_o{i}", [P, 16, P]) for i in range(2)]
            XAv = self.XA.rearrange("(fc p) t -> p fc t", p=P)
            for b in range(self.Ttot // P):
                xi, xo = xin_t[b % 2], xo_t[b % 2]
                c.dma("sp", xi[:], self.xin[b * P:(b + 1) * P, :], writes=[xi])
                for g in range(4):
                    ps = self.next_ps()
                    for j in range(4):
                        fc = g * 4 + j
                        c.op("pe", lambda e, fc=fc, j=j, ps=ps, xi=xi: e.transpose(
                            ps[:, j * P:(j + 1) * P], xi[:, fc * P:(fc + 1) * P], self.identF[:]),
                            reads=[xi, self.identF], writes=[ps])
                    eng = "act" if g % 2 else "dve"
                    if eng == "act":
                        c.op("act", lambda e, g=g, ps=ps, xo=xo: e.copy(
                            out=xo[:, g * 4:(g + 1) * 4, :], in_=ps[:].rearrange("p (a b) -> p a b", a=4)),
                            reads=[ps], writes=[xo])
                    else:
                        c.op("dve", lambda e, g=g, ps=ps, xo=xo: e.tensor_copy(
                            out=xo[:, g * 4:(g + 1) * 4, :], in_=ps[:].rearrange("p (a b) -> p a b", a=4)),
                            reads=[ps], writes=[xo])
                c.dma("sp", XAv[:, :, b * P:(b + 1) * P], xo[:], reads=[xo])
            c.barrier()

    def ph_mod(self):
        c, nc = self.c, self.nc
        depth = self.depth
        with ExitStack() as es:
            cd = self.sb(es, "md_c", [P, 16, 2])
            sc = self.sb(es, "md_s", [P, 16, 2])
            bm = self.sb(es, "md_b", [P, depth, 96])
            ng = self.sb(es, "md_g", [P, 2, depth, 16])
            wb = [self.sb(es, f"md_w{i}", [P, 16, 512]) for i in range(2)]
            c.dma("sp", cd[:], self.cond2, writes=[cd])
            c.dma("sp", bm[:], self.bmod.rearrange("l p m -> p l m"), writes=[bm])
            c.dma("sp", ng[:, 0], self.n1g.rearrange("l p m -> p l m"), writes=[ng])
            c.dma("sp", ng[:, 1], self.n2g.rearrange("l p m -> p l m"), writes=[ng])
            c.op("act", lambda e: e.activation(out=sc[:], in_=cd[:], func=AF.Silu), reads=[cd], writes=[sc])
            it = 0
            for l in range(depth):
                Wv = self.w_mod[l].rearrange("(kc p) n -> p kc n", p=P)
                for g in range(24):
                    w = wb[it % 2]
                    it += 1
                    c.dma("sp", w[:], Wv[:, :, g * 512:(g + 1) * 512], writes=[w])
                    ps = self.next_ps()
                    for j in range(4):
                        for kc in range(16):
                            c.op("pe", lambda e, w=w, j=j, kc=kc, ps=ps: e.matmul(
                                ps[:, 2 * j:2 * j + 2], w[:, kc, j * P:(j + 1) * P], sc[:, kc, :],
                                start=(kc == 0), stop=(kc == 15)), reads=[w, sc], writes=[ps])
                    c.op("dve", lambda e, l=l, g=g, ps=ps: e.tensor_tensor(
                        out=self.mod[:, l, g * 4:(g + 1) * 4, :],
                        in0=ps[:, 0:8].rearrange("p (a b) -> p a b", a=4),
                        in1=bm[:, l, g * 4:(g + 1) * 4].unsqueeze(2).broadcast_to([P, 4, 2]), op=ALU.add),
                        reads=[ps, bm], writes=[self.mod])
                for which, gs, base in ((0, self.gs1, 16), (1, self.gs2, 64)):
                    c.op("dve", lambda e, l=l, gs=gs, base=base: e.tensor_scalar(
                        out=gs[:, l], in0=self.mod[:, l, base:base + 16, :], scalar1=1.0, scalar2=None,
                        op0=ALU.add), reads=[self.mod], writes=[gs])
                    c.op("dve", lambda e, l=l, gs=gs, which=which: e.tensor_tensor(
                        out=gs[:, l], in0=gs[:, l],
                        in1=ng[:, which, l].unsqueeze(2).broadcast_to([P, 16, 2]), op=ALU.mult),
                        reads=[gs, ng], writes=[gs])
            c.barrier()

    def norm_tile(self, es_tiles, X, t0, hn, scale_ap_fn, bias_ap_fn):
        c = self.c
        xs_l, sq_l, rs_l = es_tiles
        Xv = X.rearrange("(fc p) t -> p fc t", p=P)
        SUB = 256
        for s in range(self.TT // SUB):
            xs = xs_l[s % len(xs_l)]
            sq = sq_l[s % len(sq_l)]
            rs = rs_l[s % len(rs_l)]
            c0 = t0 + s * SUB
            c.dma("sp", xs[:], Xv[:, :, c0:c0 + SUB], writes=[xs])
            c.op("act", lambda e, xs=xs, sq=sq: e.activation(out=sq[:], in_=xs[:], func=AF.Square),
                 reads=[xs], writes=[sq])
            ps = self.next_ps()
            for kc in range(16):
                c.op("pe", lambda e, kc=kc, ps=ps, sq=sq: e.matmul(
                    ps[:, :SUB], self.onesF[:], sq[:, kc, :], start=(kc == 0), stop=(kc == 15)),
                    reads=[self.onesF, sq], writes=[ps])
            c.op("act", lambda e, ps=ps, rs=rs: e.activation(
                out=rs[:], in_=ps[:, :SUB], func=AF.Sqrt, scale=1.0 / self.D, bias=self.eps_t[:]),
                reads=[ps, self.eps_t], writes=[rs])
            c.op("dve", lambda e, rs=rs: e.reciprocal(out=rs[:], in_=rs[:]), reads=[rs], writes=[rs])
            c.op("dve", lambda e, xs=xs, rs=rs: e.tensor_tensor(
                out=xs[:], in0=xs[:], in1=rs[:].unsqueeze(1).broadcast_to([P, 16, SUB]), op=ALU.mult),
                reads=[xs, rs], writes=[xs])
            for fc in range(16):
                c.op("act", lambda e, fc=fc, xs=xs, s=s: e.activation(
                    out=hn[:, fc, s * SUB:(s + 1) * SUB], in_=xs[:, fc, :], func=AF.Identity,
                    scale=scale_ap_fn(fc), bias=bias_ap_fn(fc)), reads=[xs, self.mod, self.gs1, self.gs2, self.fngt],
                    writes=[hn])

    def gemm(self, W, KC, nch, rhs, wbufs, epilogue, gw=512, chunk_filter=None):
        c = self.c
        Wv = W.rearrange("(kc p) n -> p kc n", p=P)
        cpg = gw // P
        it = 0
        for g in range(-(-nch // cpg)):
            chunks = [mc for mc in range(g * cpg, min(nch, (g + 1) * cpg)) if chunk_filter is None or chunk_filter(mc)]
            if not chunks:
                continue
            lo, hi = chunks[0], chunks[-1] + 1
            w = wbufs[it % len(wbufs)]
            it += 1
            c.dma("pool", w[:, :, 0:(hi - lo) * P], Wv[:, :, lo * P:hi * P], writes=[w])
            for mc in chunks:
                ps = self.next_ps()
                j = mc - lo
                for kc in range(KC):
                    c.op("pe", lambda e, w=w, j=j, kc=kc, ps=ps: e.matmul(
                        ps[:, :self.TT], w[:, kc, j * P:(j + 1) * P], rhs[:, kc, :],
                        start=(kc == 0), stop=(kc == KC - 1)), reads=[w, rhs], writes=[ps])
                epilogue(mc, ps)

    def ph_in(self, l, X, W, nch, Z, chunk_filter_fn=None):
        c = self.c
        with ExitStack() as es:
            xs_l = [self.sb(es, f"in_xs{i}", [P, 16, 256]) for i in range(2)]
            sq_l = [self.sb(es, f"in_sq{i}", [P, 16, 256]) for i in range(1)]
            rs_l = [self.sb(es, f"in_rs{i}", [P, 256]) for i in range(2)]
            hn_l = [self.sb(es, f"in_hn{i}", [P, 16, self.TT], BF16) for i in range(2)]
            wb = [self.sb(es, f"in_w{i}", [P, 16, 512], BF16) for i in range(3)]
            st = [self.sb(es, f"in_st{i}", [P, self.TT]) for i in range(4)]
            Zv = Z.rearrange("(mc p) t -> p mc t", p=P)
            sti = [0]
            for ti in range(self.Ttot // self.TT):
                t0 = ti * self.TT
                cond = self.tile_cond(t0)
                hn = hn_l[ti % 2]
                self.norm_tile((xs_l, sq_l, rs_l), X, t0, hn,
                               lambda fc: self.gs1[:, l, fc, cond:cond + 1],
                               lambda fc: self.mod[:, l, 0 + fc, cond:cond + 1])

                def epi(mc, ps, t0=t0):
                    s = st[sti[0] % 4]
                    sti[0] += 1
                    if sti[0] % 2:
                        c.op("act", lambda e: e.copy(out=s[:], in_=ps[:, :self.TT]), reads=[ps], writes=[s])
                    else:
                        c.op("dve", lambda e: e.tensor_copy(out=s[:], in_=ps[:, :self.TT]), reads=[ps], writes=[s])
                    c.dma("sp", Zv[:, mc, t0:t0 + self.TT], s[:], reads=[s])
                flt = None if chunk_filter_fn is None else (lambda mc, cond=cond: chunk_filter_fn(mc, cond))
                self.gemm(W, 16, nch, hn, wb, epi, chunk_filter=flt)
            c.barrier()

    def ph_out(self, l, Xin, Xout, W, KC):
        c = self.c
        with ExitStack() as es:
            mx_l = [self.sb(es, f"ou_mx{i}", [P, KC, self.TT], BF16) for i in range(2)]
            gw = 512 if KC <= 16 else 256
            wb = [self.sb(es, f"ou_w{i}", [P, KC, gw], BF16) for i in range(3)]
            xo = [self.sb(es, f"ou_x{i}", [P, self.TT]) for i in range(4)]
            Xi = Xin.rearrange("(fc p) t -> p fc t", p=P)
            Xo = Xout.rearrange("(fc p) t -> p fc t", p=P)
            Mv = self.MIX.rearrange("(kc p) t -> p kc t", p=P)
            k = [0]
            for ti in range(self.Ttot // self.TT):
                t0 = ti * self.TT
                cond = self.tile_cond(t0)
                mx = mx_l[ti % 2]
                c.dma("sp", mx[:], Mv[:, 0:KC, t0:t0 + self.TT], writes=[mx])

                def epi(mc, ps, t0=t0, cond=cond):
                    x = xo[k[0] % 4]
                    k[0] += 1
                    c.dma("sp", x[:], Xi[:, mc, t0:t0 + self.TT], writes=[x])
                    c.op("dve", lambda e: e.scalar_tensor_tensor(
                        out=x[:], in0=ps[:, :self.TT], scalar=self.mod[:, l, 32 + mc, cond:cond + 1], in1=x[:],
                        op0=ALU.mult, op1=ALU.add), reads=[ps, x, self.mod], writes=[x])
                    c.dma("sp", Xo[:, mc, t0:t0 + self.TT], x[:], reads=[x])
                self.gemm(W, KC, 16, mx, wb, epi, gw=gw)
            c.barrier()

    def ph_ffn(self, l, Xin, Xout):
        c = self.c
        TT = self.TT
        with ExitStack() as es:
            xs_l = [self.sb(es, f"ff_xs{i}", [P, 16, 256]) for i in range(1)]
            sq_l = [self.sb(es, f"ff_sq{i}", [P, 16, 256]) for i in range(1)]
            rs_l = [self.sb(es, f"ff_rs{i}", [P, 256]) for i in range(2)]
            hn_l = [self.sb(es, f"ff_hn{i}", [P, 16, TT], BF16) for i in range(1)]
            h_l = [self.sb(es, f"ff_h{i}", [P, 44, TT], BF16) for i in range(1)]
            wb = [self.sb(es, f"ff_w{i}", [P, 16, 512], BF16) for i in range(2)]
            wb2 = [self.sb(es, f"ff_v{i}", [P, 44, 128], BF16) for i in range(2)]
            sg = [self.sb(es, f"ff_sg{i}", [P, TT]) for i in range(5)]
            xo = [self.sb(es, f"ff_x{i}", [P, TT]) for i in range(3)]
            Xi = Xin.rearrange("(fc p) t -> p fc t", p=P)
            Xo = Xout.rearrange("(fc p) t -> p fc t", p=P)
            k = [0]
            for ti in range(self.Ttot // TT):
                t0 = ti * TT
                cond = self.tile_cond(t0)
                hn = hn_l[0]
                h = h_l[0]
                self.norm_tile((xs_l, sq_l, rs_l), Xin, t0, hn,
                               lambda fc: self.gs2[:, l, fc, cond:cond + 1],
                               lambda fc: self.mod[:, l, 48 + fc, cond:cond + 1])
                sgm = {}

                def epi1(mc, ps):
                    s = sg[mc % 5] if mc < 44 else None
                    if mc < 44:
                        c.op("act", lambda e: e.activation(out=s[:], in_=ps[:, :TT], func=AF.Silu),
                             reads=[ps], writes=[s])
                        sgm[mc] = s
                    else:
                        hc = mc - 44
                        s2 = sgm.pop(hc)
                        c.op("dve", lambda e: e.tensor_tensor(out=h[:, hc, :], in0=ps[:, :TT], in1=s2[:],
                                                              op=ALU.mult), reads=[ps, s2], writes=[h])
                Wv = self.ffn_w_in[l]
                for g in range(11):
                    self.gemm(Wv, 16, 88, hn, wb, epi1, chunk_filter=lambda mc, g=g: g * 4 <= mc < g * 4 + 4)
                    self.gemm(Wv, 16, 88, hn, wb, epi1, chunk_filter=lambda mc, g=g: 44 + g * 4 <= mc < 48 + g * 4)

                def epi2(mc, ps, t0=t0, cond=cond):
                    x = xo[k[0] % 3]
                    k[0] += 1
                    c.dma("sp", x[:], Xi[:, mc, t0:t0 + TT], writes=[x])
                    c.op("dve", lambda e: e.scalar_tensor_tensor(
                        out=x[:], in0=ps[:, :TT], scalar=self.mod[:, l, 80 + mc, cond:cond + 1], in1=x[:],
                        op0=ALU.mult, op1=ALU.add), reads=[ps, x, self.mod], writes=[x])
                    c.dma("sp", Xo[:, mc, t0:t0 + TT], x[:], reads=[x])
                self.gemm(self.ffn_w_out[l], 44, 16, h, wb2, epi2, gw=128)
            c.barrier()

    def ph_final(self, X):
        c = self.c
        TT = self.TT
        with ExitStack() as es:
            xs_l = [self.sb(es, f"fi_xs{i}", [P, 16, 256]) for i in range(2)]
            sq_l = [self.sb(es, f"fi_sq{i}", [P, 16, 256]) for i in range(1)]
            rs_l = [self.sb(es, f"fi_rs{i}", [P, 256]) for i in range(2)]
            yo = [self.sb(es, f"fi_y{i}", [P, 2048]) for i in range(2)]
            Xv = X.rearrange("(fc p) t -> p fc t", p=P)
            SUB = 256
            for s in range(self.Ttot // SUB):
                xs, sq, rs = xs_l[s % 2], sq_l[0], rs_l[s % 2]
                c0 = s * SUB
                c.dma("sp", xs[:], Xv[:, :, c0:c0 + SUB], writes=[xs])
                c.op("act", lambda e, xs=xs, sq=sq: e.activation(out=sq[:], in_=xs[:], func=AF.Square),
                     reads=[xs], writes=[sq])
                ps = self.next_ps()
                for kc in range(16):
                    c.op("pe", lambda e, kc=kc, ps=ps, sq=sq: e.matmul(
                        ps[:, :SUB], self.onesF[:], sq[:, kc, :], start=(kc == 0), stop=(kc == 15)),
                        reads=[self.onesF, sq], writes=[ps])
                c.op("act", lambda e, ps=ps, rs=rs: e.activation(
                    out=rs[:], in_=ps[:, :SUB], func=AF.Sqrt, scale=1.0 / self.D, bias=self.eps_t[:]),
                    reads=[ps, self.eps_t], writes=[rs])
                c.op("dve", lambda e, rs=rs: e.reciprocal(out=rs[:], in_=rs[:]), reads=[rs], writes=[rs])
                c.op("dve", lambda e, xs=xs, rs=rs: e.tensor_tensor(
                    out=xs[:], in0=xs[:], in1=rs[:].unsqueeze(1).broadcast_to([P, 16, SUB]), op=ALU.mult),
                    reads=[xs, rs], writes=[xs])
                c.op("dve", lambda e, xs=xs: e.tensor_tensor(
                    out=xs[:], in0=xs[:], in1=self.fngt[:].unsqueeze(2).broadcast_to([P, 16, SUB]), op=ALU.mult),
                    reads=[xs, self.fngt], writes=[xs])
                for tb in range(SUB // P):
                    y = yo[(s * 2 + tb) % 2]
                    for g in range(4):
                        ps = self.next_ps()
                        for j in range(4):
                            fc = g * 4 + j
                            c.op("pe", lambda e, fc=fc, j=j, ps=ps, xs=xs, tb=tb: e.transpose(
                                ps[:, j * P:(j + 1) * P], xs[:, fc, tb * P:(tb + 1) * P], self.identF[:]),
                                reads=[xs, self.identF], writes=[ps])
                        if g % 2:
                            c.op("act", lambda e, g=g, ps=ps, y=y: e.copy(out=y[:, g * 512:(g + 1) * 512], in_=ps[:]),
                                 reads=[ps], writes=[y])
                        else:
                            c.op("dve", lambda e, g=g, ps=ps, y=y: e.tensor_copy(out=y[:, g * 512:(g + 1) * 512],
                                                                                in_=ps[:]), reads=[ps], writes=[y])
                    r0 = c0 + tb * P
                    c.dma("sp", self.y_out[r0:r0 + P, :], y[:], reads=[y])
            c.barrier()


    def ph_attn(self, l2):
        c, nc = self.c, self.nc
        Ls, Lp, NPR, past = self.Ls, self.Lp, self.NPR, self.past
        Zv = self.Z.rearrange("(mc p) t -> p mc t", p=P)
        Mv = self.MIX.rearrange("(kc p) t -> p kc t", p=P)
        npb = past // P
        with ExitStack() as es:
            Lmax = max(Ls, Lp)
            QT = self.sb(es, "at_qt", [P, 8, Lmax], BF16)
            KD = self.sb(es, "at_kd", [P, 4, Lmax], BF16)
            VT = self.sb(es, "at_vt", [P, Lmax // P, 256], BF16)
            CK = self.sb(es, "at_ck", [P, 4, past], BF16)
            CV = self.sb(es, "at_cv", [P, npb, 256], BF16)
            MB = self.sb(es, "at_mb", [P, 2, P], BF16)
            onesB = self.sb(es, "at_ones", [P, 64], BF16)
            sk = self.sb(es, "at_sk", [P, 8])
            qa = [self.sb(es, f"at_qa{i}", [P, 12, P]) for i in range(2)]
            qb = [self.sb(es, f"at_qb{i}", [P, 12, P]) for i in range(2)]
            cs = [self.sb(es, f"at_cs{i}", [P, 2, P]) for i in range(2)]
            vin = [self.sb(es, f"at_vin{i}", [P, 2, P]) for i in range(2)]
            kvo = [self.sb(es, f"at_kvo{i}", [P, 2, 256]) for i in range(2)]
            ckin = self.sb(es, "at_ckin", [P, npb, 4, 2, 64])
            PT = [self.sb(es, f"at_pt{i}", [P, 8, 512], BF16) for i in range(2)]
            den = [self.sb(es, f"at_den{i}", [P, 256]) for i in range(2)]
            ao = [self.sb(es, f"at_ao{i}", [P, 2, P], BF16) for i in range(2)]
            c.op("dve", lambda e: e.memset(onesB[:], 1.0), writes=[onesB])
            c.dma("pool", MB[:], self.maskb.rearrange("a p q -> p a q"), writes=[MB])
            c.dma("sp", sk[:], self.sinkfm[l2], writes=[sk])
            c.op("act", lambda e: e.activation(out=sk[:], in_=sk[:], func=AF.Exp), reads=[sk], writes=[sk])
            c.dma("pool", CV[:], self.cache_v[l2].rearrange("(kb p) f -> p kb f", p=P), writes=[CV])
            ckv = self.cache_k[l2].rearrange("(kb p) (g d) -> p kb g d", p=P, d=64)
            for kb in range(npb):
                c.dma("sp", ckin[:, kb, :, 0, :], ckv[:, kb], writes=[ckin])
                c.dma("sp", ckin[:, kb, :, 1, :], ckv[:, kb], writes=[ckin])
            for kb in range(npb):
                ps = self.next_ps()
                for g in range(4):
                    c.op("pe", lambda e, ps=ps, g=g, kb=kb: e.transpose(
                        ps[:, g * P:(g + 1) * P], ckin[:, kb, g].rearrange("p a d -> p (a d)"), self.identF[:]),
                        reads=[ckin, self.identF], writes=[ps])
                c.op("act", lambda e, ps=ps, kb=kb: e.copy(
                    out=CK[:, :, kb * P:(kb + 1) * P], in_=ps[:].rearrange("p (g k) -> p g k", g=4)),
                    reads=[ps], writes=[CK])
            it = 0
            import os
            STG = int(os.environ.get("ATT_STAGE", "9"))
            for si in range((1 + NPR) if STG >= 1 else 0):
                ctx = si > 0
                L = Lp if ctx else Ls
                s0 = 0 if not ctx else Ls + (si - 1) * Lp
                nb = L // P
                for b in range(nb):
                    t0 = s0 + b * P
                    a, bb = qa[b % 2], qb[b % 2]
                    c.dma("sp", a[:, 0:8], Zv[:, 28:36, t0:t0 + P], writes=[a])
                    c.dma("sp", a[:, 8:12], Zv[:, 44:48, t0:t0 + P], writes=[a])
                    SK = os.environ.get("SK", "")
                    if not ctx and "r" not in SK:
                        c.dma("sp", bb[:, 0:8], Zv[:, 36:44, t0:t0 + P], writes=[bb])
                        c.dma("sp", bb[:, 8:12], Zv[:, 48:52, t0:t0 + P], writes=[bb])
                        cst = cs[b % 2]
                        c.dma("sp", cst[:, 0], self.ropec[:, b * P:(b + 1) * P], writes=[cst])
                        c.dma("sp", cst[:, 1], self.ropes[:, b * P:(b + 1) * P], writes=[cst])
                        c.op("dve", lambda e, a=a, cst=cst: e.tensor_tensor(
                            out=a[:], in0=a[:], in1=cst[:, 0:1, :].broadcast_to([P, 12, P]), op=ALU.mult),
                            reads=[a, cst], writes=[a])
                        c.op("dve", lambda e, bb=bb, cst=cst: e.tensor_tensor(
                            out=bb[:], in0=bb[:], in1=cst[:, 1:2, :].broadcast_to([P, 12, P]), op=ALU.mult),
                            reads=[bb, cst], writes=[bb])
                        c.op("dve", lambda e, a=a, bb=bb: e.tensor_tensor(out=a[:], in0=a[:], in1=bb[:], op=ALU.add),
                             reads=[a, bb], writes=[a])
                    if "q" not in SK:
                        c.op("act", lambda e, a=a, b=b: e.copy(out=QT[:, :, b * P:(b + 1) * P], in_=a[:, 0:8]),
                             reads=[a], writes=[QT])
                        c.op("act", lambda e, a=a, b=b: e.copy(out=KD[:, :, b * P:(b + 1) * P], in_=a[:, 8:12]),
                             reads=[a], writes=[KD])
                    if "v" in SK:
                        continue
                    v = vin[b % 2]
                    c.dma("sp", v[:], Zv[:, 52:54, t0:t0 + P], writes=[v])
                    ps = self.next_ps()
                    for j in range(2):
                        c.op("pe", lambda e, ps=ps, j=j, v=v: e.transpose(
                            ps[:, j * P:(j + 1) * P], v[:, j, :], self.identF[:]), reads=[v, self.identF], writes=[ps])
                    if ctx and "k" not in SK:
                        for g in range(4):
                            c.op("pe", lambda e, ps=ps, g=g, a=a: e.transpose(
                                ps[:, 256 + g * 64:256 + (g + 1) * 64], a[0:64, 8 + g, :], self.identF[0:64, 0:64]),
                                reads=[a, self.identF], writes=[ps])
                    c.op("dve", lambda e, ps=ps, b=b: e.tensor_copy(out=VT[:, b, :], in_=ps[:, 0:256]),
                         reads=[ps], writes=[VT])
                    if ctx and "o" not in SK:
                        ko = kvo[b % 2]
                        if "c" not in SK:
                            c.op("act", lambda e, ps=ps, ko=ko: e.copy(
                                out=ko[:], in_=ps[:].rearrange("p (a f) -> p a f", a=2)), reads=[ps], writes=[ko])
                        if "d" not in SK:
                            c.dma("sp", self.ncv[si - 1, l2, b * P:(b + 1) * P, :], ko[:, 0, :], reads=[ko])
                            c.dma("sp", self.nck[si - 1, l2, b * P:(b + 1) * P, :], ko[:, 1, :], reads=[ko])
                for b in range(nb if STG >= 2 else 0):
                    if ctx:
                        kbs = [("w", j, None) for j in range(nb)]
                    else:
                        kbs = [("w", j, j - b) for j in (b - 1, b, b + 1) if 0 <= j < nb] + [("c", j, None) for j in range(npb)]
                    for g in range(4):
                        pt = PT[it % 2]
                        it += 1
                        for ki, (kind, j, rel) in enumerate(kbs):
                            masked = rel is not None and rel != 0
                            for hp in range(2):
                                ps = self.next_ps()
                                if masked:
                                    mi = 0 if rel < 0 else 1
                                    c.op("pe", lambda e, ps=ps, mi=mi: e.matmul(
                                        ps[:, 0:256].rearrange("p (h q) -> p h q", h=2), self.identB[:],
                                        MB[:, mi:mi + 1, :].broadcast_to([P, 2, P]), start=True, stop=False),
                                        reads=[self.identB, MB], writes=[ps])
                                for ii in range(2):
                                    h = 4 * g + 2 * ii + hp
                                    ksrc = KD[hp * 64:(hp + 1) * 64, g, j * P:(j + 1) * P] if kind == "w" else \
                                        CK[hp * 64:(hp + 1) * 64, g, j * P:(j + 1) * P]
                                    c.op("pe", lambda e, ps=ps, ii=ii, ksrc=ksrc, hp=hp, h=h, masked=masked: e.matmul(
                                        ps[:, ii * P:(ii + 1) * P], ksrc,
                                        QT[hp * 64:(hp + 1) * 64, h // 2, b * P:(b + 1) * P],
                                        start=(ii == 0 and not masked), stop=(ii == 1)),
                                        reads=[KD, CK, QT], writes=[ps])
                                c.op("act", lambda e, ps=ps, ki=ki, pt=pt, hp=hp: e.activation(
                                    out=pt[:, ki, :].rearrange("p (i q) -> p i q", i=4)[:, hp::2, :],
                                    in_=ps[:, 0:256].rearrange("p (i q) -> p i q", i=2), func=AF.Exp, scale=0.125),
                                    reads=[ps], writes=[pt])
                        po = self.next_ps()
                        pd = self.next_ps()
                        nk = len(kbs)
                        for ki, (kind, j, rel) in enumerate(kbs if STG >= 3 else []):
                            vsrc = VT[:, j, g * 64:(g + 1) * 64] if kind == "w" else CV[:, j, g * 64:(g + 1) * 64]
                            for hp in range(2):
                                rhs = pt[:, ki, :].rearrange("p (i q) -> p i q", i=4)[:, hp::2, :]
                                c.op("pe", lambda e, po=po, hp=hp, vsrc=vsrc, rhs=rhs, ki=ki: e.matmul(
                                    po[hp * 64:(hp + 1) * 64, 0:256].rearrange("p (i q) -> p i q", i=2), vsrc, rhs,
                                    start=(ki == 0), stop=(ki == nk - 1), tile_position=(0, hp * 64)),
                                    reads=[VT, CV, pt], writes=[po])
                                c.op("pe", lambda e, pd=pd, hp=hp, rhs=rhs, ki=ki: e.matmul(
                                    pd[hp * 64:(hp + 1) * 64, 0:256].rearrange("p (i q) -> p i q", i=2), onesB[:], rhs,
                                    start=(ki == 0), stop=(ki == nk - 1), tile_position=(0, hp * 64)),
                                    reads=[onesB, pt], writes=[pd])
                        if STG < 4:
                            continue
                        dn = den[it % 2]
                        o = ao[it % 2]
                        c.op("dve", lambda e, pd=pd, dn=dn, g=g: e.tensor_tensor(
                            out=dn[:].rearrange("p (i q) -> p i q", i=2),
                            in0=pd[:, 0:256].rearrange("p (i q) -> p i q", i=2),
                            in1=sk[:, 2 * g:2 * g + 2].unsqueeze(2).broadcast_to([P, 2, P]), op=ALU.add),
                            reads=[pd, sk], writes=[dn])
                        c.op("dve", lambda e, dn=dn: e.reciprocal(out=dn[:], in_=dn[:]), reads=[dn], writes=[dn])
                        c.op("dve", lambda e, po=po, dn=dn, o=o: e.tensor_tensor(
                            out=o[:], in0=po[:, 0:256].rearrange("p (i q) -> p i q", i=2),
                            in1=dn[:].rearrange("p (i q) -> p i q", i=2), op=ALU.mult), reads=[po, dn], writes=[o])
                        t0 = s0 + b * P
                        c.dma("sp", Mv[:, 8 + 2 * g:8 + 2 * g + 2, t0:t0 + P], o[:], reads=[o])
            c.barrier()


    def ph_rw0(self, l2):
        c = self.c
        Ls, Lp, NPR = self.Ls, self.Lp, self.NPR
        Zv = self.Z.rearrange("(mc p) t -> p mc t", p=P)
        RWv = [a.rearrange("(fc p) t -> p fc t", p=P) for a in self.RW]
        with ExitStack() as es:
            sb = lambda n, sh, dt=F32: self.sb(es, "r0_" + n, sh, dt)
            mu, w0a0, vec = sb("mu", [P, 2, 28]), sb("w0a0", [P, 2, 2, 8]), sb("vec", [P, 5, 8])
            w2s, a2s, g2s = sb("w2", [P, 1024], BF16), sb("a2", [P, 1024], BF16), sb("g2", [P, 2, 1024], BF16)
            blk, omk, eps12 = sb("blk", [P, P]), sb("omk", [P, 8]), sb("eps12", [P, 1])
            zt, t1, t2 = sb("zt", [P, 28, P + 2]), sb("t1", [P, 28, P]), sb("t2", [P, 28, P])
            twd, adb, sgd = sb("twd", [P, P], BF16), sb("adb", [P, P], BF16), sb("sgd", [P, 2, P], BF16)
            LW = [sb(f"lw{i}", [P, 8, P]) for i in range(2)]
            AD = [sb(f"ad{i}", [P, 8, P]) for i in range(2)]
            KD = [sb(f"kd{i}", [P, 8, P]) for i in range(2)]
            BD = [sb(f"bd{i}", [P, 8, P]) for i in range(2)]
            G, KK, SQ, BV, U = sb("g", [P, 8, P]), sb("kk", [P, 8, P]), sb("sq", [P, 8, P]), sb("bv", [P, 8, P]), sb("u", [P, 8, P])
            c.dma("sp", mu[:], self.r_mu[l2], writes=[mu])
            c.dma("sp", w0a0[:], self.r_w0a0[l2], writes=[w0a0])
            c.dma("sp", vec[:], self.r_vec[l2], writes=[vec])
            c.dma("pool", w2s[:], self.r_w2[l2], writes=[w2s])
            c.dma("pool", a2s[:], self.r_a2[l2], writes=[a2s])
            c.dma("pool", g2s[:], self.r_g2[l2], writes=[g2s])
            c.dma("sp", blk[:], self.r_blk, writes=[blk])
            c.op("dve", lambda e: e.memset(eps12[:], 1e-12), writes=[eps12])
            c.op("dve", lambda e: e.tensor_scalar(out=omk[:], in0=vec[:, 1, :], scalar1=-1.0, scalar2=1.0,
                                                  op0=ALU.mult, op1=ALU.add), reads=[vec], writes=[omk])
            bc = lambda ap: ap.unsqueeze(2).broadcast_to([P, ap.shape[1], P])
            for si in range(1 + NPR):
                ctx = si > 0
                L = Lp if ctx else Ls
                s0 = 0 if not ctx else Ls + (si - 1) * Lp
                for b in range(L // P):
                    t0 = s0 + b * P
                    lo = 1 if b == 0 else 0
                    hi = P + 1 if b == L // P - 1 else P + 2
                    if lo or hi < P + 2:
                        c.op("dve", lambda e: e.memset(zt[:], 0.0), writes=[zt])
                    c.dma("sp", zt[:, :, lo:hi], Zv[:, 0:28, t0 - 1 + lo:t0 - 1 + hi], writes=[zt])
                    z = zt[:, :, 1:P + 1]
                    c.op("dve", lambda e: e.tensor_tensor(out=t1[:], in0=zt[:, :, 0:P], in1=z, op=ALU.subtract), reads=[zt], writes=[t1])
                    c.op("dve", lambda e: e.tensor_tensor(out=t1[:], in0=t1[:], in1=bc(mu[:, 0, :]), op=ALU.mult), reads=[t1, mu], writes=[t1])
                    c.op("dve", lambda e: e.tensor_tensor(out=t2[:], in0=zt[:, :, 2:P + 2], in1=z, op=ALU.subtract), reads=[zt], writes=[t2])
                    c.op("dve", lambda e: e.tensor_tensor(out=t2[:], in0=t2[:], in1=bc(mu[:, 1, :]), op=ALU.mult), reads=[t2, mu], writes=[t2])
                    c.op("dve", lambda e: e.tensor_tensor(out=t1[:], in0=t1[:], in1=z, op=ALU.add), reads=[t1, zt], writes=[t1])
                    c.op("dve", lambda e: e.tensor_tensor(out=t1[:], in0=t1[:], in1=t2[:], op=ALU.add), reads=[t1, t2], writes=[t1])
                    zs = t1
                    c.op("act", lambda e: e.activation(out=twd[:], in_=zs[:, 24, :], func=AF.Tanh), reads=[zs], writes=[twd])
                    c.op("act", lambda e: e.copy(out=adb[:], in_=zs[:, 25, :]), reads=[zs], writes=[adb])
                    c.op("act", lambda e: e.activation(out=sgd[:], in_=zs[:, 26:28, :], func=AF.Sigmoid), reads=[zs], writes=[sgd])
                    for which, (wsb, src, dst) in enumerate(((w2s, twd, LW), (a2s, adb, AD))):
                        for d in range(2):
                            for half in range(2):
                                ps = self.next_ps()
                                for q in range(4):
                                    oc = half * 4 + q
                                    c.op("pe", lambda e, ps=ps, q=q, oc=oc, d=d, wsb=wsb, src=src: e.matmul(
                                        ps[:, q * P:(q + 1) * P], wsb[d * 64:(d + 1) * 64, oc * P:(oc + 1) * P],
                                        src[d * 64:(d + 1) * 64, :], start=True, stop=True), reads=[wsb, src], writes=[ps])
                                for q in range(4):
                                    oc = half * 4 + q
                                    c.op("act", lambda e, ps=ps, q=q, oc=oc, d=d, which=which, dst=dst: e.activation(
                                        out=dst[d][:, oc, :], in_=ps[:, q * P:(q + 1) * P], func=AF.Sigmoid,
                                        bias=w0a0[:, which, d, oc:oc + 1]), reads=[ps, w0a0], writes=[dst[d]])
                    for d in range(2):
                        c.op("dve", lambda e, d=d: e.tensor_scalar(out=LW[d][:], in0=LW[d][:], scalar1=-math.exp(-0.5),
                                                                  scalar2=None, op0=ALU.mult), reads=[LW[d]], writes=[LW[d]])
                    for half in range(2):
                        ps = self.next_ps()
                        for q in range(4):
                            oc = half * 4 + q
                            for sl in range(2):
                                c.op("pe", lambda e, ps=ps, q=q, oc=oc, sl=sl: e.matmul(
                                    ps[:, q * P:(q + 1) * P], g2s[:, sl, oc * P:(oc + 1) * P], sgd[:, sl, :],
                                    start=(sl == 0), stop=(sl == 1)), reads=[g2s, sgd], writes=[ps])
                        c.op("act", lambda e, ps=ps, half=half: e.copy(
                            out=G[:, half * 4:(half + 1) * 4, :].rearrange("p a b -> p (a b)"), in_=ps[:]), reads=[ps], writes=[G])
                    c.op("dve", lambda e: e.tensor_tensor(out=KK[:], in0=zs[:, 8:16, :], in1=bc(vec[:, 0, :]), op=ALU.mult),
                         reads=[zs, vec], writes=[KK])
                    c.op("act", lambda e: e.activation(out=SQ[:], in_=KK[:], func=AF.Square), reads=[KK], writes=[SQ])
                    for half in range(2):
                        ps = self.next_ps()
                        for q in range(4):
                            fc = half * 4 + q
                            c.op("pe", lambda e, ps=ps, q=q, fc=fc: e.matmul(ps[:, q * P:(q + 1) * P], blk[:], SQ[:, fc, :],
                                                                             start=True, stop=True), reads=[blk, SQ], writes=[ps])
                        c.op("act", lambda e, ps=ps, half=half: e.activation(
                            out=U[:, half * 4:(half + 1) * 4, :].rearrange("p a b -> p (a b)"), in_=ps[:], func=AF.Sqrt,
                            bias=eps12[:, 0:1]), reads=[ps, eps12], writes=[U])
                    c.op("dve", lambda e: e.reciprocal(out=U[:], in_=U[:]), reads=[U], writes=[U])
                    c.op("dve", lambda e: e.tensor_tensor(out=KK[:], in0=KK[:], in1=U[:], op=ALU.mult), reads=[KK, U], writes=[KK])
                    for d in range(2):
                        c.op("dve", lambda e, d=d: e.tensor_tensor(out=U[:], in0=AD[d][:], in1=bc(vec[:, 1, :]), op=ALU.mult),
                             reads=[AD[d], vec], writes=[U])
                        c.op("dve", lambda e: e.tensor_tensor(out=U[:], in0=U[:], in1=bc(omk[:]), op=ALU.add), reads=[U, omk], writes=[U])
                        c.op("dve", lambda e, d=d: e.tensor_tensor(out=KD[d][:], in0=U[:], in1=zs[:, 8:16, :], op=ALU.mult),
                             reads=[U, zs], writes=[KD[d]])
                        c.op("dve", lambda e, d=d: e.tensor_tensor(out=BD[d][:], in0=KK[:], in1=AD[d][:], op=ALU.mult),
                             reads=[KK, AD[d]], writes=[BD[d]])
                        c.op("dve", lambda e, d=d: e.tensor_tensor(out=AD[d][:], in0=KD[d][:], in1=zs[:, 0:8, :], op=ALU.mult),
                             reads=[KD[d], zs], writes=[AD[d]])
                        c.op("dve", lambda e, d=d: e.tensor_tensor(out=AD[d][:], in0=AD[d][:], in1=bc(vec[:, 2, :]), op=ALU.mult),
                             reads=[AD[d], vec], writes=[AD[d]])
                    for half in range(2):
                        ps = self.next_ps()
                        for q in range(4):
                            fc = half * 4 + q
                            for d in range(2):
                                c.op("pe", lambda e, ps=ps, q=q, fc=fc, d=d: e.matmul(
                                    ps[:, q * P:(q + 1) * P], blk[:], AD[d][:, fc, :], start=(d == 0), stop=(d == 1)),
                                    reads=[blk, AD[d]], writes=[ps])
                        c.op("dve", lambda e, ps=ps, half=half: e.tensor_tensor(
                            out=BV[:, half * 4:(half + 1) * 4, :], in0=ps[:].rearrange("p (a b) -> p a b", a=4),
                            in1=zs[:, 16 + half * 4:20 + half * 4, :], op=ALU.mult), reads=[ps, zs], writes=[BV])
                    outs = [(0, zs[:, 0:8, :], zs), (1, zs[:, 16:24, :], zs), (2, KK[:], KK), (3, G[:], G), (4, BV[:], BV),
                            (5, LW[0][:], LW[0]), (6, LW[1][:], LW[1]), (7, KD[0][:], KD[0]), (8, KD[1][:], KD[1]),
                            (9, BD[0][:], BD[0]), (10, BD[1][:], BD[1])]
                    for idx, ap, tl in outs:
                        c.dma("sp", RWv[idx][:, :, t0:t0 + P], ap, reads=[tl])
            c.barrier()

    def ph_rw1(self, l2, d):
        c = self.c
        Ls, Lp, NPR = self.Ls, self.Lp, self.NPR
        RWv = [a.rearrange("(fc p) t -> p fc t", p=P) for a in self.RW]
        Mv = self.MIX.rearrange("(kc p) t -> p kc t", p=P)
        C = 64
        with ExitStack() as es:
            sb = lambda n, sh, dt=F32: self.sb(es, "r1_" + n, sh, dt)
            f4 = lambda n: sb(n, [P, 8, P])
            R, V, KKN, LWt, KDt, Bt = f4("R"), f4("V"), f4("KKN"), f4("LW"), f4("KD"), f4("B")
            cum, Wt, Wi, Wp, Wh, tmp = f4("cum"), f4("Wt"), f4("Wi"), f4("Wp"), f4("Wh"), f4("tmp")
            Yt, YFt = f4("Yt"), f4("YFt")
            b4 = lambda n: sb(n, [P, 8, 2, P], BF16)
            AR, AtD, KtD, KhD, VD = b4("AR"), b4("AtD"), b4("KtD"), b4("KhD"), b4("VD")
            ARf, BtDf, BsF = sb("ARf", [P, 8, 2, P]), sb("BtDf", [P, 8, 2, P]), sb("BsF", [P, 8, 2, C])
            AtDf, BhDf = sb("AtDf", [P, 8, 2, P]), sb("BhDf", [P, 8, 2, P])
            TtDf, btDf = sb("TtDf", [P, 8, P]), sb("btDf", [P, 8, P])
            RHSf, USf = sb("RHSf", [P, 8, C]), sb("USf", [P, 8, C])
            sS = lambda n: sb(n, [P, 8, C], BF16)
            sD = lambda n: sb(n, [P, 8, P], BF16)
            fS = lambda n: sb(n, [P, 8, C])
            fD = lambda n: sb(n, [P, 8, P])
            XS, XD, YS, YD = [fS("XS0"), fS("XS1")], [fD("XD0"), fD("XD1")], [fS("YS0"), fS("YS1")], [fD("YD0"), fD("YD1")]
            AakD, ArkS, ArbS = sD("AakD"), sS("ArkS"), sS("ArbS")
            TtF = sb("TtF", [P, 8, C])
            VtD, VtS, ktD = sD("VtD"), sS("VtS"), sD("ktD")
            US, UD = sS("US"), sD("UD")
            SF, SBf, STD, stmp = sb("SF", [P, 8, C]), sS("SB"), sD("STD"), sb("stmp", [P, 8, C])
            cumC, Wc = sb("cumC", [P, 8, 2]), sb("Wc", [P, 8, 2])
            mN, mS = sb("mN", [P, C]), sb("mS", [P, P])
            IS, blk64, ones64 = sb("IS", [P, C]), sb("blk64", [P, P]), sb("ones64", [P, C])
            vec, epsg = sb("vec", [P, 5, 8]), sb("epsg", [P, 1])
            sio = sb("sio", [C, 8, P])
            Gt, BVt, yo = f4("Gt"), f4("BVt"), sb("yo", [P, 8, P], BF16)
            c.dma("sp", mN[:], self.r_maskn[d], writes=[mN])
            c.dma("sp", mS[:], self.r_masks[d], writes=[mS])
            c.dma("sp", IS[:], self.r_is, writes=[IS])
            c.dma("sp", blk64[:], self.r_blk, writes=[blk64])
            c.dma("sp", vec[:], self.r_vec[l2], writes=[vec])
            c.op("dve", lambda e: e.tensor_scalar(out=blk64[:], in0=blk64[:], scalar1=1.0 / 64, scalar2=None, op0=ALU.mult),
                 reads=[blk64], writes=[blk64])
            c.op("dve", lambda e: e.memset(ones64[:], 1.0), writes=[ones64])
            c.op("dve", lambda e: e.memset(epsg[:], 64e-5), writes=[epsg])
            for t in (AtD, KtD, BtDf, KhD, VD, XD[0], XD[1], YD[0], YD[1], AakD, UD, STD, AtDf, BhDf, TtDf):
                c.op("dve", lambda e, t=t: e.memset(t[:], 0.0), writes=[t])
            H = ((0, 64), (64, 128))
            bcv = lambda ap, n: ap.unsqueeze(1).broadcast_to([ap.shape[0], n, ap.shape[1]])

            def to_diag(dst, src, rd):
                for i, (a, b) in enumerate(H):
                    eng = "act" if i else "dve"
                    if eng == "act":
                        c.op("act", lambda e, a=a, b=b: e.copy(out=dst[a:b, :, a:b], in_=src[a:b, :, :]), reads=rd, writes=[dst])
                    else:
                        c.op("dve", lambda e, a=a, b=b: e.tensor_copy(out=dst[a:b, :, a:b], in_=src[a:b, :, :]), reads=rd, writes=[dst])

            def mm8(out_ps, cols, lhs_fn, rhs_fn, reads, groups=1, accum=None):
                for fc in range(8):
                    terms = lhs_fn(fc) if isinstance(lhs_fn(fc), list) else [lhs_fn(fc)]
                    rterms = rhs_fn(fc) if isinstance(rhs_fn(fc), list) else [rhs_fn(fc)]
                    n = len(terms)
                    for i in range(n):
                        c.op("pe", lambda e, fc=fc, i=i, terms=terms, rterms=rterms, n=n: e.matmul(
                            out_ps[:, fc * cols:(fc + 1) * cols], terms[i], rterms[i], start=(i == 0), stop=(i == n - 1)),
                            reads=reads, writes=[out_ps])

            for si in range(1 + NPR):
                ctx = si > 0
                L = Lp if ctx else Ls
                s0 = 0 if not ctx else Ls + (si - 1) * Lp
                if ctx:
                    c.op("dve", lambda e: e.memset(SF[:], 0.0), writes=[SF])
                else:
                    for hh in range(2):
                        c.dma("sp", sio[:, :, hh * C:(hh + 1) * C],
                              self.state_rwkv[l2, d].rearrange("(fc hh) v k -> hh v fc k", hh=2)[hh], writes=[sio])
                    ps = self.next_ps()
                    for fc in range(8):
                        c.op("pe", lambda e, ps=ps, fc=fc: e.transpose(ps[:, fc * C:(fc + 1) * C], sio[:, fc, :],
                                                                       self.identF[0:C, 0:C]), reads=[sio, self.identF], writes=[ps])
                    c.op("dve", lambda e, ps=ps: e.tensor_copy(out=SF[:].rearrange("p a b -> p (a b)"), in_=ps[:]), reads=[ps], writes=[SF])
                c.op("act", lambda e: e.copy(out=SBf[:], in_=SF[:]), reads=[SF], writes=[SBf])
                to_diag(STD, SBf, [SBf])
                nt = L // P
                for b in (range(nt) if d == 0 else range(nt - 1, -1, -1)):
                    t0 = s0 + b * P
                    for tl, idx in ((R, 0), (V, 1), (KKN, 2), (LWt, 5 + d), (KDt, 7 + d), (Bt, 9 + d)):
                        c.dma("sp", tl[:], RWv[idx][:, :, t0:t0 + P], writes=[tl])
                    for fc in range(8):
                        for cc in range(2):
                            sl = slice(cc * C, (cc + 1) * C)
                            if d == 0:
                                c.op("dve", lambda e, fc=fc, sl=sl: e.tensor_tensor_scan(
                                    out=cum[:, fc, sl], data0=ones64[:], data1=LWt[:, fc, sl], initial=0.0,
                                    op0=ALU.mult, op1=ALU.add), reads=[ones64, LWt], writes=[cum])
                            else:
                                c.op("dve", lambda e, fc=fc, sl=sl: e.tensor_tensor_scan(
                                    out=cum[:, fc, sl][:, ::-1], data0=ones64[:], data1=LWt[:, fc, sl][:, ::-1], initial=0.0,
                                    op0=ALU.mult, op1=ALU.add), reads=[ones64, LWt], writes=[cum])
                    cend = (C - 1) if d == 0 else 0
                    c.op("dve", lambda e: e.tensor_copy(out=cumC[:], in_=cum[:].rearrange("p f (c t) -> p f c t", c=2)[:, :, :, cend]),
                         reads=[cum], writes=[cumC])
                    c.op("act", lambda e: e.activation(out=Wc[:], in_=cumC[:], func=AF.Exp), reads=[cumC], writes=[Wc])
                    c.op("act", lambda e: e.activation(out=Wt[:], in_=cum[:], func=AF.Exp), reads=[cum], writes=[Wt])
                    c.op("act", lambda e: e.activation(out=Wi[:], in_=cum[:], func=AF.Exp, scale=-1.0), reads=[cum], writes=[Wi])
                    c.op("dve", lambda e: e.tensor_tensor(out=tmp[:], in0=cum[:], in1=LWt[:], op=ALU.subtract), reads=[cum, LWt], writes=[tmp])
                    c.op("act", lambda e: e.activation(out=Wp[:], in_=tmp[:], func=AF.Exp), reads=[tmp], writes=[Wp])
                    for fc in range(8):
                        for cc in range(2):
                            sl = slice(cc * C, (cc + 1) * C)
                            c.op("act", lambda e, fc=fc, cc=cc, sl=sl: e.activation(
                                out=Wh[:, fc, sl], in_=cum[:, fc, sl], func=AF.Exp, scale=-1.0, bias=cumC[:, fc, cc:cc + 1]),
                                reads=[cum, cumC], writes=[Wh])
                    v4 = lambda t: t[:].rearrange("p f (c t) -> p f c t", c=2)
                    c.op("dve", lambda e: e.scalar_tensor_tensor(out=ARf[:, :, :, 0:C], in0=v4(KKN), scalar=-1.0, in1=v4(Wp),
                                                                 op0=ALU.mult, op1=ALU.mult), reads=[KKN, Wp], writes=[ARf])
                    c.op("dve", lambda e: e.tensor_tensor(out=ARf[:, :, :, C:P], in0=v4(R), in1=v4(Wt), op=ALU.mult), reads=[R, Wt], writes=[ARf])
                    c.op("act", lambda e: e.copy(out=AR[:], in_=ARf[:]), reads=[ARf], writes=[AR])
                    c.op("dve", lambda e: e.tensor_tensor(out=BsF[:], in0=v4(Bt), in1=v4(Wi), op=ALU.mult), reads=[Bt, Wi], writes=[BsF])
                    for a, bb in H:
                        hv = lambda t, a=a, bb=bb: t[a:bb].rearrange("p f (c t) -> p f c t", c=2)
                        c.op("dve", lambda e, a=a, bb=bb: e.tensor_copy(out=AtD[a:bb, :, :, a:bb], in_=AR[a:bb, :, :, 0:C]), reads=[AR], writes=[AtD])
                        c.op("dve", lambda e, a=a, bb=bb, hv=hv: e.scalar_tensor_tensor(
                            out=AtDf[a:bb, :, :, a:bb], in0=hv(KKN), scalar=-1.0, in1=hv(Wp), op0=ALU.mult, op1=ALU.mult),
                            reads=[KKN, Wp], writes=[AtDf])
                        c.op("dve", lambda e, a=a, bb=bb, hv=hv: e.tensor_tensor(out=BhDf[a:bb, :, :, a:bb], in0=hv(Bt), in1=hv(Wh), op=ALU.mult),
                             reads=[Bt, Wh], writes=[BhDf])
                        c.op("dve", lambda e, a=a, bb=bb, hv=hv: e.tensor_tensor(out=KtD[a:bb, :, :, a:bb], in0=hv(KDt), in1=hv(Wi), op=ALU.mult),
                             reads=[KDt, Wi], writes=[KtD])
                        c.op("dve", lambda e, a=a, bb=bb: e.tensor_copy(out=BtDf[a:bb, :, :, a:bb], in_=BsF[a:bb]), reads=[BsF], writes=[BtDf])
                        c.op("dve", lambda e, a=a, bb=bb, hv=hv: e.tensor_tensor(out=KhD[a:bb, :, :, a:bb], in0=hv(KDt), in1=hv(Wh), op=ALU.mult),
                             reads=[KDt, Wh], writes=[KhD])
                        c.op("act", lambda e, a=a, bb=bb, hv=hv: e.copy(out=VD[a:bb, :, :, a:bb], in_=hv(V)), reads=[V], writes=[VD])
                    for cc in ((0, 1) if d == 0 else (1, 0)):
                        pN = self.next_ps()
                        mm8(pN, C, lambda fc: AtDf[:, fc, cc, :], lambda fc: BsF[:, fc, cc, :], [AtDf, BsF])
                        c.op("dve", lambda e, pN=pN: e.tensor_tensor(out=XS[0][:], in0=pN[:].rearrange("p (f s) -> p f s", f=8),
                                                                     in1=bcv(mN[:], 8), op=ALU.mult), reads=[pN, mN], writes=[XS[0]])
                        to_diag(XD[0], XS[0], [XS[0]])
                        for which, lD in ((0, KtD), (1, BtDf)):
                            for half in range(2):
                                ps = self.next_ps()
                                for q in range(4):
                                    fc = half * 4 + q
                                    rA = AR if which == 0 else ARf
                                    c.op("pe", lambda e, ps=ps, q=q, fc=fc, lD=lD, rA=rA: e.matmul(
                                        ps[:, q * P:(q + 1) * P], lD[:, fc, cc, :], rA[:, fc, cc, :], start=True, stop=True),
                                        reads=[lD, rA], writes=[ps])
                                pv = ps[:].rearrange("p (f s) -> p f s", f=4)
                                fs = slice(half * 4, half * 4 + 4)
                                dstA = (AakD if which == 0 else YS[0])
                                dstR = (ArkS if which == 0 else ArbS)
                                if which == 0:
                                    for a, bb in H:
                                        c.op("dve", lambda e, a=a, bb=bb, pv=pv, fs=fs: e.tensor_tensor(
                                            out=AakD[a:bb, fs, a:bb], in0=pv[a:bb, :, 0:C], in1=bcv(mS[a:bb, 0:C], 4), op=ALU.mult),
                                            reads=[ps, mS], writes=[AakD])
                                else:
                                    c.op("dve", lambda e, pv=pv, fs=fs: e.tensor_tensor(
                                        out=YS[0][:, fs, :], in0=pv[:, :, 0:C], in1=bcv(mS[:, 0:C], 4), op=ALU.mult),
                                        reads=[ps, mS], writes=[YS[0]])
                                c.op("dve", lambda e, pv=pv, fs=fs, dstR=dstR: e.tensor_tensor(
                                    out=dstR[:, fs, :], in0=pv[:, :, C:P], in1=bcv(mS[:, C:P], 4), op=ALU.mult),
                                    reads=[ps, mS], writes=[dstR])
                        to_diag(YD[0], YS[0], [YS[0]])
                        c.op("dve", lambda e: e.tensor_tensor(out=TtF[:], in0=YS[0][:], in1=bcv(IS[:], 8), op=ALU.add),
                             reads=[YS[0], IS], writes=[TtF])
                        cur = 0
                        for lvl in range(1, 6):
                            nx = 1 - cur
                            pA = self.next_ps()
                            mm8(pA, C, lambda fc: YD[cur][:, fc, :], lambda fc: XS[cur][:, fc, :], [YD[cur], XS[cur]])
                            if lvl < 5:
                                pB = self.next_ps()
                                mm8(pB, C, lambda fc: XD[cur][:, fc, :], lambda fc: YS[cur][:, fc, :], [XD[cur], YS[cur]])
                            c.op("dve", lambda e, pA=pA, nx=nx: e.tensor_copy(out=XS[nx][:].rearrange("p a b -> p (a b)"), in_=pA[:]),
                                 reads=[pA], writes=[XS[nx]])
                            to_diag(XD[nx], XS[nx], [XS[nx]])
                            if lvl < 5:
                                c.op("act", lambda e, pB=pB, nx=nx: e.copy(out=YS[nx][:].rearrange("p a b -> p (a b)"), in_=pB[:]),
                                     reads=[pB], writes=[YS[nx]])
                                to_diag(YD[nx], YS[nx], [YS[nx]])
                            pT = self.next_ps()
                            mm8(pT, C, lambda fc: XD[nx][:, fc, :], lambda fc: TtF[:, fc, :], [XD[nx], TtF])
                            c.op("dve", lambda e, pT=pT: e.tensor_tensor(out=TtF[:].rearrange("p a b -> p (a b)"),
                                                                         in0=TtF[:].rearrange("p a b -> p (a b)"), in1=pT[:], op=ALU.add),
                                 reads=[TtF, pT], writes=[TtF])
                            cur = nx
                        to_diag(TtDf, TtF, [TtF])
                        for half in range(2):
                            ps = self.next_ps()
                            for q in range(4):
                                fc = half * 4 + q
                                c.op("pe", lambda e, ps=ps, q=q, fc=fc: e.transpose(ps[:, q * P:(q + 1) * P], BhDf[:, fc, cc, :], self.identF[:]),
                                     reads=[BhDf, self.identF], writes=[ps])
                            c.op("dve", lambda e, ps=ps, half=half: e.tensor_copy(
                                out=btDf[:, half * 4:(half + 1) * 4, :].rearrange("p a b -> p (a b)"), in_=ps[:]), reads=[ps], writes=[btDf])
                        for src, dstD in ((VD, VtD), (KhD, ktD)):
                            ps = self.next_ps()
                            pb = ps.t[:].bitcast(BF16)
                            for fc in range(8):
                                c.op("pe", lambda e, pb=pb, fc=fc, src=src: e.transpose(pb[:, fc * P:(fc + 1) * P], src[:, fc, cc, :],
                                                                                       self.identB[:]), reads=[src, self.identB], writes=[ps])
                            c.op("act", lambda e, pb=pb, dstD=dstD: e.copy(out=dstD[:].rearrange("p a b -> p (a b)"), in_=pb),
                                 reads=[ps], writes=[dstD])
                        for a, bb in H:
                            c.op("dve", lambda e, a=a, bb=bb: e.tensor_copy(out=VtS[a:bb], in_=VtD[a:bb, :, a:bb]), reads=[VtD], writes=[VtS])
                        pR = self.next_ps()
                        mm8(pR, C, lambda fc: [AtDf[:, fc, cc, :], AakD[:, fc, :]], lambda fc: [SF[:, fc, :], VtS[:, fc, :]],
                            [AtDf, AakD, SF, VtS])
                        c.op("act", lambda e, pR=pR: e.copy(out=RHSf[:].rearrange("p a b -> p (a b)"), in_=pR[:]), reads=[pR], writes=[RHSf])
                        pU = self.next_ps()
                        mm8(pU, C, lambda fc: TtDf[:, fc, :], lambda fc: RHSf[:, fc, :], [TtDf, RHSf])
                        c.op("act", lambda e, pU=pU: e.copy(out=USf[:].rearrange("p a b -> p (a b)"), in_=pU[:]), reads=[pU], writes=[USf])
                        c.op("dve", lambda e: e.tensor_copy(out=US[:], in_=USf[:]), reads=[USf], writes=[US])
                        to_diag(UD, US, [US])
                        pO = self.next_ps()
                        mm8(pO, C, lambda fc: [STD[:, fc, :], UD[:, fc, :], VtD[:, fc, :]],
                            lambda fc: [AR[:, fc, cc, C:P], ArbS[:, fc, :], ArkS[:, fc, :]], [STD, UD, VtD, AR, ArbS, ArkS])
                        c.op("dve", lambda e, pO=pO: e.tensor_copy(out=Yt[:, :, cc * C:(cc + 1) * C], in_=pO[:].rearrange("p (f t) -> p f t", f=8)),
                             reads=[pO], writes=[Yt])
                        pS = self.next_ps()
                        mm8(pS, C, lambda fc: [btDf[:, fc, :], ktD[:, fc, :]], lambda fc: [USf[:, fc, :], VtS[:, fc, :]], [btDf, ktD, USf, VtS])
                        c.op("dve", lambda e: e.tensor_tensor(out=stmp[:], in0=SF[:], in1=Wc[:, :, cc:cc + 1].broadcast_to([P, 8, C]), op=ALU.mult),
                             reads=[SF, Wc], writes=[stmp])
                        c.op("dve", lambda e, pS=pS: e.tensor_tensor(out=SF[:].rearrange("p a b -> p (a b)"),
                                                                     in0=stmp[:].rearrange("p a b -> p (a b)"), in1=pS[:], op=ALU.add),
                             reads=[stmp, pS], writes=[SF])
                        c.op("act", lambda e: e.copy(out=SBf[:], in_=SF[:]), reads=[SF], writes=[SBf])
                        to_diag(STD, SBf, [SBf])
                    if d == 0:
                        c.dma("sp", RWv[11][:, :, t0:t0 + P], Yt[:], reads=[Yt])
                    else:
                        c.dma("sp", YFt[:], RWv[11][:, :, t0:t0 + P], writes=[YFt])
                        c.dma("sp", Gt[:], RWv[3][:, :, t0:t0 + P], writes=[Gt])
                        c.dma("sp", BVt[:], RWv[4][:, :, t0:t0 + P], writes=[BVt])
                        c.op("dve", lambda e: e.tensor_tensor(out=Yt[:], in0=Yt[:], in1=YFt[:], op=ALU.add), reads=[Yt, YFt], writes=[Yt])
                        for stage in range(2):
                            src = Yt if stage == 0 else tmp
                            for half in range(2):
                                ps = self.next_ps()
                                for q in range(4):
                                    fc = half * 4 + q
                                    c.op("pe", lambda e, ps=ps, q=q, fc=fc, src=src: e.matmul(
                                        ps[:, q * P:(q + 1) * P], blk64[:], src[:, fc, :], start=True, stop=True), reads=[blk64, src], writes=[ps])
                                fs = slice(half * 4, half * 4 + 4)
                                pv = ps[:].rearrange("p (f t) -> p f t", f=4)
                                if stage == 0:
                                    c.op("dve", lambda e, pv=pv, fs=fs: e.tensor_tensor(out=Yt[:, fs, :], in0=Yt[:, fs, :], in1=pv, op=ALU.subtract),
                                         reads=[Yt, ps], writes=[Yt])
                                else:
                                    c.op("act", lambda e, pv=pv, fs=fs: e.activation(out=cum[:, fs, :], in_=pv, func=AF.Sqrt, bias=epsg[:, 0:1]),
                                         reads=[ps, epsg], writes=[cum])
                            if stage == 0:
                                c.op("act", lambda e: e.activation(out=tmp[:], in_=Yt[:], func=AF.Square), reads=[Yt], writes=[tmp])
                        c.op("dve", lambda e: e.reciprocal(out=cum[:], in_=cum[:]), reads=[cum], writes=[cum])
                        c.op("dve", lambda e: e.tensor_tensor(out=Yt[:], in0=Yt[:], in1=cum[:], op=ALU.mult), reads=[Yt, cum], writes=[Yt])
                        bc = lambda ap: ap.unsqueeze(2).broadcast_to([P, 8, P])
                        c.op("dve", lambda e: e.tensor_tensor(out=Yt[:], in0=Yt[:], in1=bc(vec[:, 3, :]), op=ALU.mult), reads=[Yt, vec], writes=[Yt])
                        c.op("dve", lambda e: e.tensor_tensor(out=Yt[:], in0=Yt[:], in1=bc(vec[:, 4, :]), op=ALU.add), reads=[Yt, vec], writes=[Yt])
                        c.op("dve", lambda e: e.tensor_tensor(out=Yt[:], in0=Yt[:], in1=BVt[:], op=ALU.add), reads=[Yt, BVt], writes=[Yt])
                        c.op("dve", lambda e: e.tensor_tensor(out=yo[:], in0=Yt[:], in1=Gt[:], op=ALU.mult), reads=[Yt, Gt], writes=[yo])
                        c.dma("sp", Mv[:, 0:8, t0:t0 + P], yo[:], reads=[yo])
                if ctx:
                    for half in range(2):
                        ps = self.next_ps()
                        for q in range(4):
                            fc = half * 4 + q
                            c.op("pe", lambda e, ps=ps, q=q, fc=fc: e.transpose(ps[0:C, q * P:(q + 1) * P], SF[:, fc, :], self.identF[:]),
                                 reads=[SF, self.identF], writes=[ps])
                        c.op("dve", lambda e, ps=ps, half=half: e.tensor_copy(
                            out=sio[:, half * 4:(half + 1) * 4, :].rearrange("p a b -> p (a b)"), in_=ps[0:C, :]), reads=[ps], writes=[sio])
                    for hh in range(2):
                        c.dma("sp", self.nsr[si - 1, l2, d].rearrange("(fc hh) v k -> hh v fc k", hh=2)[hh],
                              sio[:, :, hh * C:(hh + 1) * C], reads=[sio])
            c.barrier()

    def ph_ssd(self, j2, d):
        c, nc = self.c, self.nc
        Ls, Lp, NPR = self.Ls, self.Lp, self.NPR
        Zv = self.Z.rearrange("(mc p) t -> p mc t", p=P)
        XCv = self.XC.rearrange("(mc p) t -> p mc t", p=P)
        Mv = self.MIX.rearrange("(kc p) t -> p kc t", p=P)
        with ExitStack() as es:
            sb = lambda n, sh, dt=F32: self.sb(es, "sd_" + n, sh, dt)
            cw, cbias = sb("cw", [P, 5, 48]), sb("cb", [P, 48])
            dtb, Aneg = sb("dtb", [P, 1]), sb("A", [P, 1])
            dfm, ngf = sb("dfm", [P, 32]), sb("ng", [P, 32])
            mL = sb("mL", [P, 2, P], BF16)
            ones = sb("ones", [P, P])
            eps512 = sb("eps", [P, 1])
            xc = sb("xc", [P, 48, P], BF16)
            hF, hB = sb("hF", [P, 4096]), sb("hB", [P, 4096], BF16)
            Y = sb("Y", [P, 4096])
            LT = sb("LT", [P, 64, P], BF16)
            xdt, xdtw = sb("xdt", [P, 4096], BF16), sb("xdtw", [P, 4096], BF16)
            cbT, Btok = sb("cbT", [P, 8, P], BF16), sb("Btok", [P, 8, P], BF16)
            dtr, e1, dtt, dtA = sb("dtr", [P, P]), sb("e1", [P, P]), sb("dt", [P, P]), sb("dtA", [P, P])
            pre, suf, acum = sb("pre", [P, P]), sb("suf", [P, P]), sb("acum", [P, P])
            st4 = sb("st4", [P, 4, P])
            tokm = sb("tokm", [P, 4, P])
            alast, ealast = sb("alast", [P, 1]), sb("ealast", [P, 1])
            dgE, eaB = sb("dgE", [P, P]), sb("eaB", [P, P])
            tmp5 = sb("tmp5", [P, 512])
            big = sb("big", [P, 32, P])
            yg = sb("yg", [P, 32, P])
            if d == 0:
                xb3 = sb("xb3", [P, 16, P + 4])
                tm3 = sb("tm3", [P, 16, P])
            else:
                yo = sb("yo", [P, 32, P], BF16)
                rs = sb("rs", [P, 8, P])
            c.dma("sp", cw[:], self.s_cw[j2], writes=[cw])
            c.dma("sp", cbias[:], self.s_cb[j2], writes=[cbias])
            c.dma("sp", dtb[:], self.s_dtb[j2], writes=[dtb])
            c.dma("sp", Aneg[:], self.s_alog[j2], writes=[Aneg])
            c.dma("sp", dfm[:], self.s_dfm[j2], writes=[dfm])
            c.dma("sp", ngf[:], self.s_ng[j2], writes=[ngf])
            c.dma("pool", mL[:], self.s_maskl.rearrange("a p q -> p a q"), writes=[mL])
            c.op("act", lambda e: e.activation(out=Aneg[:], in_=Aneg[:], func=AF.Exp), reads=[Aneg], writes=[Aneg])
            c.op("dve", lambda e: e.tensor_scalar(out=Aneg[:], in0=Aneg[:], scalar1=-1.0, scalar2=None, op0=ALU.mult),
                 reads=[Aneg], writes=[Aneg])
            c.op("dve", lambda e: e.memset(ones[:], 1.0), writes=[ones])
            c.op("dve", lambda e: e.memset(eps512[:], 1e-6), writes=[eps512])

            def evac(i, fn_act, fn_dve, reads, writes):
                if i % 2:
                    c.op("act", fn_act, reads=reads, writes=writes)
                else:
                    c.op("dve", fn_dve, reads=reads, writes=writes)

            for si in range(1 + NPR):
                ctx = si > 0
                L = Lp if ctx else Ls
                s0 = 0 if not ctx else Ls + (si - 1) * Lp
                nck = L // P
                if ctx:
                    c.op("dve", lambda e: e.memset(hF[:], 0.0), writes=[hF])
                    c.op("dve", lambda e: e.memset(hB[:], 0.0), writes=[hB])
                else:
                    c.dma("sp", big[:], self.state_ssd[j2, d].rearrange("(bk p) n -> p bk n", p=P), writes=[big])
                    for bk8 in range(8):
                        ps = self.next_ps()
                        for q in range(4):
                            bk = bk8 * 4 + q
                            c.op("pe", lambda e, ps=ps, q=q, bk=bk: e.transpose(
                                ps[:, q * P:(q + 1) * P], big[:, bk, :], self.identF[:]),
                                reads=[big, self.identF], writes=[ps])
                        c.op("dve", lambda e, ps=ps, bk8=bk8: e.tensor_copy(
                            out=hF[:, bk8 * 512:(bk8 + 1) * 512], in_=ps[:]), reads=[ps], writes=[hF])
                    c.op("act", lambda e: e.copy(out=hB[:], in_=hF[:]), reads=[hF], writes=[hB])
                order = range(nck) if d == 0 else range(nck - 1, -1, -1)
                for ck in order:
                    t0 = s0 + ck * P
                    if d == 0:
                        lo = 2 if ck == 0 else 0
                        hi = P + 2 if ck == nck - 1 else P + 4
                        for third in range(3):
                            f0 = third * 16
                            if lo or hi < P + 4:
                                c.op("dve", lambda e: e.memset(xb3[:], 0.0), writes=[xb3])
                            c.dma("sp", xb3[:, :, lo:hi], Zv[:, 32 + f0:48 + f0, t0 - 2 + lo:t0 - 2 + hi], writes=[xb3])
                            acc = big[:, f0 % 32:f0 % 32 + 16, :] if False else None
                            c.op("dve", lambda e, f0=f0: e.tensor_tensor(
                                out=yg[:, 0:16, :], in0=xb3[:, :, 0:P],
                                in1=cw[:, 0, f0:f0 + 16].unsqueeze(2).broadcast_to([P, 16, P]), op=ALU.mult),
                                reads=[xb3, cw], writes=[yg])
                            for k in range(1, 5):
                                c.op("dve", lambda e, f0=f0, k=k: e.tensor_tensor(
                                    out=tm3[:], in0=xb3[:, :, k:k + P],
                                    in1=cw[:, k, f0:f0 + 16].unsqueeze(2).broadcast_to([P, 16, P]), op=ALU.mult),
                                    reads=[xb3, cw], writes=[tm3])
                                c.op("dve", lambda e: e.tensor_tensor(out=yg[:, 0:16, :], in0=yg[:, 0:16, :], in1=tm3[:],
                                                                      op=ALU.add), reads=[yg, tm3], writes=[yg])
                            for f in range(16):
                                c.op("act", lambda e, f=f, f0=f0: e.activation(
                                    out=xc[:, f0 + f, :], in_=yg[:, f, :], func=AF.Silu, bias=cbias[:, f0 + f:f0 + f + 1]),
                                    reads=[yg, cbias], writes=[xc])
                        c.dma("sp", XCv[:, :, t0:t0 + P], xc[:], reads=[xc])
                    else:
                        c.dma("sp", xc[:], XCv[:, :, t0:t0 + P], writes=[xc])
                    c.dma("sp", dtr[:], Zv[:, 80, t0:t0 + P], writes=[dtr])
                    c.op("act", lambda e: e.activation(out=e1[:], in_=dtr[:], func=AF.Exp, bias=dtb[:, 0:1]),
                         reads=[dtr, dtb], writes=[e1])
                    c.op("act", lambda e: e.activation(out=st4[:, 0, :], in_=e1[:], func=AF.Ln, bias=ones[:, 0:1]),
                         reads=[e1, ones], writes=[st4])
                    c.op("dve", lambda e: e.tensor_scalar(out=dtA[:], in0=st4[:, 0, :], scalar1=Aneg[:, 0:1], scalar2=None,
                                                          op0=ALU.mult), reads=[st4, Aneg], writes=[dtA])
                    c.op("dve", lambda e: e.tensor_tensor_scan(out=pre[:], data0=ones[:], data1=dtA[:], initial=0.0,
                                                               op0=ALU.mult, op1=ALU.add), reads=[ones, dtA], writes=[pre])
                    c.op("dve", lambda e: e.tensor_tensor_scan(out=suf[:, ::-1], data0=ones[:], data1=dtA[:, ::-1],
                                                               initial=0.0, op0=ALU.mult, op1=ALU.add),
                         reads=[ones, dtA], writes=[suf])
                    c.op("dve", lambda e: e.tensor_copy(out=acum[0:64, :], in_=pre[0:64, :]), reads=[pre], writes=[acum])
                    c.op("dve", lambda e: e.tensor_copy(out=acum[64:128, :], in_=suf[64:128, :]), reads=[suf], writes=[acum])
                    c.op("dve", lambda e: e.tensor_copy(out=alast[0:64, :], in_=pre[0:64, P - 1:P]), reads=[pre], writes=[alast])
                    c.op("dve", lambda e: e.tensor_copy(out=alast[64:128, :], in_=suf[64:128, 0:1]), reads=[suf], writes=[alast])
                    c.op("dve", lambda e: e.tensor_scalar(out=st4[:, 2, :], in0=acum[:], scalar1=-1.0, scalar2=None,
                                                          op0=ALU.mult), reads=[acum], writes=[st4])
                    c.op("act", lambda e: e.activation(out=st4[:, 3, :], in_=acum[:], func=AF.Exp), reads=[acum], writes=[st4])
                    c.op("act", lambda e: e.activation(out=e1[:], in_=acum[:], func=AF.Exp, scale=-1.0, bias=alast[:, 0:1]),
                         reads=[acum, alast], writes=[e1])
                    c.op("dve", lambda e: e.tensor_tensor(out=st4[:, 1, :], in0=st4[:, 0, :], in1=e1[:], op=ALU.mult),
                         reads=[st4, e1], writes=[st4])
                    c.op("act", lambda e: e.activation(out=ealast[:], in_=alast[:], func=AF.Exp), reads=[alast], writes=[ealast])
                    ps = self.next_ps()
                    for q in range(4):
                        c.op("pe", lambda e, ps=ps, q=q: e.transpose(ps[:, q * P:(q + 1) * P], st4[:, q, :], self.identF[:]),
                             reads=[st4, self.identF], writes=[ps])
                    c.op("dve", lambda e, ps=ps: e.tensor_copy(out=tokm[:].rearrange("p a b -> p (a b)"), in_=ps[:]),
                         reads=[ps], writes=[tokm])
                    c.op("dve", lambda e: e.tensor_scalar(out=dgE[:], in0=self.identF[:], scalar1=ealast[:, 0:1], scalar2=None,
                                                          op0=ALU.mult), reads=[self.identF, ealast], writes=[dgE])
                    ps = self.next_ps()
                    c.op("pe", lambda e, ps=ps: e.matmul(ps[:, 0:P], ones[:], dgE[:], start=True, stop=True),
                         reads=[ones, dgE], writes=[ps])
                    c.op("act", lambda e, ps=ps: e.copy(out=eaB[:], in_=ps[:, 0:P]), reads=[ps], writes=[eaB])
                    for half in range(2):
                        ps = self.next_ps()
                        for q in range(4):
                            g = half * 4 + q
                            c.op("pe", lambda e, ps=ps, q=q, g=g: e.matmul(
                                ps[:, q * P:(q + 1) * P], xc[:, 32 + g, :], xc[:, 40 + g, :], start=True, stop=True),
                                reads=[xc], writes=[ps])
                        evac(half, lambda e, ps=ps, half=half: e.copy(
                            out=cbT[:, half * 4:(half + 1) * 4, :].rearrange("p a b -> p (a b)"), in_=ps[:]),
                            lambda e, ps=ps, half=half: e.tensor_copy(
                            out=cbT[:, half * 4:(half + 1) * 4, :].rearrange("p a b -> p (a b)"), in_=ps[:]),
                            [ps], [cbT])
                    ps = self.next_ps()
                    pb = ps.t[:].bitcast(BF16)
                    for g in range(8):
                        c.op("pe", lambda e, pb=pb, g=g: e.transpose(pb[:, g * P:(g + 1) * P], xc[:, 32 + g, :], self.identB[:]),
                             reads=[xc, self.identB], writes=[ps])
                    c.op("act", lambda e, pb=pb: e.copy(out=Btok[:].rearrange("p a b -> p (a b)"), in_=pb),
                         reads=[ps], writes=[Btok])
                    for hb in range(16):
                        ps = self.next_ps()
                        for q in range(4):
                            h = hb * 4 + q
                            col = d * 64 + h
                            c.op("pe", lambda e, ps=ps, q=q, col=col: e.matmul(
                                ps[:, q * P:(q + 1) * P], self.identF[:, col:col + 1].broadcast_to([P, P]), acum[:],
                                start=True, stop=False), reads=[self.identF, acum], writes=[ps])
                            c.op("pe", lambda e, ps=ps, q=q: e.matmul(
                                ps[:, q * P:(q + 1) * P], self.identB[:], mL[:, d, :], start=False, stop=True),
                                reads=[self.identB, mL], writes=[ps])
                        for q in range(4):
                            h = hb * 4 + q
                            col = d * 64 + h
                            c.op("act", lambda e, ps=ps, q=q, h=h, col=col: e.activation(
                                out=LT[:, h, :], in_=ps[:, q * P:(q + 1) * P], func=AF.Exp, bias=tokm[:, 2, col:col + 1]),
                                reads=[ps, tokm], writes=[LT])
                    for g in range(8):
                        c.op("dve", lambda e, g=g: e.tensor_tensor(
                            out=LT[:, g * 8:(g + 1) * 8, :], in0=LT[:, g * 8:(g + 1) * 8, :],
                            in1=cbT[:, g:g + 1, :].broadcast_to([P, 8, P]), op=ALU.mult), reads=[LT, cbT], writes=[LT])
                    for bk in range(4):
                        ps = self.next_ps()
                        pb = ps.t[:].bitcast(BF16)
                        for q in range(8):
                            fc = bk * 8 + q
                            c.op("pe", lambda e, pb=pb, q=q, fc=fc: e.transpose(
                                pb[:, q * P:(q + 1) * P], xc[:, fc, :], self.identB[:]), reads=[xc, self.identB], writes=[ps])
                        hs = d * 64 + bk * 16
                        c.op("dve", lambda e, pb=pb, bk=bk, hs=hs: e.tensor_tensor(
                            out=xdt[:, bk * 1024:(bk + 1) * 1024].rearrange("p (h q) -> p h q", h=16),
                            in0=pb.rearrange("p (h q) -> p h q", h=16),
                            in1=tokm[:, 0, hs:hs + 16].unsqueeze(2).broadcast_to([P, 16, 64]), op=ALU.mult),
                            reads=[ps, tokm], writes=[xdt])
                        c.op("dve", lambda e, pb=pb, bk=bk, hs=hs: e.tensor_tensor(
                            out=xdtw[:, bk * 1024:(bk + 1) * 1024].rearrange("p (h q) -> p h q", h=16),
                            in0=pb.rearrange("p (h q) -> p h q", h=16),
                            in1=tokm[:, 1, hs:hs + 16].unsqueeze(2).broadcast_to([P, 16, 64]), op=ALU.mult),
                            reads=[ps, tokm], writes=[xdtw])
                    for g in range(8):
                        py, pst, ph = self.next_ps(), self.next_ps(), self.next_ps()
                        for q in range(8):
                            h = g * 8 + q
                            c.op("pe", lambda e, py=py, q=q, h=h: e.matmul(
                                py[:, q * 64:(q + 1) * 64], LT[:, h, :], xdt[:, h * 64:(h + 1) * 64], start=True, stop=True),
                                reads=[LT, xdt], writes=[py])
                        c.op("pe", lambda e, pst=pst, g=g: e.matmul(
                            pst[:], xc[:, 40 + g, :], hB[:, g * 512:(g + 1) * 512], start=True, stop=True),
                            reads=[xc, hB], writes=[pst])
                        c.op("pe", lambda e, ph=ph, g=g: e.matmul(
                            ph[:], Btok[:, g, :], xdtw[:, g * 512:(g + 1) * 512], start=True, stop=True),
                            reads=[Btok, xdtw], writes=[ph])
                        hs = d * 64 + g * 8
                        c.op("dve", lambda e, pst=pst, hs=hs: e.tensor_tensor(
                            out=tmp5[:].rearrange("p (h q) -> p h q", h=8), in0=pst[:].rearrange("p (h q) -> p h q", h=8),
                            in1=tokm[:, 3, hs:hs + 8].unsqueeze(2).broadcast_to([P, 8, 64]), op=ALU.mult),
                            reads=[pst, tokm], writes=[tmp5])
                        c.op("dve", lambda e, py=py, g=g: e.tensor_tensor(
                            out=Y[:, g * 512:(g + 1) * 512], in0=py[:], in1=tmp5[:], op=ALU.add),
                            reads=[py, tmp5], writes=[Y])
                        c.op("dve", lambda e, g=g, hs=hs: e.tensor_tensor(
                            out=tmp5[:].rearrange("p (h q) -> p h q", h=8),
                            in0=hF[:, g * 512:(g + 1) * 512].rearrange("p (h q) -> p h q", h=8),
                            in1=eaB[:, hs:hs + 8].unsqueeze(2).broadcast_to([P, 8, 64]), op=ALU.mult),
                            reads=[hF, eaB], writes=[tmp5])
                        c.op("dve", lambda e, ph=ph, g=g: e.tensor_tensor(
                            out=hF[:, g * 512:(g + 1) * 512], in0=ph[:], in1=tmp5[:], op=ALU.add),
                            reads=[ph, tmp5], writes=[hF])
                        c.op("act", lambda e, g=g: e.copy(out=hB[:, g * 512:(g + 1) * 512], in_=hF[:, g * 512:(g + 1) * 512]),
                             reads=[hF], writes=[hB])
                    if d == 0:
                        c.dma("sp", self.YF[t0:t0 + P, :], Y[:], reads=[Y])
                    else:
                        ygf = yg[:].rearrange("p a b -> p (a b)")
                        c.dma("sp", ygf, self.YF[t0:t0 + P, :], writes=[yg])
                        c.op("dve", lambda e: e.tensor_tensor(out=Y[:], in0=Y[:], in1=ygf, op=ALU.add), reads=[Y, yg], writes=[Y])
                        c.dma("sp", big[:], Zv[:, 0:32, t0:t0 + P], writes=[big])
                        c.op("act", lambda e: e.activation(out=big[:], in_=big[:], func=AF.Silu), reads=[big], writes=[big])
                        for bk in range(8):
                            ps = self.next_ps()
                            for q in range(4):
                                fc = bk * 4 + q
                                c.op("pe", lambda e, ps=ps, q=q, fc=fc: e.transpose(
                                    ps[:, q * P:(q + 1) * P], Y[:, fc * P:(fc + 1) * P], self.identF[:]),
                                    reads=[Y, self.identF], writes=[ps])
                            for q in range(4):
                                fc = bk * 4 + q
                                c.op("dve", lambda e, ps=ps, q=q, fc=fc: e.scalar_tensor_tensor(
                                    out=yg[:, fc, :], in0=xc[:, fc, :], scalar=dfm[:, fc:fc + 1], in1=ps[:, q * P:(q + 1) * P],
                                    op0=ALU.mult, op1=ALU.add), reads=[xc, dfm, ps], writes=[yg])
                        c.op("dve", lambda e: e.tensor_tensor(out=yg[:], in0=yg[:], in1=big[:], op=ALU.mult),
                             reads=[yg, big], writes=[yg])
                        c.op("act", lambda e: e.activation(out=big[:], in_=yg[:], func=AF.Square), reads=[yg], writes=[big])
                        for half in range(2):
                            ps = self.next_ps()
                            for gq in range(4):
                                g = half * 4 + gq
                                for q in range(4):
                                    c.op("pe", lambda e, ps=ps, gq=gq, g=g, q=q: e.matmul(
                                        ps[:, gq * P:(gq + 1) * P], ones[:], big[:, g * 4 + q, :], start=(q == 0), stop=(q == 3)),
                                        reads=[ones, big], writes=[ps])
                            c.op("act", lambda e, ps=ps, half=half: e.activation(
                                out=rs[:, half * 4:(half + 1) * 4, :].rearrange("p a b -> p (a b)"), in_=ps[:], func=AF.Sqrt,
                                scale=1.0 / 512, bias=eps512[:, 0:1]), reads=[ps, eps512], writes=[rs])
                        c.op("dve", lambda e: e.reciprocal(out=rs[:], in_=rs[:]), reads=[rs], writes=[rs])
                        c.op("dve", lambda e: e.tensor_tensor(
                            out=yg[:].rearrange("p (g q) t -> p g q t", g=8), in0=yg[:].rearrange("p (g q) t -> p g q t", g=8),
                            in1=rs[:].unsqueeze(2).broadcast_to([P, 8, 4, P]), op=ALU.mult), reads=[yg, rs], writes=[yg])
                        c.op("dve", lambda e: e.tensor_tensor(
                            out=yo[:], in0=yg[:], in1=ngf[:].unsqueeze(2).broadcast_to([P, 32, P]), op=ALU.mult),
                            reads=[yg, ngf], writes=[yo])
                        c.dma("sp", Mv[:, 0:32, t0:t0 + P], yo[:], reads=[yo])
                if ctx:
                    for bk8 in range(8):
                        ps = self.next_ps()
                        for q in range(4):
                            bk = bk8 * 4 + q
                            c.op("pe", lambda e, ps=ps, q=q, bk=bk: e.transpose(
                                ps[:, q * P:(q + 1) * P], hF[:, bk * P:(bk + 1) * P], self.identF[:]),
                                reads=[hF, self.identF], writes=[ps])
                        c.op("dve", lambda e, ps=ps, bk8=bk8: e.tensor_copy(
                            out=big[:, bk8 * 4:(bk8 + 1) * 4, :].rearrange("p a b -> p (a b)"), in_=ps[:]),
                            reads=[ps], writes=[big])
                    c.dma("sp", self.nss[si - 1, j2, d].rearrange("(bk p) n -> p bk n", p=P), big[:], reads=[big])
            c.barrier()

    def ph_mix_stub(self, KC):
        c = self.c
        with ExitStack() as es:
            z = self.sb(es, "stub_z", [P, self.TT], BF16)
            c.op("dve", lambda e: e.memset(z[:], 0.0), writes=[z])
            Mv = self.MIX.rearrange("(kc p) t -> p kc t", p=P)
            for kc in range(KC):
                for ti in range(self.Ttot // self.TT):
                    c.dma("sp", Mv[:, kc, ti * self.TT:(ti + 1) * self.TT], z[:], reads=[z])
            c.barrier()

    def build(self, stub=True):
        self.declare()
        self.ph_load()
        self.ph_mod()
        X, Xn = self.XA, self.XB
        for l in range(self.depth):
            if l % 2 == 0:
                KC, Wout = 16, self.ab_w_out[l // 2]
                self.ph_mix_stub(KC)
                if ENABLE_ATTN:
                    self.ph_in(l, X, self.ab_w_in[l // 2], 54, self.Z,
                               lambda mc, cond: cond == 1 or not (36 <= mc < 44 or 48 <= mc < 52))
                    self.c.st_pool = True
                    self.ph_attn(l // 2)
                    if ENABLE_RWKV:
                        self.ph_rw0(l // 2)
                        self.ph_rw1(l // 2, 0)
                        self.ph_rw1(l // 2, 1)
                    self.c.st_pool = False
            else:
                KC, Wout = 32, self.ssd_w_out[l // 2]
                if ENABLE_SSD:
                    self.ph_in(l, X, self.ssd_w_in[l // 2], 81, self.Z)
                    self.c.st_pool = True
                    self.ph_ssd(l // 2, 0)
                    self.ph_ssd(l // 2, 1)
                    self.c.st_pool = False
                else:
                    self.ph_mix_stub(KC)
            self.ph_out(l, X, Xn, Wout, KC)
            X, Xn = Xn, X
            self.ph_ffn(l, X, Xn)
            X, Xn = Xn, X
        self.ph_final(X)
        return self.nc


_CACHE = {}


def _perm64():
    d = np.arange(64)
    return np.where((d % 32) < 16, d + 16, d - 16)


def ab_colidx():
    Z = 5024
    idx = list(range(0, 3488)) + [Z] * 96
    q0, k0, v0 = 3488, 4512, 4768
    pm = _perm64()
    idx += list(range(q0, q0 + 1024))
    idx += [q0 + h * 64 + int(pm[d]) for h in range(16) for d in range(64)]
    for g in range(4):
        idx += [k0 + g * 64 + d for d in range(64)] * 2
    for g in range(4):
        idx += [k0 + g * 64 + int(pm[d]) for d in range(64)] * 2
    idx += list(range(v0, v0 + 256))
    assert len(idx) == 54 * P
    return np.asarray(idx)


def rope_tables(L, grid_w=64, theta=10000.0):
    rows = (np.arange(L) // grid_w).astype(np.float32)
    cols = (np.arange(L) % grid_w).astype(np.float32)
    inv = (theta ** (-np.arange(0, 32, 2, dtype=np.float32) / 32)).astype(np.float32)
    cs = np.zeros((64, L), np.float32)
    sn = np.zeros((64, L), np.float32)
    for d in range(64):
        pos = rows if d < 32 else cols
        ang = pos * inv[d % 16]
        cs[d] = np.cos(ang)
        sn[d] = np.sin(ang) * (-1.0 if (d % 32) < 16 else 1.0)
    return np.ascontiguousarray(np.concatenate([cs, cs], 0)), np.ascontiguousarray(np.concatenate([sn, sn], 0))


def kernel(**inp):
    x_prompt = np.asarray(inp["x_prompt"], np.float32)
    x_sample = np.asarray(inp["x_sample"], np.float32)
    B, Lp, D = x_prompt.shape
    DB, Ls, _ = x_sample.shape
    depth = inp["w_mod"].shape[0]
    NPR = B // NCORES
    past = inp["cache_k"].shape[2]
    assert DB == NCORES
    key = (Ls, Lp, NPR, depth, past)
    bld = Builder(*key)
    nc = bld.build()
    Ttot = bld.Ttot
    shared = {
        "w_mod": np.asarray(inp["w_mod"], np.float32),
        "bmod": fm(inp["b_mod"], 96),
        "n1g": fm(inp["norm1_g"], 16),
        "n2g": fm(inp["norm2_g"], 16),
        "fng": fm(inp["final_norm_g"], 16),
        "ffn_w_in": np.asarray(inp["ffn_w_in"], np.float32),
        "ffn_w_out": np.asarray(inp["ffn_w_out"], np.float32),
        "ident": np.eye(P, dtype=np.float32),
        "ab_w_out": np.asarray(inp["ab_w_out"], np.float32),
        "ssd_w_out": np.asarray(inp["ssd_w_out"], np.float32),
        "ssd_w_in": np.asarray(inp["ssd_w_in"], np.float32),
    }
    ci = ab_colidx()
    abw = np.asarray(inp["ab_w_in"], np.float32)
    abw = np.concatenate([abw, np.zeros(abw.shape[:2] + (1,), np.float32)], -1)
    shared["ab_w_in"] = np.ascontiguousarray(abw[:, :, ci])
    rc, rs = rope_tables(Ls)
    shared["ropec"], shared["ropes"] = rc, rs
    kj = np.arange(P)[:, None]
    qi = np.arange(P)[None, :]
    shared["maskb"] = np.stack([np.where(kj >= qi, 0.0, -30000.0), np.where(kj <= qi, 0.0, -30000.0)]).astype(np.float32)
    sink = np.asarray(inp["attn_sink"], np.float32)
    hh = 2 * np.arange(8)[None, :] + (np.arange(P) // 64)[:, None]
    shared["sinkfm"] = np.ascontiguousarray(sink[:, hh])
    f32 = lambda k: np.asarray(inp[k], np.float32)
    na = bld.n_ab
    shared["r_mu"] = np.ascontiguousarray(np.stack([fm(f32("rwkv_mu_prev"), 28), fm(f32("rwkv_mu_next"), 28)], 2))
    w0 = fm(f32("rwkv_w0"), 8)
    a0 = fm(f32("rwkv_a0"), 8)
    shared["r_w0a0"] = np.ascontiguousarray(np.stack([w0, a0], 1).transpose(0, 3, 1, 2, 4))
    shared["r_w2"] = np.ascontiguousarray(f32("rwkv_w2").reshape(na, P, 1024))
    shared["r_a2"] = np.ascontiguousarray(f32("rwkv_a2").reshape(na, P, 1024))
    g2 = np.concatenate([f32("rwkv_g2"), np.zeros((na, 96, 1024), np.float32)], 1)
    shared["r_g2"] = np.ascontiguousarray(g2.reshape(na, 2, P, 1024).transpose(0, 2, 1, 3))
    shared["r_vec"] = np.ascontiguousarray(np.stack([fm(f32("rwkv_k_k"), 8), fm(f32("rwkv_k_a"), 8),
                                                     fm(f32("rwkv_r_k").reshape(na, 1024), 8), fm(f32("rwkv_lnx_g"), 8),
                                                     fm(f32("rwkv_lnx_b"), 8)], 2))
    tt = (np.arange(P) % 64)[:, None]
    ss = np.arange(64)[None, :]
    shared["r_maskn"] = np.stack([(ss < tt), (ss > tt)]).astype(np.float32)
    sm0 = np.concatenate([(tt < ss), (tt <= ss)], 1)
    sm1 = np.concatenate([(tt > ss), (tt >= ss)], 1)
    shared["r_masks"] = np.stack([sm0, sm1]).astype(np.float32)
    shared["r_is"] = (tt == ss).astype(np.float32)
    shared["r_blk"] = np.kron(np.eye(2, dtype=np.float32), np.ones((64, 64), np.float32))
    ns = max(bld.n_ssd, 1)
    cwf = fm(inp["ssd_conv_w"], 48)
    shared["s_cw"] = np.ascontiguousarray(cwf.transpose(0, 2, 1, 3))
    shared["s_cb"] = fm(inp["ssd_conv_b"], 48)
    shared["s_dtb"] = np.ascontiguousarray(np.asarray(inp["ssd_dt_bias"], np.float32).reshape(ns, P, 1))
    shared["s_alog"] = np.ascontiguousarray(np.asarray(inp["ssd_a_log"], np.float32).reshape(ns, P, 1))
    shared["s_dfm"] = fm(np.repeat(np.asarray(inp["ssd_d"], np.float32), 64, axis=-1), 32)
    shared["s_ng"] = fm(inp["ssd_norm_g"], 32)
    jj = np.arange(P)[:, None]
    ii = np.arange(P)[None, :]
    shared["s_maskl"] = np.stack([np.where(jj <= ii, 0.0, -30000.0), np.where(jj >= ii, 0.0, -30000.0)]).astype(np.float32)
    in_maps = []
    for i in range(NCORES):
        m = dict(shared)
        m["xin"] = np.ascontiguousarray(np.concatenate(
            [x_sample[i]] + [x_prompt[NPR * i + j] for j in range(NPR)], 0))
        cc = np.stack([np.asarray(inp["c_ctx"], np.float32), np.asarray(inp["c"], np.float32)[i]], -1)
        m["cond2"] = np.ascontiguousarray(cc.reshape(16, P, 2).transpose(1, 0, 2))
        m["state_rwkv"] = np.ascontiguousarray(f32("state_rwkv")[i])
        m["state_ssd"] = np.ascontiguousarray(np.asarray(inp["state_ssd"], np.float32)[i].reshape(ns, 2, 4096, P))
        m["cache_k"] = np.ascontiguousarray(np.asarray(inp["cache_k"], np.float32)[i].reshape(-1, past, 256))
        m["cache_v"] = np.ascontiguousarray(np.asarray(inp["cache_v"], np.float32)[i].reshape(-1, past, 256))
        in_maps.append({k: m[k] for k in bld.ins})
    res = run_bass_kernel_spmd(nc, in_maps, core_ids=list(range(NCORES)))
    y_sample = np.stack([res.results[i]["y_out"][:Ls] for i in range(NCORES)], 0)
    y_prompt = np.stack([res.results[i]["y_out"][Ls + j * Lp: Ls + (j + 1) * Lp]
                         for i in range(NCORES) for j in range(NPR)], 0)
    n_ab = bld.n_ab
    nck = np.concatenate([res.results[i]["nck"] for i in range(NCORES)], 0).reshape(B, n_ab, Lp, 4, 64)
    ncv = np.concatenate([res.results[i]["ncv"] for i in range(NCORES)], 0).reshape(B, n_ab, Lp, 4, 64)
    nsr = np.concatenate([res.results[i]["nsr"] for i in range(NCORES)], 0)
    nss = np.concatenate([res.results[i]["nss"] for i in range(NCORES)], 0).reshape(B, bld.n_ssd, 2, 64, 64, 128)
    return y_prompt, y_sample, nsr, nck, ncv, nss
```
